# Optimizing a Trainium2 kernel written in Bass

```python
import math
import jax
import jax.numpy as jnp
from jax import lax
import numpy as np

D_MODEL = 2048
BATCH = 1
SEQ = 16384
DEPTH = 4

GRID_W = 64

N_MIXERS = 4
GROUP_WIDTH = D_MODEL // N_MIXERS
D_MIX = N_MIXERS * GROUP_WIDTH
HEAD_DIM = 64

NA_HEADS = GROUP_WIDTH // HEAD_DIM
NA_WIN_ROWS = 8
NA_WIN_COLS = 16

DIL_HEADS = GROUP_WIDTH // HEAD_DIM
DIL_CONFIGS = ((128, 1), (512, 4), (2048, 16))
DIL_BLOCK = 64

RET_HEADS = 4
RET_VAL_DIM = GROUP_WIDTH // RET_HEADS
RET_KEY_DIM = RET_VAL_DIM // 2
RET_CHUNK = 128
ROPE_BASE = 10000.0

S5_WIDTH = GROUP_WIDTH
S5_GROUP = 16
S5_GROUPS = S5_WIDTH // S5_GROUP
S5_STATE = 64

T5_BUCKETS = 32
T5_MAX_EXACT = 8
T5_MAX_DIST = 2048

D_FF = ((8 * D_MODEL // 3 + 255) // 256) * 256

EPS = 1e-6
NEG_INF = -1e30

NA_COLS = 3 * NA_HEADS * HEAD_DIM
DIL_COLS = 3 * DIL_HEADS * HEAD_DIM
RET_QK = RET_HEADS * RET_KEY_DIM
RET_V = RET_HEADS * RET_VAL_DIM
RET_COLS = 2 * RET_QK + 2 * RET_V
S5_COLS = S5_WIDTH
D_IN = NA_COLS + DIL_COLS + RET_COLS + S5_COLS

kernel_name = "hybrid_parallel_head_encoder"


def _rms(x, g):
    xf = x.astype(jnp.float32)
    xf = xf * lax.rsqrt(jnp.mean(xf * xf, axis=-1, keepdims=True) + EPS)
    return xf.astype(x.dtype) * g


def _swiglu(h, w_gate, w_up, w_down):
    return (jax.nn.silu(h @ w_gate) * (h @ w_up)) @ w_down


def _t5_bucket(rel):
    half = T5_BUCKETS // 2
    side = jnp.where(rel > 0, half, 0)
    n = jnp.abs(rel)
    nf = jnp.maximum(n, 1).astype(jnp.float32)
    large = T5_MAX_EXACT + (jnp.log(nf / T5_MAX_EXACT) / math.log(T5_MAX_DIST / T5_MAX_EXACT)
                            * (half - T5_MAX_EXACT)).astype(jnp.int32)
    large = jnp.minimum(large, half - 1)
    return side + jnp.where(n < T5_MAX_EXACT, n, large)


def _rotary(t):
    L, dk = t.shape[1], t.shape[-1]
    half = dk // 2
    inv_freq = ROPE_BASE ** (-jnp.arange(half, dtype=jnp.float32) / half)
    ang = jnp.arange(L, dtype=jnp.float32)[:, None] * inv_freq[None, :]
    cos = jnp.cos(ang)[None, :, None, :]
    sin = jnp.sin(ang)[None, :, None, :]
    t1, t2 = t[..., :half], t[..., half:]
    return jnp.concatenate([t1 * cos - t2 * sin, t1 * sin + t2 * cos], axis=-1)


def _neighbourhood_attn(q, k, v, rpb):
    b, L, h, dh = q.shape
    rows = L // GRID_W
    wr = min(NA_WIN_ROWS, rows)
    wc = NA_WIN_COLS
    r = jnp.arange(rows)
    row_idx = jnp.clip(r - wr // 2, 0, rows - wr)[:, None] + jnp.arange(wr)[None, :]
    c = jnp.arange(GRID_W)
    col_start = jnp.clip(c - wc // 2, 0, GRID_W - wc)
    col_ok = (c[None, :] >= col_start[:, None]) & (c[None, :] < col_start[:, None] + wc)
    qg = q.reshape(b, rows, GRID_W, h, dh)
    kg = k.reshape(b, rows, GRID_W, h, dh)[:, row_idx]
    vg = v.reshape(b, rows, GRID_W, h, dh)[:, row_idx]
    logits = jnp.einsum('brqhd,brikhd->bhrqik', qg, kg).astype(jnp.float32) * dh ** -0.5
    dr = row_idx - r[:, None] + (NA_WIN_ROWS - 1)
    dc = jnp.clip(c[None, :] - c[:, None], -(wc - 1), wc - 1) + (NA_WIN_COLS - 1)
    bias = rpb[:, dr[:, None, :, None], dc[None, :, None, :]].astype(jnp.float32)
    logits = jnp.where(col_ok[:, None, :], logits + bias, NEG_INF)
    p = jax.nn.softmax(logits, axis=(-2, -1))
    out = jnp.einsum('bhrqik,brikhd->brqhd', p.astype(v.dtype), vg)
    return out.reshape(b, L, h * dh)


def _dilated_branch(q, k, v, t5_table, window, dilation):
    b, L, h, dh = q.shape
    n_sub = L // dilation
    half = window // (2 * dilation)
    n_blk = -(-n_sub // DIL_BLOCK)
    n_pad = n_blk * DIL_BLOCK
    kb = DIL_BLOCK + 2 * half

    def to_sub(t):
        return t.reshape(b, n_sub, dilation, h, dh).transpose(0, 2, 1, 3, 4)

    qs, ks, vs = to_sub(q), to_sub(k), to_sub(v)
    qb = jnp.pad(qs, ((0, 0), (0, 0), (0, n_pad - n_sub), (0, 0), (0, 0)))
    qb = qb.reshape(b, dilation, n_blk, DIL_BLOCK, h, dh)
    q_idx = jnp.arange(n_pad).reshape(n_blk, DIL_BLOCK)
    k_idx = (jnp.arange(n_blk) * DIL_BLOCK - half)[:, None] + jnp.arange(kb)[None, :]
    k_safe = jnp.clip(k_idx, 0, n_sub - 1)
    kband = ks[:, :, k_safe]
    vband = vs[:, :, k_safe]
    rel = k_idx[:, None, :] - q_idx[:, :, None]
    ok = ((k_idx >= 0) & (k_idx < n_sub))[:, None, :] & (jnp.abs(rel) <= half)
    bias = jnp.moveaxis(t5_table[_t5_bucket(rel * dilation)], -1, 0).astype(jnp.float32)
    logits = jnp.einsum('bdnqhe,bdnkhe->bhdnqk', qb, kband).astype(jnp.float32) * dh ** -0.5
    logits = jnp.where(ok, logits + bias[:, None], NEG_INF)
    m = jnp.max(logits, axis=-1, keepdims=True)
    p = jnp.exp(logits - m)
    s = jnp.sum(p, axis=-1, keepdims=True)
    o = jnp.einsum('bhdnqk,bdnkhe->bdnqhe', p.astype(v.dtype), vband)
    o = o / s[..., 0].transpose(0, 2, 3, 4, 1)[..., None]
    lse = (m + jnp.log(s))[..., 0].transpose(0, 2, 3, 4, 1)
    o = o.reshape(b, dilation, n_pad, h, dh)[:, :, :n_sub].transpose(0, 2, 1, 3, 4).reshape(b, L, h, dh)
    lse = lse.reshape(b, dilation, n_pad, h)[:, :, :n_sub].transpose(0, 2, 1, 3).reshape(b, L, h)
    return o, lse


def _dilated_attn(q, k, v, t5_table):
    b, L, h, dh = q.shape
    outs, lses = [], []
    for window, dilation in DIL_CONFIGS:
        o, lse = _dilated_branch(q, k, v, t5_table, window, dilation)
        outs.append(o)
        lses.append(lse)
    wts = jax.nn.softmax(jnp.stack(lses, axis=0), axis=0)
    o = jnp.sum(jnp.stack(outs, axis=0) * wts[..., None], axis=0)
    return o.reshape(b, L, h * dh).astype(q.dtype)


def _retention_dir(q, k, v, log_gamma, include_diag):
    b, L, h, dk = q.shape
    dv = v.shape[-1]
    n = L // RET_CHUNK
    idx = jnp.arange(RET_CHUNK, dtype=jnp.float32)
    diff = idx[:, None] - idx[None, :]
    mask = (diff >= 0) if include_diag else (diff > 0)
    inner = jnp.where(mask[None], jnp.exp(jnp.maximum(diff, 0.0)[None] * log_gamma[:, None, None]), 0.0)
    q_dec = jnp.exp((idx[:, None] + 1.0) * log_gamma[None, :])
    k_dec = jnp.exp((RET_CHUNK - 1.0 - idx[:, None]) * log_gamma[None, :])
    c_dec = jnp.exp(RET_CHUNK * log_gamma)

    def chunks(t):
        return t.reshape(b, n, RET_CHUNK, h, t.shape[-1]).swapaxes(0, 1)

    def step(state, qkv):
        qc, kc, vc = qkv
        att = jnp.einsum('bihd,bjhd->bhij', qc, kc) * inner
        y = (jnp.einsum('bhij,bjhe->bihe', att, vc)
             + jnp.einsum('bihd,bhde->bihe', qc, state) * q_dec[None, :, :, None])
        state = state * c_dec[None, :, None, None] + jnp.einsum(
            'bjhd,bjhe->bhde', kc * k_dec[None, :, :, None], vc)
        return state, y

    s0 = jnp.zeros((b, h, dk, dv), jnp.float32)
    _, y = lax.scan(step, s0, (chunks(q), chunks(k), chunks(v)))
    return y.swapaxes(0, 1).reshape(b, L, h, dv)


def _retention(q, k, v, gate, decay_logit):
    b, L, h, dv = v.shape
    qf = _rotary(q.astype(jnp.float32))
    kf = _rotary(k.astype(jnp.float32)) * RET_KEY_DIM ** -0.5
    vf = v.astype(jnp.float32)
    log_gamma = jax.nn.log_sigmoid(decay_logit.astype(jnp.float32))
    fwd = _retention_dir(qf, kf, vf, log_gamma[0], True)
    bwd = _retention_dir(qf[:, ::-1], kf[:, ::-1], vf[:, ::-1], log_gamma[1], False)[:, ::-1]
    y = fwd + bwd
    mu = jnp.mean(y, axis=-1, keepdims=True)
    var = jnp.mean(jnp.square(y - mu), axis=-1, keepdims=True)
    y = ((y - mu) * lax.rsqrt(var + EPS)).reshape(b, L, h * dv)
    return jax.nn.silu(gate) * y.astype(gate.dtype)


def _s5_combine(e1, e2):
    a1, x1 = e1
    a2, x2 = e2
    return a1 * a2, a2 * x1 + x2


def _s5_scan(u_grp, lam_bar, b_bar, reverse):
    bu = jnp.einsum('blgc,gpc->blgp', u_grp, b_bar)
    decay = jnp.broadcast_to(lam_bar, bu.shape)
    _, states = lax.associative_scan(_s5_combine, (decay, bu), reverse=reverse, axis=1)
    return states


def _s5(u, a_re, a_im, log_step, b_re, b_im, c_re, c_im, d_skip, w_glu, b_glu):
    bsz, L, w = u.shape
    uf = u.astype(jnp.float32).reshape(bsz, L, S5_GROUPS, S5_GROUP)
    lam = lax.complex(a_re.astype(jnp.float32), a_im.astype(jnp.float32))
    dt = jnp.exp(log_step.astype(jnp.float32))[..., None]
    lam_bar = jnp.exp(lam * dt)
    b_bar = ((lam_bar - 1.0) / lam)[..., None] * lax.complex(b_re.astype(jnp.float32), b_im.astype(jnp.float32))
    x_fwd = _s5_scan(uf, lam_bar[0], b_bar[0], False)
    x_bwd = _s5_scan(uf, lam_bar[1], b_bar[1], True)
    c = lax.complex(c_re.astype(jnp.float32), c_im.astype(jnp.float32))
    y = jnp.real(jnp.einsum('blgp,gcp->blgc', x_fwd + x_bwd, c))
    y = y + d_skip.astype(jnp.float32).reshape(S5_GROUPS, S5_GROUP) * uf
    z = jax.nn.gelu(y.reshape(bsz, L, w).astype(u.dtype))
    return z * jax.nn.sigmoid(z @ w_glu + b_glu)


def _token_mixing(h, w_in, w_out, out_gain, qk_gain, na_rpb, t5_bias, ret_decay_logit,
                  s5_a_re, s5_a_im, s5_log_step, s5_b_re, s5_b_im, s5_c_re, s5_c_im,
                  s5_d, s5_w_glu, s5_b_glu):
    b, L, _ = h.shape
    proj = h @ w_in
    p_na, p_dil, p_ret, p_s5 = jnp.split(
        proj, [NA_COLS, NA_COLS + DIL_COLS, NA_COLS + DIL_COLS + RET_COLS], axis=-1)
    qkv = p_na.reshape(b, L, 3, NA_HEADS, HEAD_DIM)
    y_na = _neighbourhood_attn(_rms(qkv[:, :, 0], qk_gain[0, 0]), _rms(qkv[:, :, 1], qk_gain[0, 1]),
                               qkv[:, :, 2], na_rpb)
    qkv = p_dil.reshape(b, L, 3, DIL_HEADS, HEAD_DIM)
    y_dil = _dilated_attn(_rms(qkv[:, :, 0], qk_gain[1, 0]), _rms(qkv[:, :, 1], qk_gain[1, 1]),
                          qkv[:, :, 2], t5_bias)
    q_r, k_r, v_r, g_r = jnp.split(p_ret, [RET_QK, 2 * RET_QK, 2 * RET_QK + RET_V], axis=-1)
    y_ret = _retention(q_r.reshape(b, L, RET_HEADS, RET_KEY_DIM), k_r.reshape(b, L, RET_HEADS, RET_KEY_DIM),
                       v_r.reshape(b, L, RET_HEADS, RET_VAL_DIM), g_r, ret_decay_logit)
    y_s5 = _s5(p_s5, s5_a_re, s5_a_im, s5_log_step, s5_b_re, s5_b_im, s5_c_re, s5_c_im,
               s5_d, s5_w_glu, s5_b_glu)
    y = jnp.stack([t.astype(h.dtype) for t in (y_na, y_dil, y_ret, y_s5)], axis=2)
    y = _rms(y, out_gain.reshape(N_MIXERS, GROUP_WIDTH)).reshape(b, L, D_MIX)
    return y @ w_out


def setup_inputs(seed: int = 0) -> dict:
    key = jax.random.key(seed)
    ks = jax.random.split(key, 24)
    f32 = jnp.float32

    def nrm(k, shape, scale):
        return jax.random.normal(k, shape, f32) * scale

    gam = 1.0 - 2.0 ** (-5.0 - jnp.arange(RET_HEADS, dtype=f32))
    n_idx = jnp.arange(S5_STATE, dtype=f32)
    return {
        "x": nrm(ks[0], (BATCH, SEQ, D_MODEL), 1.0),
        "norm_gain": 1.0 + nrm(ks[1], (DEPTH, 3, D_MODEL), 0.02),
        "w_in": nrm(ks[2], (DEPTH, D_MODEL, D_IN), D_MODEL ** -0.5),
        "w_out": nrm(ks[3], (DEPTH, D_MIX, D_MODEL), D_MIX ** -0.5),
        "out_gain": 1.0 + nrm(ks[4], (DEPTH, D_MIX), 0.02),
        "qk_gain": 1.0 + nrm(ks[5], (DEPTH, 2, 2, HEAD_DIM), 0.02),
        "na_rpb": nrm(ks[6], (DEPTH, NA_HEADS, 2 * NA_WIN_ROWS - 1, 2 * NA_WIN_COLS - 1), 0.1),
        "t5_bias": nrm(ks[7], (T5_BUCKETS, DIL_HEADS), 0.1),
        "ret_decay_logit": jnp.log(gam / (1.0 - gam))[None, None, :] + nrm(ks[8], (DEPTH, 2, RET_HEADS), 0.05),
        "s5_a_re": -0.5 + nrm(ks[9], (DEPTH, 2, S5_GROUPS, S5_STATE), 0.01),
        "s5_a_im": jnp.pi * n_idx + nrm(ks[10], (DEPTH, 2, S5_GROUPS, S5_STATE), 0.01),
        "s5_log_step": jax.random.uniform(ks[11], (DEPTH, 2, S5_GROUPS), f32, math.log(1e-3), math.log(1e-1)),
        "s5_b_re": nrm(ks[12], (DEPTH, 2, S5_GROUPS, S5_STATE, S5_GROUP), (2 * S5_GROUP) ** -0.5),
        "s5_b_im": nrm(ks[13], (DEPTH, 2, S5_GROUPS, S5_STATE, S5_GROUP), (2 * S5_GROUP) ** -0.5),
        "s5_c_re": nrm(ks[14], (DEPTH, S5_GROUPS, S5_GROUP, S5_STATE), (2 * S5_STATE) ** -0.5),
        "s5_c_im": nrm(ks[15], (DEPTH, S5_GROUPS, S5_GROUP, S5_STATE), (2 * S5_STATE) ** -0.5),
        "s5_d": nrm(ks[16], (DEPTH, S5_WIDTH), 1.0),
        "s5_w_glu": nrm(ks[17], (DEPTH, S5_WIDTH, S5_WIDTH), S5_WIDTH ** -0.5),
        "s5_b_glu": nrm(ks[18], (DEPTH, S5_WIDTH), 0.01),
        "ffn_w_gate": nrm(ks[19], (DEPTH, 2, D_MODEL, D_FF), D_MODEL ** -0.5),
        "ffn_w_up": nrm(ks[20], (DEPTH, 2, D_MODEL, D_FF), D_MODEL ** -0.5),
        "ffn_w_down": nrm(ks[21], (DEPTH, 2, D_FF, D_MODEL), D_FF ** -0.5),
    }


def reference(x, norm_gain, w_in, w_out, out_gain, qk_gain, na_rpb, t5_bias, ret_decay_logit,
              s5_a_re, s5_a_im, s5_log_step, s5_b_re, s5_b_im, s5_c_re, s5_c_im, s5_d,
              s5_w_glu, s5_b_glu, ffn_w_gate, ffn_w_up, ffn_w_down):
    for l in range(DEPTH):
        h = _rms(x, norm_gain[l, 0])
        x = x + 0.5 * _swiglu(h, ffn_w_gate[l, 0], ffn_w_up[l, 0], ffn_w_down[l, 0])
        h = _rms(x, norm_gain[l, 1])
        x = x + _token_mixing(h, w_in[l], w_out[l], out_gain[l], qk_gain[l], na_rpb[l], t5_bias,
                              ret_decay_logit[l], s5_a_re[l], s5_a_im[l], s5_log_step[l],
                              s5_b_re[l], s5_b_im[l], s5_c_re[l], s5_c_im[l], s5_d[l],
                              s5_w_glu[l], s5_b_glu[l])
        h = _rms(x, norm_gain[l, 2])
        x = x + 0.5 * _swiglu(h, ffn_w_gate[l, 1], ffn_w_up[l, 1], ffn_w_down[l, 1])
    return x
```

```python
import math
from contextlib import ExitStack
import numpy as np
import ml_dtypes
import concourse.bass as bass
import concourse.mybir as mybir
from concourse.bass_utils import run_bass_kernel_spmd


F32 = mybir.dt.float32
BF16 = mybir.dt.bfloat16
ALU = mybir.AluOpType
AF = mybir.ActivationFunctionType
AX = mybir.AxisListType


class Buf:
    __slots__ = ("name", "last_w", "readers", "dsem", "dcnt")

    def __init__(self, name):
        self.name = name
        self.last_w = None
        self.readers = []
        self.dsem = None
        self.dcnt = 0


class Sched:
    EPOCH = 12000

    def __init__(self, nc, es):
        self.nc = nc
        self.es = es
        self.eng = {"pe": nc.tensor, "act": nc.scalar, "dve": nc.vector,
                    "pool": nc.gpsimd, "sp": nc.sync}
        self.cnt = {e: 0 for e in self.eng}
        self.sems = {e: [] for e in self.eng}
        self.seen = {e: {} for e in self.eng}
        self.nsem = 0
        self.nwait = 0
        self.dma_all = {}

    def newsem(self, name):
        self.nsem += 1
        return self.es.enter_context(self.nc.semaphore(name))

    def _esem(self, e, ep):
        while len(self.sems[e]) <= ep:
            self.sems[e].append(self.newsem("s_%s_%d" % (e, len(self.sems[e]))))
        return self.sems[e][ep]

    def _wait(self, e, tok):
        seen = self.seen[e]
        if tok[0] == "c":
            _, f, n = tok
            if seen.get(f, 0) >= n:
                return
            seen[f] = n
            ep = (n - 1) // self.EPOCH
            self.eng[e].wait_ge(self._esem(f, ep), n - ep * self.EPOCH)
        else:
            _, sem, k = tok
            key = ("d", id(sem))
            if seen.get(key, 0) >= k:
                return
            seen[key] = k
            self.eng[e].wait_ge(sem, 16 * k)
        self.nwait += 1

    def op(self, e, fn, reads=(), writes=(), dma=None):
        deps = []
        for b in reads:
            if b.last_w is not None:
                deps.append(b.last_w)
        for b in writes:
            if b.last_w is not None:
                deps.append(b.last_w)
            deps.extend(b.readers)
        for d in deps:
            self._wait(e, d)
        inst = fn(self.eng[e])
        if dma is not None:
            if dma.dsem is None or dma.dcnt >= 1500:
                dma.dsem = self.newsem("d_" + dma.name)
                dma.dcnt = 0
            dma.dcnt += 1
            inst.then_inc(dma.dsem, 16)
            tok = ("d", dma.dsem, dma.dcnt)
            self.dma_all[id(dma.dsem)] = (dma.dsem, dma.dcnt)
        else:
            self.cnt[e] += 1
            n = self.cnt[e]
            ep = (n - 1) // self.EPOCH
            inst.then_inc(self._esem(e, ep), 1)
            tok = ("c", e, n)
        for b in reads:
            if tok[0] == "c":
                b.readers = [r for r in b.readers if not (r[0] == "c" and r[1] == tok[1])]
            b.readers.append(tok)
        for b in writes:
            b.last_w = tok
            b.readers = []
        return tok

    def barrier(self):
        for e in self.eng:
            for f in self.eng:
                if self.cnt[f] > 0:
                    self._wait(e, ("c", f, self.cnt[f]))
            for sem, k in self.dma_all.values():
                self._wait(e, ("d", sem, k))

    def finish(self, bufs, e="sp"):
        for b in bufs:
            if b.last_w is not None:
                self._wait(e, b.last_w)
            for r in b.readers:
                self._wait(e, r)


class Pool:
    def __init__(self, nc, es, name, shape, dtype, n, psum=False):
        self.t = []
        self.b = []
        for i in range(n):
            if psum:
                t = es.enter_context(nc.psum_tensor("%s%d" % (name, i), shape, dtype))
            else:
                t = es.enter_context(nc.sbuf_tensor("%s%d" % (name, i), shape, dtype))
            self.t.append(t)
            self.b.append(Buf("%s%d" % (name, i)))
        self.i = 0
        self.n = n

    def next(self):
        i = self.i
        self.i = (i + 1) % self.n
        return self.t[i], self.b[i]


NTOK = 2048
D = 2048
DFF = 5632
DIN = 5120
EPS = 1e-6


def dram_in(nc, name, shape, dt=F32):
    return nc.dram_tensor(name, list(shape), dt, kind="ExternalInput").ap()


def dram_out(nc, name, shape, dt=F32):
    return nc.dram_tensor(name, list(shape), dt, kind="ExternalOutput").ap()


class Ctx:
    pass


def setup_common(nc, es, S):
    c = Ctx()
    c.nc, c.es, c.S = nc, es, S
    c.ident_d = dram_in(nc, "ident", [128, 128], BF16)
    c.ident = es.enter_context(nc.sbuf_tensor("ident_sb", [128, 128], BF16))
    c.identB = Buf("ident")
    S.op("sp", lambda q: q.dma_start(out=c.ident[:], in_=c.ident_d), writes=[c.identB], dma=c.identB)
    c.psA = Pool(nc, es, "psA", [128, 512], F32, 2, psum=True)
    c.psB = Pool(nc, es, "psB", [128, 512], F32, 2, psum=True)
    c.psO = Pool(nc, es, "psO", [128, 512], F32, 2, psum=True)
    c.psT = Pool(nc, es, "psT", [128, 8, 128], BF16, 2, psum=True)
    c.stat = Pool(nc, es, "stat", [128, 16], F32, 6)
    c.xn = Pool(nc, es, "xn", [128, 2048], BF16, 2)
    c.junk = es.enter_context(nc.sbuf_tensor("junk", [128, 2048], BF16))
    c.junkB = Buf("junk")
    return c


def rms_to_hT(c, x_ap, xB, gcol, gcolB, hT, hTB, off):
    S = c.S
    st, sB = c.stat.next()
    S.op("act", lambda a: a.activation(out=c.junk[:], in_=x_ap, func=AF.Square, accum_out=st[:, 0:1]),
         reads=[xB], writes=[c.junkB, sB])
    S.op("dve", lambda v: v.tensor_scalar(out=st[:, 1:2], in0=st[:, 0:1], scalar1=1.0 / D, scalar2=EPS,
                                          op0=ALU.mult, op1=ALU.add), reads=[sB], writes=[sB])
    S.op("act", lambda a: a.activation(out=st[:, 2:3], in_=st[:, 1:2], func=AF.Sqrt), reads=[sB], writes=[sB])
    S.op("dve", lambda v: v.reciprocal(out=st[:, 3:4], in_=st[:, 2:3]), reads=[sB], writes=[sB])
    xn, xnB = c.xn.next()
    S.op("act", lambda a: a.activation(out=xn[:], in_=x_ap, func=AF.Copy, scale=st[:, 3:4]),
         reads=[xB, sB], writes=[xnB])
    for half in range(2):
        pt, pB = c.psT.next()

        def tr(pe, pt=pt, half=half):
            for j in range(8):
                kc = half * 8 + j
                i = pe.transpose(out=pt[:, j, :], in_=xn[:, kc * 128:(kc + 1) * 128], identity=c.ident[:])
            return i
        S.op("pe", tr, reads=[xnB, c.identB], writes=[pB])
        S.op("dve", lambda v, pt=pt, half=half: v.tensor_tensor(
            out=hT[:, half * 8:(half + 1) * 8, off:off + 128], in0=pt[:],
            in1=gcol[:, half * 8:(half + 1) * 8].unsqueeze(2).to_broadcast([128, 8, 128]), op=ALU.mult),
            reads=[pB, gcolB], writes=[hTB])


def ffn_supertile(c, xt, xtB, hT, hTB, wg, wu, wd, W):
    S = c.S
    NFG = DFF // 512
    for fg in range(NFG):
        wgs, wgB = W.wgu.next()
        wus, wuB = W.wgu.next()
        wds, wdB = W.wd.next()
        S.op("pool", lambda q, wgs=wgs, fg=fg: q.dma_start(
            out=wgs[:], in_=wg[:, fg * 512:(fg + 1) * 512].rearrange("(kc p) n -> p kc n", p=128)),
            writes=[wgB], dma=wgB)
        S.op("pool", lambda q, wus=wus, fg=fg: q.dma_start(
            out=wus[:], in_=wu[:, fg * 512:(fg + 1) * 512].rearrange("(kc p) n -> p kc n", p=128)),
            writes=[wuB], dma=wuB)
        S.op("pool", lambda q, wds=wds, fg=fg: q.dma_start(
            out=wds[:], in_=wd[fg * 512:(fg + 1) * 512, :].rearrange("(c p) n -> p c n", p=128)),
            writes=[wdB], dma=wdB)
        aT, aTB = W.aT.next()
        for cc in range(4):
            pg, pgB = c.psA.next()
            pu, puB = c.psB.next()

            def mm(pe, pt, ws, cc=cc):
                for kc in range(16):
                    i = pe.matmul(pt[:], lhsT=ws[:, kc, cc * 128:(cc + 1) * 128], rhs=hT[:, kc, :],
                                  start=(kc == 0), stop=(kc == 15))
                return i
            S.op("pe", lambda pe, pg=pg, wgs=wgs: mm(pe, pg, wgs), reads=[wgB, hTB], writes=[pgB])
            S.op("pe", lambda pe, pu=pu, wus=wus: mm(pe, pu, wus), reads=[wuB, hTB], writes=[puB])
            sg, sgB = W.sg.next()
            S.op("act", lambda a, sg=sg, pg=pg: a.activation(out=sg[:], in_=pg[:], func=AF.Silu),
                 reads=[pgB], writes=[sgB])
            S.op("dve", lambda v, sg=sg, pu=pu, aT=aT, cc=cc: v.tensor_tensor(
                out=aT[:, cc, :], in0=sg[:], in1=pu[:], op=ALU.mult), reads=[sgB, puB], writes=[aTB])
        for tt in range(4):
            for dg in range(4):
                po, poB = c.psO.next()

                def mmo(pe, po=po, tt=tt, dg=dg, aT=aT, wds=wds):
                    for cc in range(4):
                        i = pe.matmul(po[:], lhsT=aT[:, cc, tt * 128:(tt + 1) * 128],
                                      rhs=wds[:, cc, dg * 512:(dg + 1) * 512], start=(cc == 0), stop=(cc == 3))
                    return i
                S.op("pe", mmo, reads=[aTB, wdB], writes=[poB])
                S.op("dve", lambda v, po=po, tt=tt, dg=dg: v.scalar_tensor_tensor(
                    out=xt[tt][:, dg * 512:(dg + 1) * 512], in0=po[:], scalar=0.5,
                    in1=xt[tt][:, dg * 512:(dg + 1) * 512], op0=ALU.mult, op1=ALU.add),
                    reads=[poB], writes=[xtB[tt]])


class WPools:
    pass


def make_wpools(nc, es):
    W = WPools()
    W.wgu = Pool(nc, es, "wgu", [128, 16, 512], BF16, 4)
    W.wd = Pool(nc, es, "wd", [128, 4, 2048], BF16, 2)
    W.aT = Pool(nc, es, "aT", [128, 4, 512], BF16, 2)
    W.sg = Pool(nc, es, "sg", [128, 512], F32, 2)
    return W


class InprojRes:
    pass


def make_inproj_res(c, nc, es, S):
    R = InprojRes()
    R.gqk_d = dram_in(nc, "gqk", [128, 4, 64])
    R.rot_d = dram_in(nc, "rot", [NTOK, 2, 8, 32])
    R.gqk = es.enter_context(nc.sbuf_tensor("gqk_sb", [128, 4, 64], F32)); R.gqkB = Buf("gqk")
    S.op("sp", lambda q: q.dma_start(out=R.gqk[:], in_=R.gqk_d), writes=[R.gqkB], dma=R.gqkB)
    R.rot = Pool(nc, es, "rot_sb", [128, 2, 8, 32], F32, 2)
    R.tmpA = Pool(nc, es, "ipA", [128, 512], F32, 2)
    R.tmpB = Pool(nc, es, "ipB", [128, 512], F32, 2)
    R.tb = Pool(nc, es, "ipb16", [128, 512], BF16, 3)
    R.stage = Pool(nc, es, "ipstage", [128, 4, 512], BF16, 2)
    R.sgo = Pool(nc, es, "ipsg", [128, 512], F32, 2)
    R.qTA = dram_out(nc, "qTA", [512, NTOK], BF16); R.kTA = dram_out(nc, "kTA", [512, NTOK], BF16)
    R.qTB = dram_out(nc, "qTB", [512, NTOK], BF16); R.kTB = dram_out(nc, "kTB", [512, NTOK], BF16)
    R.vA = dram_out(nc, "vA", [NTOK, 512], BF16); R.vB = dram_out(nc, "vB", [NTOK, 512], BF16)
    R.qkTR = dram_out(nc, "qkTR", [512, NTOK], BF16)
    R.kR = dram_out(nc, "kR", [NTOK, 256], BF16)
    R.vR = dram_out(nc, "vR", [NTOK, 512], BF16)
    R.sgate = dram_out(nc, "sgate", [NTOK, 512], F32)
    R.uT = dram_out(nc, "uT", [512, NTOK], BF16)
    return R


def inproj_supertile(c, R, hT, hTB, win, W, st):
    S = c.S
    tokbase = st * 512
    for cg in range(10):
        ws, wB = W.wgu.next()
        S.op("pool", lambda q, ws=ws, cg=cg: q.dma_start(
            out=ws[:], in_=win[:, cg * 512:(cg + 1) * 512].rearrange("(kc p) n -> p kc n", p=128)),
            writes=[wB], dma=wB)
        need_T = cg in (0, 1, 3, 4, 6, 9)
        if need_T:
            stg, stgB = R.stage.next()
        for tt in range(4):
            po, poB = c.psO.next()
            t0 = tokbase + tt * 128

            def mm(pe, po=po, ws=ws, tt=tt):
                for kc in range(16):
                    i = pe.matmul(po[:], lhsT=hT[:, kc, tt * 128:(tt + 1) * 128], rhs=ws[:, kc, :],
                                  start=(kc == 0), stop=(kc == 15))
                return i
            S.op("pe", mm, reads=[wB, hTB], writes=[poB])
            tb, tbB = R.tb.next()
            if cg in (0, 1, 3, 4):
                gi = {0: 0, 1: 1, 3: 2, 4: 3}[cg]
                ta, taB = R.tmpA.next()
                st_, sB = c.stat.next()
                S.op("act", lambda a, ta=ta, po=po: a.activation(out=ta[:], in_=po[:], func=AF.Square),
                     reads=[poB], writes=[taB])
                S.op("dve", lambda v, ta=ta, st_=st_: v.reduce_sum(
                    out=st_[:, 0:8], in_=ta[:].rearrange("p (h e) -> p h e", e=64), axis=AX.X),
                    reads=[taB], writes=[sB])
                S.op("dve", lambda v, st_=st_: v.tensor_scalar(out=st_[:, 8:16], in0=st_[:, 0:8], scalar1=1.0 / 64,
                                                            scalar2=EPS, op0=ALU.mult, op1=ALU.add),
                     reads=[sB], writes=[sB])
                S.op("act", lambda a, st_=st_: a.activation(out=st_[:, 0:8], in_=st_[:, 8:16], func=AF.Sqrt),
                     reads=[sB], writes=[sB])
                S.op("dve", lambda v, st_=st_: v.reciprocal(out=st_[:, 8:16], in_=st_[:, 0:8]), reads=[sB], writes=[sB])
                t2, t2B = R.tmpB.next()
                S.op("dve", lambda v, t2=t2, po=po, st_=st_: v.tensor_tensor(
                    out=t2[:].rearrange("p (h e) -> p h e", e=64), in0=po[:].rearrange("p (h e) -> p h e", e=64),
                    in1=st_[:, 8:16].unsqueeze(2).to_broadcast([128, 8, 64]), op=ALU.mult),
                    reads=[poB, sB], writes=[t2B])
                S.op("pool", lambda g, t2=t2, tb=tb, gi=gi: g.tensor_tensor(
                    out=tb[:].rearrange("p (h e) -> p h e", e=64), in0=t2[:].rearrange("p (h e) -> p h e", e=64),
                    in1=R.gqk[:, gi, :].unsqueeze(1).to_broadcast([128, 8, 64]), op=ALU.mult),
                    reads=[t2B, R.gqkB], writes=[tbB])
            elif cg in (2, 5, 7, 9):
                S.op("act", lambda a, tb=tb, po=po: a.activation(out=tb[:], in_=po[:], func=AF.Copy),
                     reads=[poB], writes=[tbB])
                if cg != 9:
                    dst = {2: R.vA, 5: R.vB, 7: R.vR}[cg]
                    S.op("sp", lambda q, tb=tb, dst=dst, t0=t0: q.dma_start(out=dst[t0:t0 + 128, :], in_=tb[:]),
                         reads=[tbB], dma=tbB)
            elif cg == 8:
                so, soB = R.sgo.next()
                S.op("act", lambda a, so=so, po=po: a.activation(out=so[:], in_=po[:], func=AF.Silu),
                     reads=[poB], writes=[soB])
                S.op("sp", lambda q, so=so, t0=t0: q.dma_start(out=R.sgate[t0:t0 + 128, :], in_=so[:]),
                     reads=[soB], dma=soB)
            elif cg == 6:
                rt, rtB = R.rot.next()
                S.op("sp", lambda q, rt=rt, t0=t0: q.dma_start(out=rt[:], in_=R.rot_d[t0:t0 + 128]),
                     writes=[rtB], dma=rtB)
                ta, taB = R.tmpA.next()
                t2, t2B = R.tmpB.next()
                pv = po[:].rearrange("p (h two e) -> p h two e", two=2, e=32)
                tav = ta[:].rearrange("p (h two e) -> p h two e", two=2, e=32)
                t2v = t2[:].rearrange("p (h two e) -> p h two e", two=2, e=32)
                tbv = tb[:].rearrange("p (h two e) -> p h two e", two=2, e=32)
                S.op("dve", lambda v, rt=rt: v.tensor_tensor(out=tav[:, :, 0, :], in0=pv[:, :, 0, :], in1=rt[:, 0], op=ALU.mult),
                     reads=[poB, rtB], writes=[taB])
                S.op("dve", lambda v, rt=rt: v.tensor_tensor(out=tav[:, :, 1, :], in0=pv[:, :, 0, :], in1=rt[:, 1], op=ALU.mult),
                     reads=[poB, rtB], writes=[taB])
                S.op("dve", lambda v, rt=rt: v.tensor_tensor(out=t2v[:, :, 0, :], in0=pv[:, :, 1, :], in1=rt[:, 1], op=ALU.mult),
                     reads=[poB, rtB], writes=[t2B])
                S.op("dve", lambda v, rt=rt: v.tensor_tensor(out=t2v[:, :, 1, :], in0=pv[:, :, 1, :], in1=rt[:, 0], op=ALU.mult),
                     reads=[poB, rtB], writes=[t2B])
                S.op("pool", lambda g: g.tensor_tensor(out=tbv[:, :, 0, :], in0=tav[:, :, 0, :], in1=t2v[:, :, 0, :], op=ALU.subtract),
                     reads=[taB, t2B], writes=[tbB])
                S.op("pool", lambda g: g.tensor_tensor(out=tbv[:, :, 1, :], in0=tav[:, :, 1, :], in1=t2v[:, :, 1, :], op=ALU.add),
                     reads=[taB, t2B], writes=[tbB])
                S.op("sp", lambda q, tb=tb, t0=t0: q.dma_start(out=R.kR[t0:t0 + 128, :], in_=tb[:, 256:512]),
                     reads=[tbB], dma=tbB)
            if need_T:
                pt, pB = c.psT.next()

                def tr(pe, pt=pt, tb=tb):
                    for j in range(4):
                        i = pe.transpose(out=pt[:, j, :], in_=tb[:, j * 128:(j + 1) * 128], identity=c.ident[:])
                    return i
                S.op("pe", tr, reads=[tbB, c.identB], writes=[pB])
                S.op("act", lambda a, pt=pt, stg=stg, tt=tt: a.activation(
                    out=stg[:, :, tt * 128:(tt + 1) * 128], in_=pt[:, 0:4, :], func=AF.Copy),
                    reads=[pB], writes=[stgB])
        if need_T:
            dst = {0: R.qTA, 1: R.kTA, 3: R.qTB, 4: R.kTB, 6: R.qkTR, 9: R.uT}[cg]
            S.op("sp", lambda q, stg=stg, dst=dst: q.dma_start(
                out=dst[:, tokbase:tokbase + 512].rearrange("(c p) n -> p c n", p=128), in_=stg[:]),
                reads=[stgB], dma=stgB)


def build_LA(inproj=True, sfin=True):
    nc = bass.Bass("TRN2", target_bir_lowering=False)
    x = dram_in(nc, "x", [NTOK, D])
    g0 = dram_in(nc, "g0", [128, 16])
    wg = dram_in(nc, "wg", [D, DFF]); wu = dram_in(nc, "wu", [D, DFF]); wd = dram_in(nc, "wd", [DFF, D])
    x1 = dram_out(nc, "x1", [NTOK, D])
    if inproj:
        g1 = dram_in(nc, "g1", [128, 16])
        win = dram_in(nc, "win", [D, DIN])
    with ExitStack() as es0:
        S = Sched(nc, es0)
        with ExitStack() as es:
            c = setup_common(nc, es, S)
            W = make_wpools(nc, es)
            gcol = es.enter_context(nc.sbuf_tensor("gcol", [128, 2, 16], F32)); gB = Buf("gcol")
            S.op("sp", lambda q: q.dma_start(out=gcol[:, 0, :], in_=g0), writes=[gB], dma=gB)
            if inproj:
                S.op("sp", lambda q: q.dma_start(out=gcol[:, 1, :], in_=g1), writes=[gB], dma=gB)
                R = make_inproj_res(c, nc, es, S)
            hT = es.enter_context(nc.sbuf_tensor("hT", [128, 16, 512], BF16)); hTB = Buf("hT")
            xt = []; xtB = []
            for tt in range(4):
                xt.append(es.enter_context(nc.sbuf_tensor("xt%d" % tt, [128, 2048], F32))); xtB.append(Buf("xt%d" % tt))
            for st in range(NTOK // 512):
                for tt in range(4):
                    r0 = st * 512 + tt * 128
                    S.op("sp", lambda q, tt=tt, r0=r0: q.dma_start(out=xt[tt][:], in_=x[r0:r0 + 128, :]),
                         writes=[xtB[tt]], dma=xtB[tt])
                for tt in range(4):
                    rms_to_hT(c, xt[tt][:], xtB[tt], gcol[:, 0, :], gB, hT, hTB, tt * 128)
                ffn_supertile(c, xt, xtB, hT, hTB, wg, wu, wd, W)
                for tt in range(4):
                    r0 = st * 512 + tt * 128
                    S.op("sp", lambda q, tt=tt, r0=r0: q.dma_start(out=x1[r0:r0 + 128, :], in_=xt[tt][:]),
                         reads=[xtB[tt]], dma=xtB[tt])
                if inproj:
                    for tt in range(4):
                        rms_to_hT(c, xt[tt][:], xtB[tt], gcol[:, 1, :], gB, hT, hTB, tt * 128)
                    inproj_supertile(c, R, hT, hTB, win, W, st)
            fin = list(xtB)
            if inproj:
                fin += R.stage.b + R.tb.b + R.sgo.b
            S.finish(fin, "sp")
            S.barrier()
        if inproj and sfin:
            with ExitStack() as es2:
                c2 = Ctx()
                c2.psO = Pool(nc, es2, "psO2", [128, 512], F32, 2, psum=True)
                dl = dram_in(nc, "ret_dl", [128, 8]); esf = dram_in(nc, "ret_esf", [128, 16, 2])
                sf = dram_out(nc, "sfin", [8, 64, 128])
                soB = ret_sfin(c2, nc, es2, S, R.kR, R.vR, dl, esf, sf)
                S.finish([soB], "sp")
        print("LA sems", S.nsem, "waits", S.nwait, "cnt", S.cnt)
    return nc


def v1_lhsT(Vt, idx, h):
    return Vt[(slice(None),) + tuple(idx) + (h, slice(None))]


class AttnRes:
    pass


def make_attn_res(c, nc, es, S, sbuf=True):
    A = AttnRes()
    A.psS = Pool(nc, es, "psS", [128, 4, 128], F32, 2, psum=True)
    A.psV = Pool(nc, es, "psV", [128, 128], F32, 2, psum=True)
    A.psF = Pool(nc, es, "psF", [128, 128], F32, 2, psum=True)
    if not sbuf:
        return A
    A.tmp = Pool(nc, es, "atmp", [128, 4, 128], F32, 3)
    A.pb = Pool(nc, es, "apb", [128, 4, 128], BF16, 3)
    A.identf_d = dram_in(nc, "identf", [128, 128], F32)
    A.identf = es.enter_context(nc.sbuf_tensor("identf_sb", [128, 128], F32)); A.identfB = Buf("identf")
    S.op("sp", lambda q: q.dma_start(out=A.identf[:], in_=A.identf_d), writes=[A.identfB], dma=A.identfB)
    A.rz = Pool(nc, es, "arz", [128, 2], F32, 4)
    A.yst = Pool(nc, es, "ayst", [128, 16, 64], F32, 2)
    A.KT = Pool(nc, es, "aKT", [64, 4096], BF16, 2)
    A.QT = Pool(nc, es, "aQT", [64, 2048], BF16, 2)
    return A


def attn_unit(c, A, qaps, kaps, vaps, bias_ap, biasB, rB, accs, first):
    S = c.S
    ps, psB = A.psS.next()
    n = len(qaps)

    def mm(pe):
        for i in range(n):
            ins = pe.matmul(ps[:, i, :], lhsT=kaps[i], rhs=qaps[i], start=True, stop=True)
        return ins
    S.op("pe", mm, reads=rB, writes=[psB])
    tm, tmB = A.tmp.next()
    S.op("dve", lambda v: v.scalar_tensor_tensor(out=tm[:, 0:n, :], in0=ps[:, 0:n, :], scalar=0.125, in1=bias_ap,
                                                 op0=ALU.mult, op1=ALU.add), reads=[psB, biasB], writes=[tmB])
    pb, pbB = A.pb.next()
    S.op("act", lambda a: a.activation(out=pb[:, 0:n, :], in_=tm[:, 0:n, :], func=AF.Exp), reads=[tmB], writes=[pbB])
    for (acc_ap, accB, slots) in accs:
        pv, pvB = A.psV.next()

        def mv(pe, pv=pv, slots=slots):
            for k, i in enumerate(slots):
                ins = pe.matmul(pv[:], lhsT=vaps[i], rhs=pb[:, i, :], start=(k == 0), stop=(k == len(slots) - 1))
            return ins
        S.op("pe", mv, reads=[pbB] + rB, writes=[pvB])
        if first:
            S.op("act", lambda a, pv=pv, acc_ap=acc_ap: a.activation(out=acc_ap, in_=pv[:], func=AF.Copy),
                 reads=[pvB], writes=[accB])
        else:
            S.op("dve", lambda v, pv=pv, acc_ap=acc_ap: v.tensor_tensor(out=acc_ap, in0=acc_ap, in1=pv[:], op=ALU.add),
                 reads=[pvB], writes=[accB])


def attn_finalize_head(c, A, acc, accB, y_dram, h):
    S = c.S
    yst, ystB = A.yst.next()
    for t in range(16):
        pf, pfB = A.psF.next()
        S.op("pe", lambda pe, pf=pf, t=t: pe.transpose(out=pf[:], in_=acc[:, t * 128:(t + 1) * 128], identity=A.identf[:]),
             reads=[accB, A.identfB], writes=[pfB])
        rz, rzB = A.rz.next()
        S.op("dve", lambda v, pf=pf, rz=rz: v.reciprocal(out=rz[:, 0:1], in_=pf[:, 64:65]), reads=[pfB], writes=[rzB])
        S.op("dve", lambda v, pf=pf, rz=rz, t=t: v.tensor_scalar(out=yst[:, t, :], in0=pf[:, 0:64], scalar1=rz[:, 0:1],
                                                              scalar2=None, op0=ALU.mult),
             reads=[pfB, rzB], writes=[ystB])
    S.op("sp", lambda q: q.dma_start(out=y_dram[:, h * 64:(h + 1) * 64].rearrange("(t p) e -> p t e", p=128), in_=yst[:]),
         reads=[ystB], dma=ystB)
    return ystB


def build_attn_A(c, A, nc, es, S):
    qT = dram_in(nc, "qTA", [512, 2048], BF16)
    kT = dram_in(nc, "kTAh", [512, 3072], BF16)
    vh = dram_in(nc, "vAh", [3072, 512], BF16)
    val = dram_in(nc, "valA", [3072, 512], BF16)
    bias = dram_in(nc, "biasA", [8, 5, 128, 8, 128], F32)
    yA = dram_out(nc, "yA", [2048, 512], F32)
    Vt = es.enter_context(nc.sbuf_tensor("VtA", [128, 24, 8, 128], BF16)); VtB = Buf("VtA")
    for hh in range(8):
        S.op("sp", lambda q, hh=hh: q.dma_start(out=Vt[:, :, hh, 0:64], in_=vh[:, hh * 64:(hh + 1) * 64].rearrange("(c p) e -> p c e", p=128)),
             writes=[VtB], dma=VtB)
        S.op("sp", lambda q, hh=hh: q.dma_start(out=Vt[:, :, hh, 64:128], in_=val[:, hh * 64:(hh + 1) * 64].rearrange("(c p) e -> p c e", p=128)),
             writes=[VtB], dma=VtB)
    bt = Pool(nc, es, "biasA_sb", [128, 5, 8, 128], F32, 2)
    accp = Pool(nc, es, "accA", [128, 2048], F32, 2)
    fin = []
    for h in range(8):
        K, KB = A.KT.next()
        Q, QB = A.QT.next()
        b_, bB = bt.next()
        S.op("sp", lambda q, K=K, h=h: q.dma_start(out=K[:, 0:3072], in_=kT[h * 64:(h + 1) * 64, :]), writes=[KB], dma=KB)
        S.op("sp", lambda q, Q=Q, h=h: q.dma_start(out=Q[:], in_=qT[h * 64:(h + 1) * 64, :]), writes=[QB], dma=QB)
        S.op("sp", lambda q, b_=b_, h=h: q.dma_start(out=b_[:], in_=bias[h].rearrange("t p c q -> p t c q")), writes=[bB], dma=bB)
        acc, accB = accp.next()
        for b in range(16):
            ty = {0: 1, 1: 2, 14: 3, 15: 4}.get(b, 0)
            qap = Q[:, b * 128:(b + 1) * 128]
            for half in range(2):
                kaps = [K[:, (b + half * 4 + i) * 128:(b + half * 4 + i + 1) * 128] for i in range(4)]
                vaps = [v1_lhsT(Vt, (b + half * 4 + i,), h) for i in range(4)]
                attn_unit(c, A, [qap] * 4, kaps, vaps, b_[:, ty, half * 4:(half + 1) * 4, :], bB, [KB, QB, VtB],
                          [(acc[:, b * 128:(b + 1) * 128], accB, [0, 1, 2, 3])], first=(half == 0))
        fin.append(attn_finalize_head(c, A, acc, accB, yA, h))
    return fin


def build_attn_B(c, A, nc, es, S):
    qT = dram_in(nc, "qTB", [512, 2048], BF16)
    kT = dram_in(nc, "kTBh", [512, 4096], BF16)
    vh = dram_in(nc, "vBh", [4096 + 16, 512], BF16)
    val = dram_in(nc, "valB", [4096 + 16, 512], BF16)
    bias = dram_in(nc, "biasB", [3, 8, 128, 2, 128], F32)
    yB = dram_out(nc, "yB", [2048, 512], F32)
    Vt = es.enter_context(nc.sbuf_tensor("VtB", [128, 32, 8, 128], BF16)); VtB = Buf("VtB")
    bt = es.enter_context(nc.sbuf_tensor("biasB_sb", [128, 3, 8, 2, 128], F32)); btB = Buf("biasBt")
    S.op("sp", lambda q: q.dma_start(out=bt[:].rearrange("p a h c q -> p (a h) c q"),
                                     in_=bias.rearrange("a h p c q -> p (a h) c q")), writes=[btB], dma=btB)
    accs = []
    for h in range(8):
        accs.append((es.enter_context(nc.sbuf_tensor("accB%d" % h, [128, 2048], F32)), Buf("accB%d" % h)))
    for bi, d in enumerate((1, 4, 16)):
        nsub = 2048 // d
        nblk = nsub // 128
        nch = nblk + 1
        for rho in range(d):
            base = 1024 + rho - 64 * d
            for hh in range(8):
                src = vh[base:base + d * 128 * nch, hh * 64:(hh + 1) * 64].rearrange("(m p dd) e -> p m dd e", p=128, dd=d)[:, :, 0, :]
                srcv = val[base:base + d * 128 * nch, hh * 64:(hh + 1) * 64].rearrange("(m p dd) e -> p m dd e", p=128, dd=d)[:, :, 0, :]
                S.op("sp", lambda q, rho=rho, src=src, hh=hh: q.dma_start(out=Vt[:, rho * nch:(rho + 1) * nch, hh, 0:64], in_=src),
                     writes=[VtB], dma=VtB)
                S.op("sp", lambda q, rho=rho, srcv=srcv, hh=hh: q.dma_start(out=Vt[:, rho * nch:(rho + 1) * nch, hh, 64:128], in_=srcv),
                     writes=[VtB], dma=VtB)
        for h in range(8):
            K, KB = A.KT.next()
            Q, QB = A.QT.next()
            S.op("sp", lambda q, K=K, h=h: q.dma_start(out=K[:], in_=kT[h * 64:(h + 1) * 64, :]), writes=[KB], dma=KB)
            S.op("sp", lambda q, Q=Q, h=h: q.dma_start(out=Q[:], in_=qT[h * 64:(h + 1) * 64, :]), writes=[QB], dma=QB)
            acc, accB = accs[h]
            units = [(rho, j) for rho in range(d) for j in range(nblk)]
            for u0 in range(0, len(units), 2):
                qaps, kaps, vaps, acl = [], [], [], []
                for ui, (rho, j) in enumerate(units[u0:u0 + 2]):
                    q0 = rho + d * 128 * j
                    qap = Q[:, q0:q0 + d * 127 + 1:d]
                    for m in range(2):
                        k0 = 1024 + rho + d * (128 * (j + m) - 64)
                        kaps.append(K[:, k0:k0 + d * 127 + 1:d])
                        qaps.append(qap)
                        vaps.append(v1_lhsT(Vt, (rho * nch + j + m,), h))
                    acl.append((acc[:, q0:q0 + d * 127 + 1:d], accB, [2 * ui, 2 * ui + 1]))
                nb = len(acl)
                a_ = bt[:, bi, h]
                bap = bass.AP(a_.tensor, a_.offset, [list(a_.ap[0]), [0, nb], [128, 2], [1, 128]])
                attn_unit_b(c, A, qaps, kaps, vaps, bap, btB, [KB, QB, VtB], acl, first=(bi == 0), nb=nb)
    fin = []
    for h in range(8):
        fin.append(attn_finalize_head(c, A, accs[h][0], accs[h][1], yB, h))
    return fin


def attn_unit_b(c, A, qaps, kaps, vaps, bias_ap, biasB, rB, accs, first, nb):
    S = c.S
    ps, psB = A.psS.next()
    n = len(qaps)

    def mm(pe):
        for i in range(n):
            ins = pe.matmul(ps[:, i, :], lhsT=kaps[i], rhs=qaps[i], start=True, stop=True)
        return ins
    S.op("pe", mm, reads=rB, writes=[psB])
    tm, tmB = A.tmp.next()
    S.op("dve", lambda v: v.scalar_tensor_tensor(
        out=tm[:, 0:n, :].rearrange("p (a c) q -> p a c q", c=2), in0=ps[:, 0:n, :].rearrange("p (a c) q -> p a c q", c=2),
        scalar=0.125, in1=bias_ap, op0=ALU.mult, op1=ALU.add), reads=[psB, biasB], writes=[tmB])
    pb, pbB = A.pb.next()
    S.op("act", lambda a: a.activation(out=pb[:, 0:n, :], in_=tm[:, 0:n, :], func=AF.Exp), reads=[tmB], writes=[pbB])
    for (acc_ap, accB, slots) in accs:
        pv, pvB = A.psV.next()

        def mv(pe, pv=pv, slots=slots):
            for k, i in enumerate(slots):
                ins = pe.matmul(pv[:], lhsT=vaps[i], rhs=pb[:, i, :], start=(k == 0), stop=(k == len(slots) - 1))
            return ins
        S.op("pe", mv, reads=[pbB] + rB, writes=[pvB])
        if first:
            S.op("act", lambda a, pv=pv, acc_ap=acc_ap: a.activation(out=acc_ap, in_=pv[:], func=AF.Copy),
                 reads=[pvB], writes=[accB])
        else:
            S.op("dve", lambda v, pv=pv, acc_ap=acc_ap: v.tensor_tensor(out=acc_ap, in0=acc_ap, in1=pv[:], op=ALU.add),
                 reads=[pvB], writes=[accB])


def setup_lb(nc, es, S):
    c = Ctx()
    c.nc, c.es, c.S = nc, es, S
    return c


def build_LB_attn(doA=True, doB=True):
    nc = bass.Bass("TRN2", target_bir_lowering=False)
    with ExitStack() as es:
        S = Sched(nc, es)
        c = setup_lb(nc, es, S)
        A = make_attn_res(c, nc, es, S)
        if doA:
            with ExitStack() as esA:
                fin = build_attn_A(c, A, nc, esA, S)
                S.finish(fin, "sp")
                S.barrier()
        if doB:
            with ExitStack() as esB:
                fin = build_attn_B(c, A, nc, esB, S)
                S.finish(fin, "sp")
                S.barrier()
        print("LB sems", S.nsem, "waits", S.nwait, "cnt", S.cnt)
    return nc


BIG = 1.0e7


def ret_loggamma(c, nc, es, S, dl_d):
    t = es.enter_context(nc.sbuf_tensor("ret_lg", [128, 4, 8], F32)); tB = Buf("ret_lg")
    S.op("sp", lambda q: q.dma_start(out=t[:, 0, :], in_=dl_d), writes=[tB], dma=tB)
    S.op("act", lambda a: a.activation(out=t[:, 1, :], in_=t[:, 0, :], func=AF.Exp, scale=-1.0), reads=[tB], writes=[tB])
    S.op("act", lambda a: a.activation(out=t[:, 2, :], in_=t[:, 1, :], func=AF.Ln, bias=1.0), reads=[tB], writes=[tB])
    S.op("dve", lambda v: v.tensor_scalar(out=t[:, 3, :], in0=t[:, 2, :], scalar1=-1.0, scalar2=None, op0=ALU.mult),
         reads=[tB], writes=[tB])
    return t[:, 3, :], tB


def ret_sfin(c, nc, es, S, kR, vR, dl_d, esf_d, sfin_out):
    lg, lgB = ret_loggamma(c, nc, es, S, dl_d)
    E = es.enter_context(nc.sbuf_tensor("sf_E", [128, 16, 2], F32)); EB = Buf("sf_E")
    S.op("sp", lambda q: q.dma_start(out=E[:], in_=esf_d), writes=[EB], dma=EB)
    dec = es.enter_context(nc.sbuf_tensor("sf_dec", [128, 16, 2, 4], F32)); decB = Buf("sf_dec")
    S.op("dve", lambda v: v.tensor_tensor(out=dec[:], in0=E[:].unsqueeze(3).to_broadcast([128, 16, 2, 4]),
                                          in1=lg.rearrange("p (d h) -> p d h", d=2).unsqueeze(1).to_broadcast([128, 16, 2, 4]),
                                          op=ALU.mult), reads=[EB, lgB], writes=[decB])
    S.op("act", lambda a: a.activation(out=dec[:], in_=dec[:], func=AF.Exp), reads=[decB], writes=[decB])
    kt = es.enter_context(nc.sbuf_tensor("sf_k", [128, 16, 256], BF16)); ktB = Buf("sf_k")
    vt = es.enter_context(nc.sbuf_tensor("sf_v", [128, 16, 512], BF16)); vtB = Buf("sf_v")
    S.op("sp", lambda q: q.dma_start(out=kt[:], in_=kR.rearrange("(t p) n -> p t n", p=128)), writes=[ktB], dma=ktB)
    S.op("sp", lambda q: q.dma_start(out=vt[:], in_=vR.rearrange("(t p) n -> p t n", p=128)), writes=[vtB], dma=vtB)
    kd = es.enter_context(nc.sbuf_tensor("sf_kd", [128, 16, 2, 256], BF16)); kdB = Buf("sf_kd")
    for d in range(2):
        S.op("dve", lambda v, d=d: v.tensor_tensor(
            out=kd[:, :, d, :].rearrange("p t (h e) -> p t h e", e=64), in0=kt[:].rearrange("p t (h e) -> p t h e", e=64),
            in1=dec[:, :, d, :].unsqueeze(3).to_broadcast([128, 16, 4, 64]), op=ALU.mult), reads=[ktB, decB], writes=[kdB])
    so = es.enter_context(nc.sbuf_tensor("sf_out", [64, 8, 128], F32)); soB = Buf("sf_out")
    for d in range(2):
        for h in range(4):
            ps, psB = c.psO.next()

            def mm(pe, ps=ps, d=d, h=h):
                for t in range(16):
                    i = pe.matmul(ps[0:64, 0:128], lhsT=kd[:, t, d, h * 64:(h + 1) * 64], rhs=vt[:, t, h * 128:(h + 1) * 128],
                                  start=(t == 0), stop=(t == 15))
                return i
            S.op("pe", mm, reads=[kdB, vtB], writes=[psB])
            S.op("act", lambda a, ps=ps, d=d, h=h: a.activation(out=so[:, d * 4 + h, :], in_=ps[0:64, 0:128], func=AF.Copy),
                 reads=[psB], writes=[soB])
    S.op("sp", lambda q: q.dma_start(out=sfin_out.rearrange("g k e -> k g e"), in_=so[:]), reads=[soB], dma=soB)
    return soB


def build_ret(c, A, nc, es, S):
    qkT = dram_in(nc, "qkTR", [512, 2048], BF16)
    kR = dram_in(nc, "kR", [2048, 256], BF16)
    vR = dram_in(nc, "vR", [2048, 512], BF16)
    sg = dram_in(nc, "sgate", [2048, 512], F32)
    sfa = dram_in(nc, "sfin_all", [8, 8, 64, 128], F32)
    dl_d = dram_in(nc, "ret_dl", [128, 8], F32)
    ncoef_d = dram_in(nc, "ret_ncoef", [128, 8, 2], F32)
    ekd_d = dram_in(nc, "ret_ekd", [128, 2], F32)
    eqd_d = dram_in(nc, "ret_eqd", [64, 2, 128], F32)
    emask_d = dram_in(nc, "ret_emask", [128, 2, 128], F32)
    yR = dram_out(nc, "yR", [2048, 512], F32)
    lg, lgB = ret_loggamma(c, nc, es, S, dl_d)
    lg3 = lg.rearrange("p (d h) -> p d h", d=2)

    def load(name, shape, src, dt=F32):
        t = es.enter_context(nc.sbuf_tensor(name, shape, dt)); b = Buf(name)
        S.op("sp", lambda q: q.dma_start(out=t[:], in_=src), writes=[b], dma=b)
        return t, b
    ncoef, ncoefB = load("r_ncoef", [128, 8, 2], ncoef_d)
    ekd, ekdB = load("r_ekd", [128, 2], ekd_d)
    eqd, eqdB = load("r_eqd", [64, 2, 128], eqd_d)
    emask, emaskB = load("r_emask", [128, 2, 128], emask_d)
    coef = es.enter_context(nc.sbuf_tensor("r_coef", [128, 8, 2, 4], F32)); coefB = Buf("r_coef")
    S.op("dve", lambda v: v.tensor_tensor(out=coef[:], in0=ncoef[:].unsqueeze(3).to_broadcast([128, 8, 2, 4]),
                                          in1=lg3.unsqueeze(1).to_broadcast([128, 8, 2, 4]), op=ALU.mult),
         reads=[ncoefB, lgB], writes=[coefB])
    S.op("act", lambda a: a.activation(out=coef[:], in_=coef[:], func=AF.Exp), reads=[coefB], writes=[coefB])
    kdt = es.enter_context(nc.sbuf_tensor("r_kdt", [128, 2, 4], F32)); kdtB = Buf("r_kdt")
    S.op("dve", lambda v: v.tensor_tensor(out=kdt[:], in0=ekd[:].unsqueeze(2).to_broadcast([128, 2, 4]), in1=lg3, op=ALU.mult),
         reads=[ekdB, lgB], writes=[kdtB])
    S.op("act", lambda a: a.activation(out=kdt[:], in_=kdt[:], func=AF.Exp), reads=[kdtB], writes=[kdtB])
    qdt = es.enter_context(nc.sbuf_tensor("r_qdt", [64, 2, 4, 128], F32)); qdtB = Buf("r_qdt")
    S.op("dve", lambda v: v.tensor_tensor(out=qdt[:], in0=eqd[:].unsqueeze(2).to_broadcast([64, 2, 4, 128]),
                                          in1=lg3[0:64].unsqueeze(3).to_broadcast([64, 2, 4, 128]), op=ALU.mult),
         reads=[eqdB, lgB], writes=[qdtB])
    S.op("act", lambda a: a.activation(out=qdt[:], in_=qdt[:], func=AF.Exp), reads=[qdtB], writes=[qdtB])
    dm = es.enter_context(nc.sbuf_tensor("r_dm", [128, 2, 4, 128], F32)); dmB = Buf("r_dm")
    S.op("dve", lambda v: v.tensor_tensor(out=dm[:], in0=emask[:].unsqueeze(2).to_broadcast([128, 2, 4, 128]),
                                          in1=lg3.unsqueeze(3).to_broadcast([128, 2, 4, 128]), op=ALU.mult),
         reads=[emaskB, lgB], writes=[dmB])
    S.op("act", lambda a: a.activation(out=dm[:], in_=dm[:], func=AF.Exp), reads=[dmB], writes=[dmB])
    dcomb = es.enter_context(nc.sbuf_tensor("r_dcomb", [128, 4, 128], F32)); dcB = Buf("r_dcomb")
    S.op("dve", lambda v: v.tensor_tensor(out=dcomb[:], in0=dm[:, 0], in1=dm[:, 1], op=ALU.add), reads=[dmB], writes=[dcB])
    c128 = es.enter_context(nc.sbuf_tensor("r_c128", [128, 2, 4], F32)); c128B = Buf("r_c128")
    S.op("act", lambda a: a.activation(out=c128[:], in_=lg3, func=AF.Exp, scale=128.0), reads=[lgB], writes=[c128B])
    sall = es.enter_context(nc.sbuf_tensor("r_sall", [64, 8, 8, 128], F32)); sallB = Buf("r_sall")
    for cc in range(8):
        S.op("sp", lambda q, cc=cc: q.dma_start(out=sall[:, cc], in_=sfa[cc].rearrange("g k e -> k g e")), writes=[sallB], dma=sallB)
    S.op("dve", lambda v: v.tensor_tensor(out=sall[:], in0=sall[:],
                                          in1=coef[0:64].rearrange("p c d h -> p c (d h)").unsqueeze(3).to_broadcast([64, 8, 8, 128]),
                                          op=ALU.mult), reads=[coefB], writes=[sallB])
    sin_ = es.enter_context(nc.sbuf_tensor("r_sin", [64, 8, 128], F32)); sinB = Buf("r_sin")
    S.op("dve", lambda v: v.reduce_sum(out=sin_[:], in_=sall[:].rearrange("p c g e -> p g e c"), axis=AX.X),
         reads=[sallB], writes=[sinB])
    QT, QTB = load("r_QT", [64, 4, 2048], qkT[0:256, :].rearrange("(h k) n -> k h n", k=64), BF16)
    KT, KTB = load("r_KT", [64, 4, 2048], qkT[256:512, :].rearrange("(h k) n -> k h n", k=64), BF16)
    kt, ktB = load("r_k", [128, 16, 256], kR.rearrange("(t p) n -> p t n", p=128), BF16)
    vt, vtB = load("r_v", [128, 16, 512], vR.rearrange("(t p) n -> p t n", p=128), BF16)
    qd = es.enter_context(nc.sbuf_tensor("r_qd", [64, 2, 4, 2048], BF16)); qdB = Buf("r_qd")
    kd = es.enter_context(nc.sbuf_tensor("r_kd", [128, 16, 2, 256], BF16)); kdB = Buf("r_kd")
    for d in range(2):
        for h in range(4):
            S.op("dve", lambda v, d=d, h=h: v.tensor_tensor(
                out=qd[:, d, h, :].rearrange("p (t j) -> p t j", j=128), in0=QT[:, h, :].rearrange("p (t j) -> p t j", j=128),
                in1=qdt[:, d, h, :].unsqueeze(1).to_broadcast([64, 16, 128]), op=ALU.mult), reads=[QTB, qdtB], writes=[qdB])
            S.op("pool", lambda g, d=d, h=h: g.tensor_scalar(
                out=kd[:, :, d, h * 64:(h + 1) * 64], in0=kt[:, :, h * 64:(h + 1) * 64], scalar1=kdt[:, d, h:h + 1],
                scalar2=None, op0=ALU.mult), reads=[ktB, kdtB], writes=[kdB])
    st32 = es.enter_context(nc.sbuf_tensor("r_st32", [64, 8, 128], F32)); st32B = [Buf("r_st32_%d" % i) for i in range(8)]
    stb = es.enter_context(nc.sbuf_tensor("r_stb", [64, 8, 16, 128], BF16)); stbB = [Buf("r_stb_%d" % i) for i in range(8)]
    for g in range(8):
        S.op("act", lambda a, g=g: a.activation(out=st32[:, g, :], in_=sin_[:, g, :], func=AF.Copy), reads=[sinB], writes=[st32B[g]])
    for step in range(16):
        for d in range(2):
            k = step if d == 0 else 15 - step
            for h in range(4):
                g = d * 4 + h
                S.op("act", lambda a, g=g, k=k: a.activation(out=stb[:, g, k, :], in_=st32[:, g, :], func=AF.Copy),
                     reads=[st32B[g]], writes=[stbB[g]])
                if step == 15:
                    continue
                ps, psB = A.psV.next()
                S.op("pe", lambda pe, ps=ps, d=d, h=h, k=k: pe.matmul(
                    ps[0:64, :], lhsT=kd[:, k, d, h * 64:(h + 1) * 64], rhs=vt[:, k, h * 128:(h + 1) * 128], start=True, stop=True),
                    reads=[kdB, vtB], writes=[psB])
                S.op("dve", lambda v, ps=ps, g=g, d=d, h=h: v.scalar_tensor_tensor(
                    out=st32[:, g, :], in0=st32[:, g, :], scalar=c128[0:64, d, h:h + 1], in1=ps[0:64, :],
                    op0=ALU.mult, op1=ALU.add), reads=[psB, c128B], writes=[st32B[g]])
    ytile = Pool(nc, es, "r_y", [128, 4, 128], F32, 2)
    sgt = Pool(nc, es, "r_sg", [128, 512], F32, 2)
    sq = Pool(nc, es, "r_sq", [128, 4, 128], F32, 2)
    pbt = Pool(nc, es, "r_pb", [128, 4, 128], BF16, 2)
    fin = []
    for k in range(16):
        ps, psB = A.psS.next()

        def mm(pe, ps=ps, k=k):
            for h in range(4):
                i = pe.matmul(ps[:, h, :], lhsT=KT[:, h, k * 128:(k + 1) * 128], rhs=QT[:, h, k * 128:(k + 1) * 128],
                              start=True, stop=True)
            return i
        S.op("pe", mm, reads=[KTB, QTB], writes=[psB])
        pb, pbB = pbt.next()
        S.op("dve", lambda v, ps=ps, pb=pb: v.tensor_tensor(out=pb[:], in0=ps[:], in1=dcomb[:], op=ALU.mult),
             reads=[psB, dcB], writes=[pbB])
        py, pyB = A.psS.next()

        def mo(pe, py=py, pb=pb, k=k):
            for h in range(4):
                pe.matmul(py[:, h, :], lhsT=pb[:, h, :], rhs=vt[:, k, h * 128:(h + 1) * 128], start=True, stop=False)
                pe.matmul(py[:, h, :], lhsT=qd[:, 0, h, k * 128:(k + 1) * 128], rhs=stb[:, h, k, :], start=False, stop=False)
                i = pe.matmul(py[:, h, :], lhsT=qd[:, 1, h, k * 128:(k + 1) * 128], rhs=stb[:, 4 + h, k, :], start=False, stop=True)
            return i
        S.op("pe", mo, reads=[pbB, vtB, qdB] + stbB, writes=[pyB])
        st_, sB = c.stat.next()
        y, yB = ytile.next()
        sgg, sgB = sgt.next()
        S.op("sp", lambda q, sgg=sgg, k=k: q.dma_start(out=sgg[:], in_=sg[k * 128:(k + 1) * 128, :]), writes=[sgB], dma=sgB)
        S.op("dve", lambda v, py=py, st_=st_: v.reduce_sum(out=st_[:, 0:4], in_=py[:], axis=AX.X), reads=[pyB], writes=[sB])
        S.op("dve", lambda v, st_=st_: v.tensor_scalar(out=st_[:, 4:8], in0=st_[:, 0:4], scalar1=1.0 / 128, scalar2=None, op0=ALU.mult),
             reads=[sB], writes=[sB])
        S.op("dve", lambda v, py=py, y=y, st_=st_: v.tensor_tensor(out=y[:], in0=py[:], in1=st_[:, 4:8].unsqueeze(2).to_broadcast([128, 4, 128]),
                                                                op=ALU.subtract), reads=[pyB, sB], writes=[yB])
        s2, s2B = sq.next()
        S.op("act", lambda a, s2=s2, y=y: a.activation(out=s2[:], in_=y[:], func=AF.Square), reads=[yB], writes=[s2B])
        S.op("dve", lambda v, s2=s2, st_=st_: v.reduce_sum(out=st_[:, 8:12], in_=s2[:], axis=AX.X), reads=[s2B], writes=[sB])
        S.op("dve", lambda v, st_=st_: v.tensor_scalar(out=st_[:, 12:16], in0=st_[:, 8:12], scalar1=1.0 / 128, scalar2=EPS,
                                                    op0=ALU.mult, op1=ALU.add), reads=[sB], writes=[sB])
        S.op("act", lambda a, st_=st_: a.activation(out=st_[:, 8:12], in_=st_[:, 12:16], func=AF.Sqrt), reads=[sB], writes=[sB])
        S.op("dve", lambda v, st_=st_: v.reciprocal(out=st_[:, 12:16], in_=st_[:, 8:12]), reads=[sB], writes=[sB])
        S.op("dve", lambda v, y=y, st_=st_: v.tensor_tensor(out=y[:], in0=y[:], in1=st_[:, 12:16].unsqueeze(2).to_broadcast([128, 4, 128]),
                                                         op=ALU.mult), reads=[sB], writes=[yB])
        S.op("pool", lambda g_, y=y, sgg=sgg: g_.tensor_tensor(out=y[:], in0=y[:], in1=sgg[:].rearrange("p (h e) -> p h e", e=128),
                                                             op=ALU.mult), reads=[sgB], writes=[yB])
        S.op("sp", lambda q, y=y, k=k: q.dma_start(out=yR[k * 128:(k + 1) * 128, :].rearrange("p (h e) -> p h e", e=128), in_=y[:]),
             reads=[yB], dma=yB)
    return ytile.b


def build_LB_ret():
    nc = bass.Bass("TRN2", target_bir_lowering=False)
    with ExitStack() as es:
        S = Sched(nc, es)
        c = setup_lb(nc, es, S)
        c.stat = Pool(nc, es, "stat", [128, 16], F32, 6)
        A = make_attn_res(c, nc, es, S, sbuf=False)
        fin = build_ret(c, A, nc, es, S)
        S.finish(fin, "sp")
        print("LBret sems", S.nsem, "waits", S.nwait, "cnt", S.cnt)
    return nc


def build_sfin_only():
    nc = bass.Bass("TRN2", target_bir_lowering=False)
    kR = dram_in(nc, "kR", [2048, 256], BF16); vR = dram_in(nc, "vR", [2048, 512], BF16)
    dl = dram_in(nc, "ret_dl", [128, 8], F32); esf = dram_in(nc, "ret_esf", [128, 16, 2], F32)
    sf = dram_out(nc, "sfin", [8, 64, 128], F32)
    with ExitStack() as es:
        S = Sched(nc, es)
        c = setup_lb(nc, es, S)
        c.psO = Pool(nc, es, "psO", [128, 512], F32, 2, psum=True)
        b = ret_sfin(c, nc, es, S, kR, vR, dl, esf, sf)
        S.finish([b], "sp")
    return nc


TWO_PI = 2.0 * math.pi
NE = 136


def build_s5(c, nc, es, S):
    uT_d = dram_in(nc, "s5_uT", [64, 16384], BF16)
    are_d = dram_in(nc, "s5_are", [128, 8]); aim_d = dram_in(nc, "s5_aim", [128, 8]); lst_d = dram_in(nc, "s5_lst", [128, 8])
    p1_d = dram_in(nc, "s5_p1", [128, 8, 16]); p2_d = dram_in(nc, "s5_p2", [128, 8, 16])
    cx_d = dram_in(nc, "s5_cx", [128, 4, 16]); cy_d = dram_in(nc, "s5_cy", [128, 4, 16])
    dsk_d = dram_in(nc, "s5_dsk", [128, 4])
    sgn_d = dram_in(nc, "s5_sgn", [128, 1])
    expo_d = dram_in(nc, "s5_expo", [128, 2, NE])
    idf_d = dram_in(nc, "identf", [128, 128]); jsw_d = dram_in(nc, "s5_jsw", [128, 128])
    msk_d = dram_in(nc, "s5_msk", [128, 2, 128])
    sel_d = dram_in(nc, "s5_sel", [64, 4, 8, 128], BF16)
    y8_d = dram_out(nc, "s5_y8", [4, 128, 2048], F32)

    def load(name, shape, src, dt=F32, eng="sp"):
        t = es.enter_context(nc.sbuf_tensor(name, shape, dt)); b = Buf(name)
        S.op(eng, lambda q: q.dma_start(out=t[:], in_=src), writes=[b], dma=b)
        return t, b

    def alloc(name, shape, dt=F32):
        return es.enter_context(nc.sbuf_tensor(name, shape, dt)), Buf(name)
    uT, uTB = load("s5uT", [64, 16384], uT_d, BF16)
    are, areB = load("s5are", [128, 8], are_d); aim, aimB = load("s5aim", [128, 8], aim_d); lst, lstB = load("s5lst", [128, 8], lst_d)
    p1, p1B = load("s5p1", [128, 8, 16], p1_d); p2, p2B = load("s5p2", [128, 8, 16], p2_d)
    cx, cxB = load("s5cx", [128, 4, 16], cx_d); cy, cyB = load("s5cy", [128, 4, 16], cy_d)
    dsk, dskB = load("s5dsk", [128, 4], dsk_d); sgn, sgnB = load("s5sgn", [128, 1], sgn_d)
    expo, expoB = load("s5expo", [128, 2, NE], expo_d)
    idf, idfB = load("s5idf", [128, 128], idf_d); jsw, jswB = load("s5jsw", [128, 128], jsw_d)
    msk, mskB = load("s5msk", [128, 2, 128], msk_d)
    sel, selB = load("s5sel", [64, 4, 8, 128], sel_d, BF16)

    V = lambda fn, r, w: S.op("dve", fn, reads=r, writes=w)
    Pq = lambda fn, r, w: S.op("pool", fn, reads=r, writes=w)
    ACT = lambda fn, r, w: S.op("act", fn, reads=r, writes=w)

    sc, scB = alloc("s5sc", [128, 12, 8])
    DT, RHO, TH, NR, NI, L2, CR, CI, T0, T1, NSG, T2 = range(12)
    ACT(lambda a: a.activation(out=sc[:, DT], in_=lst[:], func=AF.Exp), [lstB], [scB])
    V(lambda v: v.tensor_tensor(out=sc[:, RHO], in0=are[:], in1=sc[:, DT], op=ALU.mult), [areB, scB], [scB])
    V(lambda v: v.tensor_tensor(out=sc[:, TH], in0=aim[:], in1=sc[:, DT], op=ALU.mult), [aimB, scB], [scB])
    V(lambda v: v.tensor_scalar(out=sc[:, NSG, 0:1], in0=sgn[:], scalar1=-1.0, scalar2=None, op0=ALU.mult), [sgnB], [scB])
    nsg = sc[:, NSG, 0:1]
    pw, pwB = alloc("s5pw", [128, 2, 2, 8, NE])
    ph, phB = alloc("s5ph", [128, 8, NE])
    mg, mgB = alloc("s5mg", [128, 8, NE])
    phi, phiB = alloc("s5phi", [128, 8, NE], mybir.dt.int32)
    phf, phfB = alloc("s5phf", [128, 8, NE])
    for o in range(2):
        ex = expo[:, o, :].unsqueeze(1).to_broadcast([128, 8, NE])
        V(lambda v, ex=ex: v.tensor_tensor(out=mg[:], in0=ex, in1=sc[:, RHO].unsqueeze(2).to_broadcast([128, 8, NE]), op=ALU.mult),
          [expoB, scB], [mgB])
        ACT(lambda a: a.activation(out=mg[:], in_=mg[:], func=AF.Exp), [mgB], [mgB])
        for ri, shift in ((1, 0.0), (0, 0.25)):
            V(lambda v, ex=ex: v.tensor_tensor(out=ph[:], in0=ex, in1=sc[:, TH].unsqueeze(2).to_broadcast([128, 8, NE]), op=ALU.mult),
              [expoB, scB], [phB])
            V(lambda v, shift=shift: v.tensor_scalar(out=ph[:], in0=ph[:], scalar1=1.0 / TWO_PI, scalar2=shift, op0=ALU.mult,
                                                     op1=ALU.add), [phB], [phB])
            V(lambda v: v.tensor_copy(out=phi[:], in_=ph[:]), [phB], [phiB])
            V(lambda v: v.tensor_copy(out=phf[:], in_=phi[:]), [phiB], [phfB])
            V(lambda v: v.tensor_tensor(out=ph[:], in0=ph[:], in1=phf[:], op=ALU.subtract), [phfB], [phB])
            ACT(lambda a: a.activation(out=ph[:], in_=ph[:], func=AF.Sin, scale=6.283185), [phB], [phB])
            V(lambda v, o=o, ri=ri: v.tensor_tensor(out=pw[:, o, ri], in0=ph[:], in1=mg[:], op=ALU.mult), [phB, mgB], [pwB])
    i1 = 1 + 7
    V(lambda v: v.tensor_scalar(out=sc[:, NR], in0=pw[:, 0, 0, :, i1], scalar1=-1.0, scalar2=None, op0=ALU.add), [pwB], [scB])
    V(lambda v: v.tensor_copy(out=sc[:, NI], in_=pw[:, 0, 1, :, i1]), [pwB], [scB])
    V(lambda v: v.tensor_tensor(out=sc[:, T0], in0=are[:], in1=are[:], op=ALU.mult), [areB], [scB])
    V(lambda v: v.tensor_tensor(out=sc[:, T1], in0=aim[:], in1=aim[:], op=ALU.mult), [aimB], [scB])
    V(lambda v: v.tensor_tensor(out=sc[:, L2], in0=sc[:, T0], in1=sc[:, T1], op=ALU.add), [scB], [scB])
    V(lambda v: v.reciprocal(out=sc[:, L2], in_=sc[:, L2]), [scB], [scB])
    V(lambda v: v.tensor_tensor(out=sc[:, T0], in0=sc[:, NR], in1=are[:], op=ALU.mult), [scB, areB], [scB])
    V(lambda v: v.tensor_tensor(out=sc[:, T1], in0=sc[:, NI], in1=aim[:], op=ALU.mult), [scB, aimB], [scB])
    V(lambda v: v.tensor_tensor(out=sc[:, T2], in0=sc[:, T0], in1=sc[:, T1], op=ALU.add), [scB], [scB])
    V(lambda v: v.tensor_tensor(out=sc[:, CR], in0=sc[:, T2], in1=sc[:, L2], op=ALU.mult), [scB], [scB])
    V(lambda v: v.tensor_tensor(out=sc[:, T0], in0=sc[:, NI], in1=are[:], op=ALU.mult), [scB, areB], [scB])
    V(lambda v: v.tensor_tensor(out=sc[:, T1], in0=sc[:, NR], in1=aim[:], op=ALU.mult), [scB, aimB], [scB])
    V(lambda v: v.tensor_tensor(out=sc[:, T2], in0=sc[:, T0], in1=sc[:, T1], op=ALU.subtract), [scB], [scB])
    V(lambda v: v.tensor_tensor(out=sc[:, CI], in0=sc[:, T2], in1=sc[:, L2], op=ALU.mult), [scB], [scB])
    bxy, bxyB = alloc("s5bxy", [128, 2, 8, 16])
    cxy, cxyB = alloc("s5cxy", [128, 2, 4, 16])
    tb, tbB = alloc("s5tb", [128, 3, 8, 16])
    V(lambda v: v.tensor_scalar(out=tb[:, 2], in0=p2[:], scalar1=sgn[:, 0:1], scalar2=None, op0=ALU.mult), [p2B, sgnB], [tbB])
    crb = sc[:, CR].unsqueeze(2).to_broadcast([128, 8, 16]); cib = sc[:, CI].unsqueeze(2).to_broadcast([128, 8, 16])
    V(lambda v: v.tensor_tensor(out=tb[:, 0], in0=p1[:], in1=crb, op=ALU.mult), [p1B, scB], [tbB])
    V(lambda v: v.tensor_tensor(out=tb[:, 1], in0=tb[:, 2], in1=cib, op=ALU.mult), [tbB, scB], [tbB])
    V(lambda v: v.tensor_tensor(out=bxy[:, 0], in0=tb[:, 0], in1=tb[:, 1], op=ALU.add), [tbB], [bxyB])
    V(lambda v: v.tensor_tensor(out=tb[:, 0], in0=tb[:, 2], in1=crb, op=ALU.mult), [tbB, scB], [tbB])
    V(lambda v: v.tensor_tensor(out=tb[:, 1], in0=p1[:], in1=cib, op=ALU.mult), [p1B, scB], [tbB])
    V(lambda v: v.tensor_tensor(out=bxy[:, 1], in0=tb[:, 0], in1=tb[:, 1], op=ALU.subtract), [tbB], [bxyB])
    V(lambda v: v.tensor_scalar(out=cxy[:, 0], in0=cx[:], scalar1=nsg, scalar2=None, op0=ALU.mult), [cxB, scB], [cxyB])
    V(lambda v: v.tensor_scalar(out=cxy[:, 1], in0=cy[:], scalar1=-1.0, scalar2=None, op0=ALU.mult), [cyB], [cxyB])
    cab, cabB = alloc("s5cab", [128, 2, 7, 8])
    V(lambda v: v.tensor_copy(out=cab[:, 0, 0], in_=pw[:, 0, 0, :, 128 + 7]), [pwB], [cabB])
    V(lambda v: v.tensor_scalar(out=cab[:, 1, 0], in0=pw[:, 0, 1, :, 128 + 7], scalar1=nsg, scalar2=None, op0=ALU.mult), [pwB, scB], [cabB])
    for i in range(1, 7):
        V(lambda v, i=i: v.tensor_tensor(out=sc[:, T0], in0=cab[:, 0, i - 1], in1=cab[:, 0, i - 1], op=ALU.mult), [cabB], [scB])
        V(lambda v, i=i: v.tensor_tensor(out=sc[:, T1], in0=cab[:, 1, i - 1], in1=cab[:, 1, i - 1], op=ALU.mult), [cabB], [scB])
        V(lambda v, i=i: v.tensor_tensor(out=cab[:, 0, i], in0=sc[:, T0], in1=sc[:, T1], op=ALU.subtract), [scB], [cabB])
        V(lambda v, i=i: v.tensor_tensor(out=sc[:, T2], in0=cab[:, 0, i - 1], in1=cab[:, 1, i - 1], op=ALU.mult), [cabB], [scB])
        V(lambda v, i=i: v.tensor_scalar(out=cab[:, 1, i], in0=sc[:, T2], scalar1=2.0, scalar2=None, op0=ALU.mult), [scB], [cabB])

    tabR = Pool(nc, es, "s5tabR", [128, NE, 16], F32, 2)
    tabL = Pool(nc, es, "s5tabL", [128, NE, 16], F32, 2)
    tmpT = Pool(nc, es, "s5tmpT", [128, NE, 16], F32, 2)
    u8p = Pool(nc, es, "s5u8", [128, 16 + 4096 + 16], BF16, 2)
    lagp = Pool(nc, es, "s5lag", [128, 31, 128], BF16, 2)
    bmp = Pool(nc, es, "s5bm", [128, 2, 16, 128], BF16, 2)
    y8p = Pool(nc, es, "s5y8", [128, 2048], F32, 2)
    zp = Pool(nc, es, "s5z", [128, 128], F32, 6)
    xp = Pool(nc, es, "s5x", [128, 2, 128], F32, 2)
    mtp = Pool(nc, es, "s5mt", [128, 128], F32, 3)
    f0p = Pool(nc, es, "s5f0", [128, 2, 128], F32, 2)
    psB_ = Pool(nc, es, "s5psB", [128, 512], F32, 3, psum=True)
    psS_ = Pool(nc, es, "s5psS", [128, 128], F32, 4, psum=True)
    fin = []
    for gl in range(4):
        gdf, gdb = gl * 2, gl * 2 + 1
        def gen(pool, o, gd, xy, which):
            t, tB_ = pool.next()
            t1, t1B = tmpT.next()
            X = xy[:, 0, which].unsqueeze(1).to_broadcast([128, NE, 16]); Y = xy[:, 1, which].unsqueeze(1).to_broadcast([128, NE, 16])
            xyB = bxyB if xy is bxy else cxyB
            V(lambda v: v.tensor_tensor(out=t[:], in0=pw[:, o, 0, gd].unsqueeze(2).to_broadcast([128, NE, 16]), in1=X, op=ALU.mult),
              [pwB, xyB], [tB_])
            Pq(lambda g: g.tensor_tensor(out=t1[:], in0=pw[:, o, 1, gd].unsqueeze(2).to_broadcast([128, NE, 16]), in1=Y, op=ALU.mult),
               [pwB, xyB], [t1B])
            V(lambda v: v.tensor_tensor(out=t[:], in0=t[:], in1=t1[:], op=ALU.add), [t1B], [tB_])
            return t, tB_
        RfA, RfAB = gen(tabR, 0, gdf, cxy, gl)
        RbD, RbDB = gen(tabR, 1, gdb, cxy, gl)
        LfD, LfDB = gen(tabL, 1, gdf, bxy, gdf)
        LbA, LbAB = gen(tabL, 0, gdb, bxy, gdb)

        def blk(t, j0):
            return t[:, j0:j0 + 8, :].rearrange("p x c -> p (x c)")
        Lf = blk(LfD, 128)
        Lb = blk(LbA, 7)
        u8, u8B = u8p.next()
        V(lambda v: v.memset(u8[:], 0.0), [], [u8B])
        for ct in range(4):
            ps, psB = psB_.next()

            def mm(pe, ps=ps, ct=ct):
                for s in range(8):
                    b0 = ct * 4096 + s
                    i = pe.matmul(ps[:], lhsT=sel[:, gl, s, :], rhs=uT[:, b0:b0 + 8 * 511 + 1:8], start=(s == 0), stop=(s == 7))
                return i
            S.op("pe", mm, reads=[selB, uTB], writes=[psB])
            ACT(lambda a, ps=ps, ct=ct: a.activation(
                out=u8[:, 16 + ct * 1024:16 + (ct + 1) * 1024].rearrange("p (k m) -> p k m", m=32)[:, :, 0:16],
                in_=ps[:].rearrange("p (k m) -> p k m", m=16), func=AF.Copy), [psB], [u8B])
        lag, lagB = lagp.next()
        f0, f0B = f0p.next()
        for dl in range(16):
            ps, psB = psS_.next()
            S.op("pe", lambda pe, ps=ps, dl=dl: pe.matmul(ps[:], lhsT=Lf, rhs=blk(RfA, 8 * dl + 7), start=True, stop=True),
                 reads=[LfDB, RfAB], writes=[psB])
            if dl == 0:
                V(lambda v, ps=ps: v.tensor_tensor(out=f0[:, 0], in0=ps[:], in1=msk[:, 0], op=ALU.mult), [psB, mskB], [f0B])
            else:
                ACT(lambda a, ps=ps, dl=dl: a.activation(out=lag[:, dl - 1, :], in_=ps[:], func=AF.Copy), [psB], [lagB])
            ps, psB = psS_.next()
            S.op("pe", lambda pe, ps=ps, dl=dl: pe.matmul(ps[:], lhsT=Lb, rhs=blk(RbD, 128 - 8 * dl), start=True, stop=True),
                 reads=[LbAB, RbDB], writes=[psB])
            if dl == 0:
                V(lambda v, ps=ps: v.tensor_tensor(out=f0[:, 1], in0=ps[:], in1=msk[:, 1], op=ALU.mult), [psB, mskB], [f0B])
                V(lambda v: v.tensor_tensor(out=f0[:, 0], in0=f0[:, 0], in1=f0[:, 1], op=ALU.add), [], [f0B])
                V(lambda v: v.scalar_tensor_tensor(out=lag[:, 30, :], in0=idf[:], scalar=dsk[:, gl:gl + 1], in1=f0[:, 0],
                                                   op0=ALU.mult, op1=ALU.add), [idfB, dskB, f0B], [lagB])
            else:
                ACT(lambda a, ps=ps, dl=dl: a.activation(out=lag[:, 14 + dl, :], in_=ps[:], func=AF.Copy), [psB], [lagB])
        bm, bmB = bmp.next()
        for m in range(16):
            for d_, (tbl, tblB, j0) in enumerate(((LfD, LfDB, 1 + 8 * m), (LbA, LbAB, 7 + 8 * m))):
                ps, psB = psS_.next()
                S.op("pe", lambda pe, ps=ps, tbl=tbl, j0=j0: pe.matmul(ps[:], lhsT=blk(tbl, j0), rhs=idf[:], start=True, stop=True),
                     reads=[tblB, idfB], writes=[psB])
                ACT(lambda a, ps=ps, d_=d_, m=m: a.activation(out=bm[:, d_, m, :], in_=ps[:], func=AF.Copy), [psB], [bmB])
        y8, y8B = y8p.next()
        for xt in range(8):
            ps, psB = psB_.next()
            x0 = 16 + xt * 512

            def mm(pe, ps=ps, x0=x0):
                pe.matmul(ps[:], lhsT=lag[:, 30, :], rhs=u8[:, x0:x0 + 512], start=True, stop=False)
                for dl in range(1, 16):
                    pe.matmul(ps[:], lhsT=lag[:, dl - 1, :], rhs=u8[:, x0 - dl:x0 - dl + 512], start=False, stop=False)
                for dl in range(1, 16):
                    i = pe.matmul(ps[:], lhsT=lag[:, 14 + dl, :], rhs=u8[:, x0 + dl:x0 + dl + 512], start=False, stop=(dl == 15))
                return i
            S.op("pe", mm, reads=[lagB, u8B], writes=[psB])
            ACT(lambda a, ps=ps, xt=xt: a.activation(
                out=y8[:, xt * 256:(xt + 1) * 256].rearrange("p (k m) -> p k m", m=16),
                in_=ps[:].rearrange("p (k m) -> p k m", m=32)[:, :, 0:16], func=AF.Copy), [psB], [y8B])
        xs, xsB = xp.next()
        for d_ in range(2):
            gd = gl * 2 + d_
            ps, psB = psS_.next()

            def mm(pe, ps=ps, d_=d_):
                for m in range(16):
                    i = pe.matmul(ps[:], lhsT=bm[:, d_, m, :], rhs=u8[:, 16 + m:16 + m + 32 * 127 + 1:32], start=(m == 0), stop=(m == 15))
                return i
            S.op("pe", mm, reads=[bmB, u8B], writes=[psB])
            z, zB = zp.next()
            ACT(lambda a, ps=ps, z=z: a.activation(out=z[:], in_=ps[:], func=AF.Copy), [psB], [zB])
            for i in range(7):
                sh = 1 << i
                mt, mtB = mtp.next()
                V(lambda v, mt=mt, i=i, gd=gd: v.tensor_scalar(out=mt[:], in0=idf[:], scalar1=cab[:, 0, i, gd:gd + 1], scalar2=None,
                                                              op0=ALU.mult), [idfB, cabB], [mtB])
                V(lambda v, mt=mt, i=i, gd=gd: v.scalar_tensor_tensor(out=mt[:], in0=jsw[:], scalar=cab[:, 1, i, gd:gd + 1], in1=mt[:],
                                                                     op0=ALU.mult, op1=ALU.add), [jswB, cabB], [mtB])
                ps2, ps2B = psS_.next()
                zn, znB = zp.next()
                if d_ == 0:
                    S.op("pe", lambda pe, ps2=ps2, mt=mt, z=z, sh=sh: pe.matmul(ps2[:, sh:128], lhsT=mt[:], rhs=z[:, 0:128 - sh],
                                                                             start=True, stop=True), reads=[mtB, zB], writes=[ps2B])
                    V(lambda v, zn=zn, z=z, ps2=ps2, sh=sh: v.tensor_tensor(out=zn[:, sh:128], in0=z[:, sh:128], in1=ps2[:, sh:128],
                                                                         op=ALU.add), [zB, ps2B], [znB])
                    Pq(lambda g, zn=zn, z=z, sh=sh: g.tensor_copy(out=zn[:, 0:sh], in_=z[:, 0:sh]), [zB], [znB])
                else:
                    S.op("pe", lambda pe, ps2=ps2, mt=mt, z=z, sh=sh: pe.matmul(ps2[:, 0:128 - sh], lhsT=mt[:], rhs=z[:, sh:128],
                                                                             start=True, stop=True), reads=[mtB, zB], writes=[ps2B])
                    V(lambda v, zn=zn, z=z, ps2=ps2, sh=sh: v.tensor_tensor(out=zn[:, 0:128 - sh], in0=z[:, 0:128 - sh],
                                                                         in1=ps2[:, 0:128 - sh], op=ALU.add), [zB, ps2B], [znB])
                    Pq(lambda g, zn=zn, z=z, sh=sh: g.tensor_copy(out=zn[:, 128 - sh:128], in_=z[:, 128 - sh:128]), [zB], [znB])
                z, zB = zn, znB
            if d_ == 0:
                V(lambda v, z=z: v.tensor_copy(out=xs[:, 0, 1:128], in_=z[:, 0:127]), [zB], [xsB])
                V(lambda v: v.memset(xs[:, 0, 0:1], 0.0), [], [xsB])
            else:
                V(lambda v, z=z: v.tensor_copy(out=xs[:, 1, 0:127], in_=z[:, 1:128]), [zB], [xsB])
                V(lambda v: v.memset(xs[:, 1, 127:128], 0.0), [], [xsB])
        for m in range(16):
            ps, psB = psS_.next()

            def mm(pe, ps=ps, m=m):
                pe.matmul(ps[:], lhsT=blk(RfA, 8 * m + 1 + 7), rhs=xs[:, 0, :], start=True, stop=False)
                return pe.matmul(ps[:], lhsT=blk(RbD, 8 * m), rhs=xs[:, 1, :], start=False, stop=True)
            S.op("pe", mm, reads=[RfAB, RbDB, xsB], writes=[psB])
            V(lambda v, ps=ps, m=m: v.tensor_tensor(out=y8[:, m:m + 16 * 127 + 1:16], in0=y8[:, m:m + 16 * 127 + 1:16], in1=ps[:],
                                                  op=ALU.add), [psB], [y8B])
        S.op("sp", lambda q, y8=y8, gl=gl: q.dma_start(out=y8_d[gl], in_=y8[:]), reads=[y8B], dma=y8B)
        fin.append(y8B)
    return fin


def build_LB_s5():
    nc = bass.Bass("TRN2", target_bir_lowering=False)
    with ExitStack() as es:
        S = Sched(nc, es)
        c = setup_lb(nc, es, S)
        fin = build_s5(c, nc, es, S)
        S.finish(fin, "sp")
        print("LBs5 sems", S.nsem, "waits", S.nwait, "cnt", S.cnt)
    return nc


def build_LC():
    nc = bass.Bass("TRN2", target_bir_lowering=False)
    x1 = dram_in(nc, "x", [NTOK, D])
    yA = dram_in(nc, "yA", [NTOK, 512]); yB = dram_in(nc, "yB", [NTOK, 512]); yR = dram_in(nc, "yR", [NTOK, 512])
    yS = dram_in(nc, "yS", [NTOK, 512])
    wglu_d = dram_in(nc, "wglu", [512, 512]); bglu_d = dram_in(nc, "bglu", [128, 512])
    og_d = dram_in(nc, "ogain", [128, 2048])
    wout = dram_in(nc, "wout", [D, D])
    g0 = dram_in(nc, "g0", [128, 16])
    wg = dram_in(nc, "wg", [D, DFF]); wu = dram_in(nc, "wu", [D, DFF]); wd = dram_in(nc, "wd", [DFF, D])
    xo = dram_out(nc, "x1", [NTOK, D])
    with ExitStack() as es:
        S = Sched(nc, es)
        c = setup_common(nc, es, S)
        W = make_wpools(nc, es)
        gcol = es.enter_context(nc.sbuf_tensor("gcol", [128, 16], F32)); gB = Buf("gcol")
        S.op("sp", lambda q: q.dma_start(out=gcol[:], in_=g0), writes=[gB], dma=gB)
        og = es.enter_context(nc.sbuf_tensor("og", [128, 2048], F32)); ogB = Buf("og")
        S.op("sp", lambda q: q.dma_start(out=og[:], in_=og_d), writes=[ogB], dma=ogB)
        bglu = es.enter_context(nc.sbuf_tensor("bglu_sb", [128, 512], F32)); bgB = Buf("bglu")
        S.op("sp", lambda q: q.dma_start(out=bglu[:], in_=bglu_d), writes=[bgB], dma=bgB)
        wglu = es.enter_context(nc.sbuf_tensor("wglu_sb", [128, 4, 512], BF16)); wgluB = Buf("wglu")
        S.op("pool", lambda q: q.dma_start(out=wglu[:], in_=wglu_d.rearrange("(kc p) n -> p kc n", p=128)), writes=[wgluB], dma=wgluB)
        hT = es.enter_context(nc.sbuf_tensor("hT", [128, 16, 512], BF16)); hTB = Buf("hT")
        xt = []; xtB = []
        for tt in range(4):
            xt.append(es.enter_context(nc.sbuf_tensor("xt%d" % tt, [128, 2048], F32))); xtB.append(Buf("xt%d" % tt))
        y4p = Pool(nc, es, "y4", [128, 4, 512], F32, 1)
        t5 = Pool(nc, es, "lct", [128, 512], F32, 3)
        zbp = Pool(nc, es, "lczb", [128, 512], BF16, 2)
        zTp = Pool(nc, es, "lczT", [128, 4, 128], BF16, 2)
        for st in range(NTOK // 512):
            for tt in range(4):
                r0 = st * 512 + tt * 128
                S.op("sp", lambda q, tt=tt, r0=r0: q.dma_start(out=xt[tt][:], in_=x1[r0:r0 + 128, :]), writes=[xtB[tt]], dma=xtB[tt])
                y4, y4B = y4p.next()
                for i, src in enumerate((yA, yB, yR, yS)):
                    S.op("sp", lambda q, y4=y4, i=i, src=src, r0=r0: q.dma_start(out=y4[:, i, :], in_=src[r0:r0 + 128, :]), writes=[y4B], dma=y4B)
                ys = y4[:, 3, :]
                a, aB = t5.next(); b, bB = t5.next()
                S.op("act", lambda A_, a=a, ys=ys: A_.activation(out=a[:], in_=ys, func=AF.Square), reads=[y4B], writes=[aB])
                S.op("dve", lambda v, a=a: v.tensor_scalar(out=a[:], in0=a[:], scalar1=0.044715, scalar2=1.0, op0=ALU.mult, op1=ALU.add),
                     reads=[], writes=[aB])
                S.op("dve", lambda v, a=a, ys=ys: v.tensor_tensor(out=a[:], in0=a[:], in1=ys, op=ALU.mult), reads=[y4B], writes=[aB])
                S.op("act", lambda A_, a=a: A_.activation(out=a[:], in_=a[:], func=AF.Sigmoid, scale=1.5957691216057308), reads=[], writes=[aB])
                S.op("dve", lambda v, a=a, ys=ys: v.tensor_tensor(out=a[:], in0=a[:], in1=ys, op=ALU.mult), reads=[y4B], writes=[aB])
                zb, zbB = zbp.next()
                S.op("act", lambda A_, a=a, zb=zb: A_.activation(out=zb[:], in_=a[:], func=AF.Copy), reads=[aB], writes=[zbB])
                pt, pB = c.psT.next()

                def tr(pe, pt=pt, zb=zb):
                    for j in range(4):
                        i = pe.transpose(out=pt[:, j, :], in_=zb[:, j * 128:(j + 1) * 128], identity=c.ident[:])
                    return i
                S.op("pe", tr, reads=[zbB, c.identB], writes=[pB])
                zT, zTB = zTp.next()
                S.op("act", lambda A_, pt=pt, zT=zT: A_.activation(out=zT[:], in_=pt[:, 0:4, :], func=AF.Copy), reads=[pB], writes=[zTB])
                po, poB = c.psO.next()

                def mmg(pe, po=po, zT=zT):
                    for kc in range(4):
                        i = pe.matmul(po[:], lhsT=zT[:, kc, :], rhs=wglu[:, kc, :], start=(kc == 0), stop=(kc == 3))
                    return i
                S.op("pe", mmg, reads=[zTB, wgluB], writes=[poB])
                S.op("dve", lambda v, b=b, po=po: v.tensor_tensor(out=b[:], in0=po[:], in1=bglu[:], op=ALU.add), reads=[poB, bgB], writes=[bB])
                S.op("act", lambda A_, b=b: A_.activation(out=b[:], in_=b[:], func=AF.Sigmoid), reads=[], writes=[bB])
                S.op("dve", lambda v, a=a, b=b, y4=y4: v.tensor_tensor(out=y4[:, 3, :], in0=a[:], in1=b[:], op=ALU.mult), reads=[aB, bB], writes=[y4B])
                sq, sqB = t5.next()
                st_, sB = c.stat.next()
                for i in range(4):
                    S.op("act", lambda A_, sq=sq, y4=y4, i=i, st_=st_: A_.activation(out=sq[:], in_=y4[:, i, :], func=AF.Square,
                                                                                     accum_out=st_[:, i:i + 1]), reads=[y4B], writes=[sqB, sB])
                S.op("dve", lambda v, st_=st_: v.tensor_scalar(out=st_[:, 4:8], in0=st_[:, 0:4], scalar1=1.0 / 512, scalar2=EPS,
                                                            op0=ALU.mult, op1=ALU.add), reads=[], writes=[sB])
                S.op("act", lambda A_, st_=st_: A_.activation(out=st_[:, 8:12], in_=st_[:, 4:8], func=AF.Sqrt), reads=[], writes=[sB])
                S.op("dve", lambda v, st_=st_: v.reciprocal(out=st_[:, 12:16], in_=st_[:, 8:12]), reads=[], writes=[sB])
                S.op("dve", lambda v, y4=y4, st_=st_: v.tensor_tensor(out=y4[:], in0=y4[:], in1=st_[:, 12:16].unsqueeze(2).to_broadcast([128, 4, 512]),
                                                                   op=ALU.mult), reads=[sB], writes=[y4B])
                xn, xnB = c.xn.next()
                S.op("pool", lambda g_, y4=y4, xn=xn: g_.tensor_tensor(out=xn[:], in0=y4[:].rearrange("p g e -> p (g e)"), in1=og[:], op=ALU.mult),
                     reads=[y4B, ogB], writes=[xnB])
                for half in range(2):
                    pt, pB = c.psT.next()

                    def tr2(pe, pt=pt, half=half, xn=xn):
                        for j in range(8):
                            kc = half * 8 + j
                            i = pe.transpose(out=pt[:, j, :], in_=xn[:, kc * 128:(kc + 1) * 128], identity=c.ident[:])
                        return i
                    S.op("pe", tr2, reads=[xnB, c.identB], writes=[pB])
                    S.op("act", lambda A_, pt=pt, half=half, tt=tt: A_.activation(
                        out=hT[:, half * 8:(half + 1) * 8, tt * 128:(tt + 1) * 128], in_=pt[:], func=AF.Copy), reads=[pB], writes=[hTB])
            for dg in range(4):
                ws, wB = W.wgu.next()
                S.op("pool", lambda q, ws=ws, dg=dg: q.dma_start(
                    out=ws[:], in_=wout[:, dg * 512:(dg + 1) * 512].rearrange("(kc p) n -> p kc n", p=128)), writes=[wB], dma=wB)
                for tt in range(4):
                    po, poB = c.psO.next()

                    def mm(pe, po=po, ws=ws, tt=tt):
                        for kc in range(16):
                            i = pe.matmul(po[:], lhsT=hT[:, kc, tt * 128:(tt + 1) * 128], rhs=ws[:, kc, :], start=(kc == 0), stop=(kc == 15))
                        return i
                    S.op("pe", mm, reads=[wB, hTB], writes=[poB])
                    S.op("dve", lambda v, po=po, tt=tt, dg=dg: v.tensor_tensor(
                        out=xt[tt][:, dg * 512:(dg + 1) * 512], in0=xt[tt][:, dg * 512:(dg + 1) * 512], in1=po[:], op=ALU.add),
                        reads=[poB], writes=[xtB[tt]])
            for tt in range(4):
                rms_to_hT(c, xt[tt][:], xtB[tt], gcol[:], gB, hT, hTB, tt * 128)
            ffn_supertile(c, xt, xtB, hT, hTB, wg, wu, wd, W)
            for tt in range(4):
                r0 = st * 512 + tt * 128
                S.op("sp", lambda q, tt=tt, r0=r0: q.dma_start(out=xo[r0:r0 + 128, :], in_=xt[tt][:]), reads=[xtB[tt]], dma=xtB[tt])
        S.finish(list(xtB), "sp")
        print("LC sems", S.nsem, "waits", S.nwait, "cnt", S.cnt)
    return nc


BF = ml_dtypes.bfloat16

def gcol(g):
    return np.ascontiguousarray(g.reshape(16, 128).T).astype(np.float32)

def rot_table(core):
    pos = (np.arange(2048) + core * 2048).astype(np.float32)
    half = 32
    inv = (np.float32(10000.0) ** (-(np.arange(half, dtype=np.float32) / np.float32(half)))).astype(np.float32)
    ang = (pos[:, None] * inv[None, :]).astype(np.float32).astype(np.float64)
    cs = np.stack([np.cos(ang), np.sin(ang)], axis=1)
    t = np.repeat(cs[:, :, None, :], 8, axis=2)
    t[:, :, 4:8, :] *= 0.125
    return t.astype(np.float32)

def la_inputs(inp, l, core, which=0, inproj=True):
    x = inp["x"][0, core * 2048:(core + 1) * 2048]
    d = {"x": x, "g0": gcol(inp["norm_gain"][l, 0 if which == 0 else 2]),
         "wg": inp["ffn_w_gate"][l, which], "wu": inp["ffn_w_up"][l, which], "wd": inp["ffn_w_down"][l, which],
         "ident": np.eye(128, dtype=BF)}
    if inproj:
        d["g1"] = gcol(inp["norm_gain"][l, 1])
        d["win"] = inp["w_in"][l]
        d["gqk"] = np.ascontiguousarray(np.broadcast_to(inp["qk_gain"][l].reshape(1, 4, 64), (128, 4, 64))).astype(np.float32)
        d["rot"] = rot_table(core)
    return d

NEG = -30000.0

def t5_bucket_np(rel):
    n = np.abs(rel); nf = np.maximum(n, 1).astype(np.float32)
    large = 8 + (np.log(nf / np.float32(8)) / np.float32(math.log(256)) * np.float32(8)).astype(np.int32)
    large = np.minimum(large, 15)
    return np.where(rel > 0, 16, 0) + np.where(n < 8, n, large)

def biasB_tables(t5):
    out = np.empty((3, 8, 128, 2, 128), np.float32)
    kk = np.arange(128)[:, None, None]; m = np.arange(2)[None, :, None]; i = np.arange(128)[None, None, :]
    rel = (128 * m + kk - 64) - i
    ok = np.abs(rel) <= 64
    for bi, d in enumerate((1, 4, 16)):
        b = t5_bucket_np(rel * d)
        tab = t5[b]
        out[bi] = np.where(ok[None], np.moveaxis(tab, -1, 0), NEG)
    return out

def biasA_tables(rpb, core):
    out = np.empty((8, 5, 128, 8, 128), np.float32)
    p = np.arange(128); kkr = p // 64; kc = p % 64
    q = np.arange(128); qr = q // 64; qc = q % 64
    cs = np.clip(qc - 8, 0, 48)
    colok = (kc[:, None] >= cs[None, :]) & (kc[:, None] < cs[None, :] + 16)
    dc = np.clip(kc[:, None] - qc[None, :], -15, 15) + 15
    for ti, gb in enumerate((8, 16 * core, 16 * core + 1, 16 * core + 14, 16 * core + 15)):
        r = 2 * gb + qr
        R0 = np.clip(r - 4, 0, 248)
        for j in range(8):
            kr = 2 * gb - 7 + 2 * j + kkr
            rowok = (kr[:, None] >= R0[None, :]) & (kr[:, None] < R0[None, :] + 8) & (kr[:, None] >= 0) & (kr[:, None] <= 255)
            dr = np.clip(kr[:, None] - r[None, :] + 7, 0, 14)
            ok = rowok & colok
            out[:, ti, :, j, :] = np.where(ok[None], rpb[:, dr, dc], NEG)
    return out

def halo(arr_full, lo, hi, axis):
    N = arr_full.shape[axis]
    shp = list(arr_full.shape); shp[axis] = hi - lo
    out = np.zeros(shp, arr_full.dtype)
    a = max(lo, 0); b = min(hi, N)
    src = [slice(None)] * arr_full.ndim; dst = [slice(None)] * arr_full.ndim
    src[axis] = slice(a, b); dst[axis] = slice(a - lo, b - lo)
    out[tuple(dst)] = arr_full[tuple(src)]
    return out

def attn_inputs(G, inp, l, core):
    t0 = core * 2048
    d = {"identf": np.eye(128, dtype=np.float32)}
    d["qTA"] = np.ascontiguousarray(G["qTA"][:, t0:t0 + 2048])
    lo = t0 - 7 * 64; hi = lo + 3072
    d["kTAh"] = halo(G["kTA"], lo, hi, 1); d["vAh"] = halo(G["vA"], lo, hi, 0)
    ones = np.ones((16384, 512), BF)
    d["valA"] = halo(ones, lo, hi, 0)
    d["biasA"] = biasA_tables(inp["na_rpb"][l], core)
    d["qTB"] = np.ascontiguousarray(G["qTB"][:, t0:t0 + 2048])
    lo = t0 - 1024; hi = lo + 4096
    d["kTBh"] = halo(G["kTB"], lo, hi, 1); d["vBh"] = halo(G["vB"], lo, hi + 16, 0)
    d["valB"] = halo(ones, lo, hi + 16, 0)
    d["biasB"] = biasB_tables(inp["t5_bias"])
    return d

BIGE = 1.0e7

def ret_consts(inp, l, core):
    d = {}
    d["ret_dl"] = np.ascontiguousarray(np.broadcast_to(inp["ret_decay_logit"][l].reshape(1, 8), (128, 8))).astype(np.float32)
    p = np.arange(128)[:, None]; t = np.arange(16)[None, :]
    tl = t * 128 + p
    d["ret_esf"] = np.stack([2047 - tl, tl], axis=2).astype(np.float32)
    nc_ = np.full((128, 8, 2), BIGE, np.float32)
    for c2 in range(8):
        if c2 < core: nc_[:, c2, 0] = 2048.0 * (core - 1 - c2)
        if c2 > core: nc_[:, c2, 1] = 2048.0 * (c2 - core - 1)
    d["ret_ncoef"] = nc_
    j = np.arange(128)
    d["ret_ekd"] = np.stack([127 - j, j], axis=1).astype(np.float32)
    d["ret_eqd"] = np.ascontiguousarray(np.broadcast_to(np.stack([j + 1, 128 - j], axis=0)[None], (64, 2, 128))).astype(np.float32)
    s = np.arange(128)[:, None]; tt = np.arange(128)[None, :]
    ef = np.where(s <= tt, tt - s, BIGE); eb = np.where(s > tt, s - tt, BIGE)
    d["ret_emask"] = np.stack([ef, eb], axis=1).astype(np.float32)
    return d

def s5_consts():
    d = {}
    d["s5_sgn"] = np.concatenate([-np.ones(64), np.ones(64)]).astype(np.float32).reshape(128, 1)
    asc = np.arange(-7, 129).astype(np.float32); desc = asc[::-1].copy()
    d["s5_expo"] = np.ascontiguousarray(np.broadcast_to(np.stack([asc, desc], 0)[None], (128, 2, 136))).astype(np.float32)
    d["identf"] = np.eye(128, dtype=np.float32)
    j = np.zeros((128, 128), np.float32)
    for k in range(64):
        j[k, k + 64] = 1.0; j[k + 64, k] = 1.0
    d["s5_jsw"] = j
    s = (np.arange(128) // 16)[:, None]; t = (np.arange(128) // 16)[None, :]
    d["s5_msk"] = np.stack([(s <= t), (s >= t)], axis=1).astype(np.float32)
    sel = np.zeros((64, 4, 8, 128), np.float32)
    for g in range(4):
        for s_ in range(8):
            for ci in range(16):
                sel[g * 16 + ci, g, s_, s_ * 16 + ci] = 1.0
    d["s5_sel"] = sel.astype(BF)
    return d

def s5_inputs(inp, l, core, uT_full):
    d = s5_consts()
    gs = slice(4 * core, 4 * core + 4)
    d["s5_uT"] = np.ascontiguousarray(uT_full[64 * core:64 * core + 64, :])
    def dup(a):
        x = np.transpose(a, (2, 1, 0)).reshape(64, 8)
        return np.ascontiguousarray(np.concatenate([x, x], 0)).astype(np.float32)
    d["s5_are"] = dup(inp["s5_a_re"][l][:, gs]); d["s5_aim"] = dup(inp["s5_a_im"][l][:, gs])
    ls = inp["s5_log_step"][l][:, gs]
    d["s5_lst"] = np.ascontiguousarray(np.broadcast_to(np.transpose(ls, (1, 0)).reshape(1, 8), (128, 8))).astype(np.float32)
    br = np.transpose(inp["s5_b_re"][l][:, gs], (2, 1, 0, 3)).reshape(64, 8, 16)
    bi = np.transpose(inp["s5_b_im"][l][:, gs], (2, 1, 0, 3)).reshape(64, 8, 16)
    d["s5_p1"] = np.ascontiguousarray(np.concatenate([br, bi], 0)).astype(np.float32)
    d["s5_p2"] = np.ascontiguousarray(np.concatenate([bi, br], 0)).astype(np.float32)
    cr = np.transpose(inp["s5_c_re"][l][gs], (2, 0, 1))
    ci = np.transpose(inp["s5_c_im"][l][gs], (2, 0, 1))
    d["s5_cx"] = np.ascontiguousarray(np.concatenate([cr, ci], 0)).astype(np.float32)
    d["s5_cy"] = np.ascontiguousarray(np.concatenate([ci, cr], 0)).astype(np.float32)
    dsk = inp["s5_d"][l].reshape(32, 16)[gs]
    d["s5_dsk"] = np.ascontiguousarray(np.tile(np.transpose(dsk, (1, 0)), (8, 1))).astype(np.float32)
    return d


_PROGS = {}


def _prog(name, fn):
    if name not in _PROGS:
        _PROGS[name] = fn()
    return _PROGS[name]


def build_LA_full():
    return build_LA(True)


def _run(nc, maps):
    res = run_bass_kernel_spmd(nc, maps, core_ids=list(range(8)))
    return res.results


def _f32(a):
    return np.ascontiguousarray(np.asarray(a), dtype=np.float32)


def _layer(x_cores, inp, l):
    ident = np.eye(128, dtype=BF)
    nc = _prog("LA", build_LA_full)
    maps = []
    for c in range(8):
        m = {"x": x_cores[c], "g0": gcol(inp["norm_gain"][l, 0]), "g1": gcol(inp["norm_gain"][l, 1]),
             "wg": inp["ffn_w_gate"][l, 0], "wu": inp["ffn_w_up"][l, 0], "wd": inp["ffn_w_down"][l, 0],
             "win": inp["w_in"][l], "ident": ident,
             "gqk": np.ascontiguousarray(np.broadcast_to(inp["qk_gain"][l].reshape(1, 4, 64), (128, 4, 64))).astype(np.float32),
             "rot": rot_table(c)}
        rc = ret_consts(inp, l, c)
        m["ret_dl"] = rc["ret_dl"]; m["ret_esf"] = rc["ret_esf"]
        maps.append(m)
    ra = _run(nc, maps)
    G = {}
    for k in ("qTA", "kTA", "qTB", "kTB", "uT"):
        G[k] = np.concatenate([np.asarray(ra[c][k]) for c in range(8)], axis=1)
    for k in ("vA", "vB"):
        G[k] = np.concatenate([np.asarray(ra[c][k]) for c in range(8)], axis=0)
    sfin_all = np.stack([np.asarray(ra[c]["sfin"]) for c in range(8)], 0)
    nca = _prog("LBattn", lambda: build_LB_attn(True, True))
    maps = [attn_inputs(G, inp, l, c) for c in range(8)]
    rb = _run(nca, maps)
    ncr = _prog("LBret", build_LB_ret)
    maps = []
    for c in range(8):
        rc = ret_consts(inp, l, c)
        m = {"qkTR": np.asarray(ra[c]["qkTR"]), "kR": np.asarray(ra[c]["kR"]), "vR": np.asarray(ra[c]["vR"]),
             "sgate": np.asarray(ra[c]["sgate"]), "sfin_all": sfin_all}
        for k in ("ret_dl", "ret_ncoef", "ret_ekd", "ret_eqd", "ret_emask"):
            m[k] = rc[k]
        maps.append(m)
    rr = _run(ncr, maps)
    ncs = _prog("LBs5", build_LB_s5)
    maps = [s5_inputs(inp, l, c, G["uT"]) for c in range(8)]
    rs = _run(ncs, maps)
    y8 = np.stack([np.asarray(rs[c]["s5_y8"]) for c in range(8)], 0)
    yS = y8.reshape(8, 4, 8, 16, 2048).transpose(4, 2, 0, 1, 3).reshape(16384, 512)
    ncc = _prog("LC", build_LC)
    maps = []
    for c in range(8):
        sl = slice(c * 2048, (c + 1) * 2048)
        maps.append({"x": np.asarray(ra[c]["x1"]), "yA": np.asarray(rb[c]["yA"]), "yB": np.asarray(rb[c]["yB"]),
                     "yR": np.asarray(rr[c]["yR"]), "yS": np.ascontiguousarray(yS[sl]),
                     "wglu": inp["s5_w_glu"][l],
                     "bglu": np.ascontiguousarray(np.broadcast_to(inp["s5_b_glu"][l][None], (128, 512))).astype(np.float32),
                     "ogain": np.ascontiguousarray(np.broadcast_to(inp["out_gain"][l][None], (128, 2048))).astype(np.float32),
                     "wout": inp["w_out"][l], "g0": gcol(inp["norm_gain"][l, 2]),
                     "wg": inp["ffn_w_gate"][l, 1], "wu": inp["ffn_w_up"][l, 1], "wd": inp["ffn_w_down"][l, 1], "ident": ident})
    rc_ = _run(ncc, maps)
    return [np.asarray(rc_[c]["x1"]) for c in range(8)]


def kernel(**inputs):
    inp = {k: np.asarray(v) for k, v in inputs.items()}
    x = _f32(inp["x"])[0]
    xc = [np.ascontiguousarray(x[c * 2048:(c + 1) * 2048]) for c in range(8)]
    for l in range(4):
        xc = _layer(xc, inp, l)
    return np.concatenate(xc, axis=0)[None].astype(np.float32)
```

```python
import math
from contextlib import ExitStack
import numpy as np
import ml_dtypes
import concourse.bass as bass
import concourse.mybir as mybir
from concourse.bass_utils import run_bass_kernel_spmd


F32 = mybir.dt.float32
BF16 = mybir.dt.bfloat16
ALU = mybir.AluOpType
AF = mybir.ActivationFunctionType
AX = mybir.AxisListType


NAME_PFX = [""]
DRAM_PFX = [""]
DRAM_CACHE = {}
DRAM_OVERRIDE = {}


def SB(nc, name, shape, dtype):
    return nc.sbuf_tensor(NAME_PFX[0] + name, shape, dtype)


def PS(nc, name, shape, dtype):
    return nc.psum_tensor(NAME_PFX[0] + name, shape, dtype)


class Buf:
    __slots__ = ("name", "last_w", "readers", "dsem", "dcnt", "csem", "ccnt")

    def __init__(self, name):
        self.name = name
        self.last_w = None
        self.readers = []
        self.dsem = None
        self.dcnt = 0
        self.csem = None
        self.ccnt = 0


class Sched:
    EPOCH = 12000

    def __init__(self, nc, es):
        self.nc = nc
        self.es = es
        self.eng = {"pe": nc.tensor, "act": nc.scalar, "dve": nc.vector,
                    "pool": nc.gpsimd, "sp": nc.sync}
        self.cnt = {e: 0 for e in self.eng}
        self.sems = {e: [] for e in self.eng}
        self.seen = {e: {} for e in self.eng}
        self.nsem = 0
        self.nwait = 0
        self.dma_all = {}

    def newsem(self, name):
        self.nsem += 1
        return self.es.enter_context(self.nc.semaphore(NAME_PFX[0] + name))

    def _esem(self, e, ep):
        while len(self.sems[e]) <= ep:
            self.sems[e].append(self.newsem("s_%s_%d" % (e, len(self.sems[e]))))
        return self.sems[e][ep]

    def _wait(self, e, tok):
        seen = self.seen[e]
        if tok[0] == "c":
            _, f, n = tok
            if seen.get(f, 0) >= n:
                return
            seen[f] = n
            ep = (n - 1) // self.EPOCH
            self.eng[e].wait_ge(self._esem(f, ep), n - ep * self.EPOCH)
        else:
            _, sem, k, mult = tok
            key = ("d", id(sem))
            if seen.get(key, 0) >= k:
                return
            seen[key] = k
            self.eng[e].wait_ge(sem, mult * k)
        self.nwait += 1

    def op(self, e, fn, reads=(), writes=(), dma=None, cc=False):
        deps = []
        for b in reads:
            if b.last_w is not None:
                deps.append(b.last_w)
        for b in writes:
            if b.last_w is not None:
                deps.append(b.last_w)
            deps.extend(b.readers)
        for d in deps:
            self._wait(e, d)
        inst = fn(self.eng[e])
        if dma is not None and cc:
            if dma.csem is None:
                dma.csem = self.newsem("c_" + dma.name)
            dma.ccnt += 1
            inst.then_inc(dma.csem)
            tok = ("d", dma.csem, dma.ccnt, 1)
            self.dma_all[id(dma.csem)] = tok
        elif dma is not None:
            if dma.dsem is None or dma.dcnt >= 1500:
                dma.dsem = self.newsem("d_" + dma.name)
                dma.dcnt = 0
            dma.dcnt += 1
            inst.then_inc(dma.dsem, 16)
            tok = ("d", dma.dsem, dma.dcnt, 16)
            self.dma_all[id(dma.dsem)] = tok
        else:
            self.cnt[e] += 1
            n = self.cnt[e]
            ep = (n - 1) // self.EPOCH
            inst.then_inc(self._esem(e, ep), 1)
            tok = ("c", e, n)
        for b in reads:
            if tok[0] == "c":
                b.readers = [r for r in b.readers if not (r[0] == "c" and r[1] == tok[1])]
            b.readers.append(tok)
        for b in writes:
            b.last_w = tok
            b.readers = []
        return tok

    def barrier(self):
        for e in self.eng:
            for f in self.eng:
                if self.cnt[f] > 0:
                    self._wait(e, ("c", f, self.cnt[f]))
            for tok in self.dma_all.values():
                self._wait(e, tok)

    def finish(self, bufs, e="sp"):
        for b in bufs:
            if b.last_w is not None:
                self._wait(e, b.last_w)
            for r in b.readers:
                self._wait(e, r)


class Pool:
    def __init__(self, nc, es, name, shape, dtype, n, psum=False):
        self.t = []
        self.b = []
        for i in range(n):
            if psum:
                t = es.enter_context(PS(nc, "%s%d" % (name, i), shape, dtype))
            else:
                t = es.enter_context(SB(nc, "%s%d" % (name, i), shape, dtype))
            self.t.append(t)
            self.b.append(Buf("%s%d" % (name, i)))
        self.i = 0
        self.n = n

    def next(self):
        i = self.i
        self.i = (i + 1) % self.n
        return self.t[i], self.b[i]


NTOK = 2048
D = 2048
DFF = 5632
DIN = 5120
EPS = 1e-6


def _dram(nc, name, shape, dt, kind):
    if name in DRAM_OVERRIDE:
        return DRAM_OVERRIDE[name]
    full = DRAM_PFX[0] + name
    key = (id(nc), full)
    if key not in DRAM_CACHE:
        DRAM_CACHE[key] = nc.dram_tensor(full, list(shape), dt, kind=kind).ap()
    return DRAM_CACHE[key]


def dram_in(nc, name, shape, dt=F32):
    return _dram(nc, name, shape, dt, "ExternalInput")


def dram_out(nc, name, shape, dt=F32):
    return _dram(nc, name, shape, dt, "ExternalOutput")


class Ctx:
    pass


def setup_common(nc, es, S):
    c = Ctx()
    c.nc, c.es, c.S = nc, es, S
    c.ident_d = dram_in(nc, "ident", [128, 128], BF16)
    c.ident = es.enter_context(SB(nc, "ident_sb", [128, 128], BF16))
    c.identB = Buf("ident")
    S.op("sp", lambda q: q.dma_start(out=c.ident[:], in_=c.ident_d), writes=[c.identB], dma=c.identB)
    c.psA = Pool(nc, es, "psA", [128, 512], F32, 2, psum=True)
    c.psB = Pool(nc, es, "psB", [128, 512], F32, 2, psum=True)
    c.psO = Pool(nc, es, "psO", [128, 512], F32, 2, psum=True)
    c.psT = Pool(nc, es, "psT", [128, 8, 128], BF16, 2, psum=True)
    c.stat = Pool(nc, es, "stat", [128, 16], F32, 6)
    c.xn = Pool(nc, es, "xn", [128, 2048], BF16, 2)
    c.junk = es.enter_context(SB(nc, "junk", [128, 2048], BF16))
    c.junkB = Buf("junk")
    return c


def rms_to_hT(c, x_ap, xB, gcol, gcolB, hT, hTB, off):
    S = c.S
    st, sB = c.stat.next()
    S.op("act", lambda a: a.activation(out=c.junk[:], in_=x_ap, func=AF.Square, accum_out=st[:, 0:1]),
         reads=[xB], writes=[c.junkB, sB])
    S.op("dve", lambda v: v.tensor_scalar(out=st[:, 1:2], in0=st[:, 0:1], scalar1=1.0 / D, scalar2=EPS,
                                          op0=ALU.mult, op1=ALU.add), reads=[sB], writes=[sB])
    S.op("act", lambda a: a.activation(out=st[:, 2:3], in_=st[:, 1:2], func=AF.Sqrt), reads=[sB], writes=[sB])
    S.op("dve", lambda v: v.reciprocal(out=st[:, 3:4], in_=st[:, 2:3]), reads=[sB], writes=[sB])
    xn, xnB = c.xn.next()
    S.op("act", lambda a: a.activation(out=xn[:], in_=x_ap, func=AF.Copy, scale=st[:, 3:4]),
         reads=[xB, sB], writes=[xnB])
    for half in range(2):
        pt, pB = c.psT.next()

        def tr(pe, pt=pt, half=half):
            for j in range(8):
                kc = half * 8 + j
                i = pe.transpose(out=pt[:, j, :], in_=xn[:, kc * 128:(kc + 1) * 128], identity=c.ident[:])
            return i
        S.op("pe", tr, reads=[xnB, c.identB], writes=[pB])
        S.op("dve", lambda v, pt=pt, half=half: v.tensor_tensor(
            out=hT[:, half * 8:(half + 1) * 8, off:off + 128], in0=pt[:],
            in1=gcol[:, half * 8:(half + 1) * 8].unsqueeze(2).to_broadcast([128, 8, 128]), op=ALU.mult),
            reads=[pB, gcolB], writes=[hTB])


def ffn_supertile(c, xt, xtB, hT, hTB, wg, wu, wd, W):
    S = c.S
    NFG = DFF // 512
    for fg in range(NFG):
        wgs, wgB = W.wgu.next()
        wus, wuB = W.wgu.next()
        wds, wdB = W.wd.next()
        S.op("pool", lambda q, wgs=wgs, fg=fg: q.dma_start(
            out=wgs[:], in_=wg[:, fg * 512:(fg + 1) * 512].rearrange("(kc p) n -> p kc n", p=128)),
            writes=[wgB], dma=wgB)
        S.op("pool", lambda q, wus=wus, fg=fg: q.dma_start(
            out=wus[:], in_=wu[:, fg * 512:(fg + 1) * 512].rearrange("(kc p) n -> p kc n", p=128)),
            writes=[wuB], dma=wuB)
        S.op("pool", lambda q, wds=wds, fg=fg: q.dma_start(
            out=wds[:], in_=wd[fg * 512:(fg + 1) * 512, :].rearrange("(c p) n -> p c n", p=128)),
            writes=[wdB], dma=wdB)
        aT, aTB = W.aT.next()
        for cc in range(4):
            pg, pgB = c.psA.next()
            pu, puB = c.psB.next()

            def mm(pe, pt, ws, cc=cc):
                for kc in range(16):
                    i = pe.matmul(pt[:], lhsT=ws[:, kc, cc * 128:(cc + 1) * 128], rhs=hT[:, kc, :],
                                  start=(kc == 0), stop=(kc == 15))
                return i
            S.op("pe", lambda pe, pg=pg, wgs=wgs: mm(pe, pg, wgs), reads=[wgB, hTB], writes=[pgB])
            S.op("pe", lambda pe, pu=pu, wus=wus: mm(pe, pu, wus), reads=[wuB, hTB], writes=[puB])
            sg, sgB = W.sg.next()
            S.op("act", lambda a, sg=sg, pg=pg: a.activation(out=sg[:], in_=pg[:], func=AF.Silu),
                 reads=[pgB], writes=[sgB])
            S.op("dve", lambda v, sg=sg, pu=pu, aT=aT, cc=cc: v.tensor_tensor(
                out=aT[:, cc, :], in0=sg[:], in1=pu[:], op=ALU.mult), reads=[sgB, puB], writes=[aTB])
        for tt in range(4):
            for dg in range(4):
                po, poB = c.psO.next()

                def mmo(pe, po=po, tt=tt, dg=dg, aT=aT, wds=wds):
                    for cc in range(4):
                        i = pe.matmul(po[:], lhsT=aT[:, cc, tt * 128:(tt + 1) * 128],
                                      rhs=wds[:, cc, dg * 512:(dg + 1) * 512], start=(cc == 0), stop=(cc == 3))
                    return i
                S.op("pe", mmo, reads=[aTB, wdB], writes=[poB])
                S.op("dve", lambda v, po=po, tt=tt, dg=dg: v.scalar_tensor_tensor(
                    out=xt[tt][:, dg * 512:(dg + 1) * 512], in0=po[:], scalar=0.5,
                    in1=xt[tt][:, dg * 512:(dg + 1) * 512], op0=ALU.mult, op1=ALU.add),
                    reads=[poB], writes=[xtB[tt]])


class WPools:
    pass


def make_wpools(nc, es):
    W = WPools()
    W.wgu = Pool(nc, es, "wgu", [128, 16, 512], BF16, 4)
    W.wd = Pool(nc, es, "wd", [128, 4, 2048], BF16, 2)
    W.aT = Pool(nc, es, "aT", [128, 4, 512], BF16, 2)
    W.sg = Pool(nc, es, "sg", [128, 512], F32, 2)
    return W


class InprojRes:
    pass


def make_inproj_res(c, nc, es, S):
    R = InprojRes()
    R.gqk_d = dram_in(nc, "gqk", [128, 4, 64])
    R.rot_d = dram_in(nc, "rot", [NTOK, 2, 8, 32])
    R.gqk = es.enter_context(SB(nc, "gqk_sb", [128, 4, 64], F32)); R.gqkB = Buf("gqk")
    S.op("sp", lambda q: q.dma_start(out=R.gqk[:], in_=R.gqk_d), writes=[R.gqkB], dma=R.gqkB)
    R.rot = Pool(nc, es, "rot_sb", [128, 2, 8, 32], F32, 2)
    R.tmpA = Pool(nc, es, "ipA", [128, 512], F32, 2)
    R.tmpB = Pool(nc, es, "ipB", [128, 512], F32, 2)
    R.tb = Pool(nc, es, "ipb16", [128, 512], BF16, 3)
    R.stage = Pool(nc, es, "ipstage", [128, 4, 512], BF16, 2)
    R.sgo = Pool(nc, es, "ipsg", [128, 512], F32, 2)
    R.qTA = dram_out(nc, "qTA", [512, NTOK], BF16); R.kTA = dram_out(nc, "kTA", [512, NTOK], BF16)
    R.qTB = dram_out(nc, "qTB", [512, NTOK], BF16); R.kTB = dram_out(nc, "kTB", [512, NTOK], BF16)
    R.vA = dram_out(nc, "vA", [NTOK, 512], BF16); R.vB = dram_out(nc, "vB", [NTOK, 512], BF16)
    R.qkTR = dram_out(nc, "qkTR", [512, NTOK], BF16)
    R.kR = dram_out(nc, "kR", [NTOK, 256], BF16)
    R.vR = dram_out(nc, "vR", [NTOK, 512], BF16)
    R.sgate = dram_out(nc, "sgate", [NTOK, 512], F32)
    R.uT = dram_out(nc, "uT", [512, NTOK], BF16)
    return R


def inproj_supertile(c, R, hT, hTB, win, W, st):
    S = c.S
    tokbase = st * 512
    for cg in range(10):
        ws, wB = W.wgu.next()
        S.op("pool", lambda q, ws=ws, cg=cg: q.dma_start(
            out=ws[:], in_=win[:, cg * 512:(cg + 1) * 512].rearrange("(kc p) n -> p kc n", p=128)),
            writes=[wB], dma=wB)
        need_T = cg in (0, 1, 3, 4, 6, 9)
        if need_T:
            stg, stgB = R.stage.next()
        for tt in range(4):
            po, poB = c.psO.next()
            t0 = tokbase + tt * 128

            def mm(pe, po=po, ws=ws, tt=tt):
                for kc in range(16):
                    i = pe.matmul(po[:], lhsT=hT[:, kc, tt * 128:(tt + 1) * 128], rhs=ws[:, kc, :],
                                  start=(kc == 0), stop=(kc == 15))
                return i
            S.op("pe", mm, reads=[wB, hTB], writes=[poB])
            tb, tbB = R.tb.next()
            if cg in (0, 1, 3, 4):
                gi = {0: 0, 1: 1, 3: 2, 4: 3}[cg]
                ta, taB = R.tmpA.next()
                st_, sB = c.stat.next()
                S.op("act", lambda a, ta=ta, po=po: a.activation(out=ta[:], in_=po[:], func=AF.Square),
                     reads=[poB], writes=[taB])
                S.op("dve", lambda v, ta=ta, st_=st_: v.reduce_sum(
                    out=st_[:, 0:8], in_=ta[:].rearrange("p (h e) -> p h e", e=64), axis=AX.X),
                    reads=[taB], writes=[sB])
                S.op("dve", lambda v, st_=st_: v.tensor_scalar(out=st_[:, 8:16], in0=st_[:, 0:8], scalar1=1.0 / 64,
                                                            scalar2=EPS, op0=ALU.mult, op1=ALU.add),
                     reads=[sB], writes=[sB])
                S.op("act", lambda a, st_=st_: a.activation(out=st_[:, 0:8], in_=st_[:, 8:16], func=AF.Sqrt),
                     reads=[sB], writes=[sB])
                S.op("dve", lambda v, st_=st_: v.reciprocal(out=st_[:, 8:16], in_=st_[:, 0:8]), reads=[sB], writes=[sB])
                t2, t2B = R.tmpB.next()
                S.op("dve", lambda v, t2=t2, po=po, st_=st_: v.tensor_tensor(
                    out=t2[:].rearrange("p (h e) -> p h e", e=64), in0=po[:].rearrange("p (h e) -> p h e", e=64),
                    in1=st_[:, 8:16].unsqueeze(2).to_broadcast([128, 8, 64]), op=ALU.mult),
                    reads=[poB, sB], writes=[t2B])
                S.op("pool", lambda g, t2=t2, tb=tb, gi=gi: g.tensor_tensor(
                    out=tb[:].rearrange("p (h e) -> p h e", e=64), in0=t2[:].rearrange("p (h e) -> p h e", e=64),
                    in1=R.gqk[:, gi, :].unsqueeze(1).to_broadcast([128, 8, 64]), op=ALU.mult),
                    reads=[t2B, R.gqkB], writes=[tbB])
            elif cg in (2, 5, 7, 9):
                S.op("act", lambda a, tb=tb, po=po: a.activation(out=tb[:], in_=po[:], func=AF.Copy),
                     reads=[poB], writes=[tbB])
                if cg != 9:
                    dst = {2: R.vA, 5: R.vB, 7: R.vR}[cg]
                    S.op("sp", lambda q, tb=tb, dst=dst, t0=t0: q.dma_start(out=dst[t0:t0 + 128, :], in_=tb[:]),
                         reads=[tbB], dma=tbB)
            elif cg == 8:
                so, soB = R.sgo.next()
                S.op("act", lambda a, so=so, po=po: a.activation(out=so[:], in_=po[:], func=AF.Silu),
                     reads=[poB], writes=[soB])
                S.op("sp", lambda q, so=so, t0=t0: q.dma_start(out=R.sgate[t0:t0 + 128, :], in_=so[:]),
                     reads=[soB], dma=soB)
            elif cg == 6:
                rt, rtB = R.rot.next()
                S.op("sp", lambda q, rt=rt, t0=t0: q.dma_start(out=rt[:], in_=R.rot_d[t0:t0 + 128]),
                     writes=[rtB], dma=rtB)
                ta, taB = R.tmpA.next()
                t2, t2B = R.tmpB.next()
                pv = po[:].rearrange("p (h two e) -> p h two e", two=2, e=32)
                tav = ta[:].rearrange("p (h two e) -> p h two e", two=2, e=32)
                t2v = t2[:].rearrange("p (h two e) -> p h two e", two=2, e=32)
                tbv = tb[:].rearrange("p (h two e) -> p h two e", two=2, e=32)
                S.op("dve", lambda v, rt=rt: v.tensor_tensor(out=tav[:, :, 0, :], in0=pv[:, :, 0, :], in1=rt[:, 0], op=ALU.mult),
                     reads=[poB, rtB], writes=[taB])
                S.op("dve", lambda v, rt=rt: v.tensor_tensor(out=tav[:, :, 1, :], in0=pv[:, :, 0, :], in1=rt[:, 1], op=ALU.mult),
                     reads=[poB, rtB], writes=[taB])
                S.op("dve", lambda v, rt=rt: v.tensor_tensor(out=t2v[:, :, 0, :], in0=pv[:, :, 1, :], in1=rt[:, 1], op=ALU.mult),
                     reads=[poB, rtB], writes=[t2B])
                S.op("dve", lambda v, rt=rt: v.tensor_tensor(out=t2v[:, :, 1, :], in0=pv[:, :, 1, :], in1=rt[:, 0], op=ALU.mult),
                     reads=[poB, rtB], writes=[t2B])
                S.op("pool", lambda g: g.tensor_tensor(out=tbv[:, :, 0, :], in0=tav[:, :, 0, :], in1=t2v[:, :, 0, :], op=ALU.subtract),
                     reads=[taB, t2B], writes=[tbB])
                S.op("pool", lambda g: g.tensor_tensor(out=tbv[:, :, 1, :], in0=tav[:, :, 1, :], in1=t2v[:, :, 1, :], op=ALU.add),
                     reads=[taB, t2B], writes=[tbB])
                S.op("sp", lambda q, tb=tb, t0=t0: q.dma_start(out=R.kR[t0:t0 + 128, :], in_=tb[:, 256:512]),
                     reads=[tbB], dma=tbB)
            if need_T:
                pt, pB = c.psT.next()

                def tr(pe, pt=pt, tb=tb):
                    for j in range(4):
                        i = pe.transpose(out=pt[:, j, :], in_=tb[:, j * 128:(j + 1) * 128], identity=c.ident[:])
                    return i
                S.op("pe", tr, reads=[tbB, c.identB], writes=[pB])
                S.op("act", lambda a, pt=pt, stg=stg, tt=tt: a.activation(
                    out=stg[:, :, tt * 128:(tt + 1) * 128], in_=pt[:, 0:4, :], func=AF.Copy),
                    reads=[pB], writes=[stgB])
        if need_T:
            dst = {0: R.qTA, 1: R.kTA, 3: R.qTB, 4: R.kTB, 6: R.qkTR, 9: R.uT}[cg]
            S.op("sp", lambda q, stg=stg, dst=dst: q.dma_start(
                out=dst[:, tokbase:tokbase + 512].rearrange("(c p) n -> p c n", p=128), in_=stg[:]),
                reads=[stgB], dma=stgB)


def body_LA(nc, S, inproj=True, sfin=True):
    x = dram_in(nc, "x", [NTOK, D])
    g0 = dram_in(nc, "g0", [128, 16])
    wg = dram_in(nc, "wg", [D, DFF]); wu = dram_in(nc, "wu", [D, DFF]); wd = dram_in(nc, "wd", [DFF, D])
    x1 = dram_out(nc, "x1", [NTOK, D])
    if inproj:
        g1 = dram_in(nc, "g1", [128, 16])
        win = dram_in(nc, "win", [D, DIN])
    if True:
        with ExitStack() as es:
            c = setup_common(nc, es, S)
            W = make_wpools(nc, es)
            gcol = es.enter_context(SB(nc, "gcol", [128, 2, 16], F32)); gB = Buf("gcol")
            S.op("sp", lambda q: q.dma_start(out=gcol[:, 0, :], in_=g0), writes=[gB], dma=gB)
            if inproj:
                S.op("sp", lambda q: q.dma_start(out=gcol[:, 1, :], in_=g1), writes=[gB], dma=gB)
                R = make_inproj_res(c, nc, es, S)
            hT = es.enter_context(SB(nc, "hT", [128, 16, 512], BF16)); hTB = Buf("hT")
            xt = []; xtB = []
            for tt in range(4):
                xt.append(es.enter_context(SB(nc, "xt%d" % tt, [128, 2048], F32))); xtB.append(Buf("xt%d" % tt))
            for st in range(NTOK // 512):
                for tt in range(4):
                    r0 = st * 512 + tt * 128
                    S.op("sp", lambda q, tt=tt, r0=r0: q.dma_start(out=xt[tt][:], in_=x[r0:r0 + 128, :]),
                         writes=[xtB[tt]], dma=xtB[tt])
                for tt in range(4):
                    rms_to_hT(c, xt[tt][:], xtB[tt], gcol[:, 0, :], gB, hT, hTB, tt * 128)
                ffn_supertile(c, xt, xtB, hT, hTB, wg, wu, wd, W)
                for tt in range(4):
                    r0 = st * 512 + tt * 128
                    S.op("sp", lambda q, tt=tt, r0=r0: q.dma_start(out=x1[r0:r0 + 128, :], in_=xt[tt][:]),
                         reads=[xtB[tt]], dma=xtB[tt])
                if inproj:
                    for tt in range(4):
                        rms_to_hT(c, xt[tt][:], xtB[tt], gcol[:, 1, :], gB, hT, hTB, tt * 128)
                    inproj_supertile(c, R, hT, hTB, win, W, st)
            fin = list(xtB)
            if inproj:
                fin += R.stage.b + R.tb.b + R.sgo.b
            S.finish(fin, "sp")
            S.barrier()
        if inproj and sfin:
            with ExitStack() as es2:
                c2 = Ctx()
                c2.psO = Pool(nc, es2, "psO2", [128, 512], F32, 2, psum=True)
                dl = dram_in(nc, "ret_dl", [128, 8]); esf = dram_in(nc, "ret_esf", [128, 16, 2])
                sf = dram_out(nc, "sfin", [8, 64, 128])
                soB = ret_sfin(c2, nc, es2, S, R.kR, R.vR, dl, esf, sf)
                S.finish([soB], "sp")
                S.barrier()


def build_LA(inproj=True, sfin=True):
    nc = bass.Bass("TRN2", target_bir_lowering=False)
    with ExitStack() as es0:
        S = Sched(nc, es0)
        body_LA(nc, S, inproj, sfin)
        print("LA sems", S.nsem, "waits", S.nwait, "cnt", S.cnt)
    return nc


def v1_lhsT(Vt, idx, h):
    return Vt[(slice(None),) + tuple(idx) + (h, slice(None))]


class AttnRes:
    pass


def make_attn_res(c, nc, es, S, sbuf=True):
    A = AttnRes()
    A.psS = Pool(nc, es, "psS", [128, 4, 128], F32, 2, psum=True)
    A.psV = Pool(nc, es, "psV", [128, 128], F32, 2, psum=True)
    A.psF = Pool(nc, es, "psF", [128, 128], F32, 2, psum=True)
    if not sbuf:
        return A
    A.tmp = Pool(nc, es, "atmp", [128, 4, 128], F32, 3)
    A.pb = Pool(nc, es, "apb", [128, 4, 128], BF16, 3)
    A.identf_d = dram_in(nc, "identf", [128, 128], F32)
    A.identf = es.enter_context(SB(nc, "identf_sb", [128, 128], F32)); A.identfB = Buf("identf")
    S.op("sp", lambda q: q.dma_start(out=A.identf[:], in_=A.identf_d), writes=[A.identfB], dma=A.identfB)
    A.rz = Pool(nc, es, "arz", [128, 2], F32, 4)
    A.yst = Pool(nc, es, "ayst", [128, 16, 64], F32, 2)
    A.KT = Pool(nc, es, "aKT", [64, 4096], BF16, 2)
    A.QT = Pool(nc, es, "aQT", [64, 2048], BF16, 2)
    return A


def attn_unit(c, A, qaps, kaps, vaps, bias_ap, biasB, rB, accs, first):
    S = c.S
    ps, psB = A.psS.next()
    n = len(qaps)

    def mm(pe):
        for i in range(n):
            ins = pe.matmul(ps[:, i, :], lhsT=kaps[i], rhs=qaps[i], start=True, stop=True)
        return ins
    S.op("pe", mm, reads=rB, writes=[psB])
    tm, tmB = A.tmp.next()
    S.op("dve", lambda v: v.scalar_tensor_tensor(out=tm[:, 0:n, :], in0=ps[:, 0:n, :], scalar=0.125, in1=bias_ap,
                                                 op0=ALU.mult, op1=ALU.add), reads=[psB, biasB], writes=[tmB])
    pb, pbB = A.pb.next()
    S.op("act", lambda a: a.activation(out=pb[:, 0:n, :], in_=tm[:, 0:n, :], func=AF.Exp), reads=[tmB], writes=[pbB])
    for (acc_ap, accB, slots) in accs:
        pv, pvB = A.psV.next()

        def mv(pe, pv=pv, slots=slots):
            for k, i in enumerate(slots):
                ins = pe.matmul(pv[:], lhsT=vaps[i], rhs=pb[:, i, :], start=(k == 0), stop=(k == len(slots) - 1))
            return ins
        S.op("pe", mv, reads=[pbB] + rB, writes=[pvB])
        if first:
            S.op("act", lambda a, pv=pv, acc_ap=acc_ap: a.activation(out=acc_ap, in_=pv[:], func=AF.Copy),
                 reads=[pvB], writes=[accB])
        else:
            S.op("dve", lambda v, pv=pv, acc_ap=acc_ap: v.tensor_tensor(out=acc_ap, in0=acc_ap, in1=pv[:], op=ALU.add),
                 reads=[pvB], writes=[accB])


def attn_finalize_head(c, A, acc, accB, y_dram, h):
    S = c.S
    yst, ystB = A.yst.next()
    for t in range(16):
        pf, pfB = A.psF.next()
        S.op("pe", lambda pe, pf=pf, t=t: pe.transpose(out=pf[:], in_=acc[:, t * 128:(t + 1) * 128], identity=A.identf[:]),
             reads=[accB, A.identfB], writes=[pfB])
        rz, rzB = A.rz.next()
        S.op("dve", lambda v, pf=pf, rz=rz: v.reciprocal(out=rz[:, 0:1], in_=pf[:, 64:65]), reads=[pfB], writes=[rzB])
        S.op("dve", lambda v, pf=pf, rz=rz, t=t: v.tensor_scalar(out=yst[:, t, :], in0=pf[:, 0:64], scalar1=rz[:, 0:1],
                                                              scalar2=None, op0=ALU.mult),
             reads=[pfB, rzB], writes=[ystB])
    S.op("sp", lambda q: q.dma_start(out=y_dram[:, h * 64:(h + 1) * 64].rearrange("(t p) e -> p t e", p=128), in_=yst[:]),
         reads=[ystB], dma=ystB)
    return ystB


def build_attn_A(c, A, nc, es, S):
    qT = dram_in(nc, "qTA", [512, 2048], BF16)
    kT = dram_in(nc, "kTAh", [512, 3072], BF16)
    vh = dram_in(nc, "vAh", [3072, 512], BF16)
    val = dram_in(nc, "valA", [3072, 512], BF16)
    bias = dram_in(nc, "biasA", [8, 5, 128, 8, 128], F32)
    yA = dram_out(nc, "yA", [2048, 512], F32)
    Vt = es.enter_context(SB(nc, "VtA", [128, 24, 8, 128], BF16)); VtB = Buf("VtA")
    for hh in range(8):
        S.op("sp", lambda q, hh=hh: q.dma_start(out=Vt[:, :, hh, 0:64], in_=vh[:, hh * 64:(hh + 1) * 64].rearrange("(c p) e -> p c e", p=128)),
             writes=[VtB], dma=VtB)
        S.op("sp", lambda q, hh=hh: q.dma_start(out=Vt[:, :, hh, 64:128], in_=val[:, hh * 64:(hh + 1) * 64].rearrange("(c p) e -> p c e", p=128)),
             writes=[VtB], dma=VtB)
    bt = Pool(nc, es, "biasA_sb", [128, 5, 8, 128], F32, 2)
    accp = Pool(nc, es, "accA", [128, 2048], F32, 2)
    fin = []
    for h in range(8):
        K, KB = A.KT.next()
        Q, QB = A.QT.next()
        b_, bB = bt.next()
        S.op("sp", lambda q, K=K, h=h: q.dma_start(out=K[:, 0:3072], in_=kT[h * 64:(h + 1) * 64, :]), writes=[KB], dma=KB)
        S.op("sp", lambda q, Q=Q, h=h: q.dma_start(out=Q[:], in_=qT[h * 64:(h + 1) * 64, :]), writes=[QB], dma=QB)
        S.op("sp", lambda q, b_=b_, h=h: q.dma_start(out=b_[:], in_=bias[h].rearrange("t p c q -> p t c q")), writes=[bB], dma=bB)
        acc, accB = accp.next()
        for b in range(16):
            ty = {0: 1, 1: 2, 14: 3, 15: 4}.get(b, 0)
            qap = Q[:, b * 128:(b + 1) * 128]
            for half in range(2):
                kaps = [K[:, (b + half * 4 + i) * 128:(b + half * 4 + i + 1) * 128] for i in range(4)]
                vaps = [v1_lhsT(Vt, (b + half * 4 + i,), h) for i in range(4)]
                attn_unit(c, A, [qap] * 4, kaps, vaps, b_[:, ty, half * 4:(half + 1) * 4, :], bB, [KB, QB, VtB],
                          [(acc[:, b * 128:(b + 1) * 128], accB, [0, 1, 2, 3])], first=(half == 0))
        fin.append(attn_finalize_head(c, A, acc, accB, yA, h))
    return fin


def build_attn_B(c, A, nc, es, S):
    qT = dram_in(nc, "qTB", [512, 2048], BF16)
    kT = dram_in(nc, "kTBh", [512, 4096], BF16)
    vh = dram_in(nc, "vBh", [4096 + 16, 512], BF16)
    val = dram_in(nc, "valB", [4096 + 16, 512], BF16)
    bias = dram_in(nc, "biasB", [3, 8, 128, 2, 128], F32)
    yB = dram_out(nc, "yB", [2048, 512], F32)
    Vt = es.enter_context(SB(nc, "VtB", [128, 32, 8, 128], BF16)); VtB = Buf("VtB")
    bt = es.enter_context(SB(nc, "biasB_sb", [128, 3, 8, 2, 128], F32)); btB = Buf("biasBt")
    S.op("sp", lambda q: q.dma_start(out=bt[:].rearrange("p a h c q -> p (a h) c q"),
                                     in_=bias.rearrange("a h p c q -> p (a h) c q")), writes=[btB], dma=btB)
    accs = []
    for h in range(8):
        accs.append((es.enter_context(SB(nc, "accB%d" % h, [128, 2048], F32)), Buf("accB%d" % h)))
    for bi, d in enumerate((1, 4, 16)):
        nsub = 2048 // d
        nblk = nsub // 128
        nch = nblk + 1
        for rho in range(d):
            base = 1024 + rho - 64 * d
            for hh in range(8):
                src = vh[base:base + d * 128 * nch, hh * 64:(hh + 1) * 64].rearrange("(m p dd) e -> p m dd e", p=128, dd=d)[:, :, 0, :]
                srcv = val[base:base + d * 128 * nch, hh * 64:(hh + 1) * 64].rearrange("(m p dd) e -> p m dd e", p=128, dd=d)[:, :, 0, :]
                S.op("sp", lambda q, rho=rho, src=src, hh=hh: q.dma_start(out=Vt[:, rho * nch:(rho + 1) * nch, hh, 0:64], in_=src),
                     writes=[VtB], dma=VtB)
                S.op("sp", lambda q, rho=rho, srcv=srcv, hh=hh: q.dma_start(out=Vt[:, rho * nch:(rho + 1) * nch, hh, 64:128], in_=srcv),
                     writes=[VtB], dma=VtB)
        for h in range(8):
            K, KB = A.KT.next()
            Q, QB = A.QT.next()
            S.op("sp", lambda q, K=K, h=h: q.dma_start(out=K[:], in_=kT[h * 64:(h + 1) * 64, :]), writes=[KB], dma=KB)
            S.op("sp", lambda q, Q=Q, h=h: q.dma_start(out=Q[:], in_=qT[h * 64:(h + 1) * 64, :]), writes=[QB], dma=QB)
            acc, accB = accs[h]
            units = [(rho, j) for rho in range(d) for j in range(nblk)]
            for u0 in range(0, len(units), 2):
                qaps, kaps, vaps, acl = [], [], [], []
                for ui, (rho, j) in enumerate(units[u0:u0 + 2]):
                    q0 = rho + d * 128 * j
                    qap = Q[:, q0:q0 + d * 127 + 1:d]
                    for m in range(2):
                        k0 = 1024 + rho + d * (128 * (j + m) - 64)
                        kaps.append(K[:, k0:k0 + d * 127 + 1:d])
                        qaps.append(qap)
                        vaps.append(v1_lhsT(Vt, (rho * nch + j + m,), h))
                    acl.append((acc[:, q0:q0 + d * 127 + 1:d], accB, [2 * ui, 2 * ui + 1]))
                nb = len(acl)
                a_ = bt[:, bi, h]
                bap = bass.AP(a_.tensor, a_.offset, [list(a_.ap[0]), [0, nb], [128, 2], [1, 128]])
                attn_unit_b(c, A, qaps, kaps, vaps, bap, btB, [KB, QB, VtB], acl, first=(bi == 0), nb=nb)
    fin = []
    for h in range(8):
        fin.append(attn_finalize_head(c, A, accs[h][0], accs[h][1], yB, h))
    return fin


def attn_unit_b(c, A, qaps, kaps, vaps, bias_ap, biasB, rB, accs, first, nb):
    S = c.S
    ps, psB = A.psS.next()
    n = len(qaps)

    def mm(pe):
        for i in range(n):
            ins = pe.matmul(ps[:, i, :], lhsT=kaps[i], rhs=qaps[i], start=True, stop=True)
        return ins
    S.op("pe", mm, reads=rB, writes=[psB])
    tm, tmB = A.tmp.next()
    S.op("dve", lambda v: v.scalar_tensor_tensor(
        out=tm[:, 0:n, :].rearrange("p (a c) q -> p a c q", c=2), in0=ps[:, 0:n, :].rearrange("p (a c) q -> p a c q", c=2),
        scalar=0.125, in1=bias_ap, op0=ALU.mult, op1=ALU.add), reads=[psB, biasB], writes=[tmB])
    pb, pbB = A.pb.next()
    S.op("act", lambda a: a.activation(out=pb[:, 0:n, :], in_=tm[:, 0:n, :], func=AF.Exp), reads=[tmB], writes=[pbB])
    for (acc_ap, accB, slots) in accs:
        pv, pvB = A.psV.next()

        def mv(pe, pv=pv, slots=slots):
            for k, i in enumerate(slots):
                ins = pe.matmul(pv[:], lhsT=vaps[i], rhs=pb[:, i, :], start=(k == 0), stop=(k == len(slots) - 1))
            return ins
        S.op("pe", mv, reads=[pbB] + rB, writes=[pvB])
        if first:
            S.op("act", lambda a, pv=pv, acc_ap=acc_ap: a.activation(out=acc_ap, in_=pv[:], func=AF.Copy),
                 reads=[pvB], writes=[accB])
        else:
            S.op("dve", lambda v, pv=pv, acc_ap=acc_ap: v.tensor_tensor(out=acc_ap, in0=acc_ap, in1=pv[:], op=ALU.add),
                 reads=[pvB], writes=[accB])


def setup_lb(nc, es, S):
    c = Ctx()
    c.nc, c.es, c.S = nc, es, S
    return c


def body_attn(nc, S, doA=True, doB=True):
    with ExitStack() as es:
        c = setup_lb(nc, es, S)
        A = make_attn_res(c, nc, es, S)
        if doA:
            with ExitStack() as esA:
                fin = build_attn_A(c, A, nc, esA, S)
                S.finish(fin, "sp")
                S.barrier()
        if doB:
            with ExitStack() as esB:
                fin = build_attn_B(c, A, nc, esB, S)
                S.finish(fin, "sp")
                S.barrier()


def build_LB_attn(doA=True, doB=True):
    nc = bass.Bass("TRN2", target_bir_lowering=False)
    with ExitStack() as es0:
        S = Sched(nc, es0)
        body_attn(nc, S, doA, doB)
        print("LB sems", S.nsem, "waits", S.nwait, "cnt", S.cnt)
    return nc


BIG = 1.0e7


def ret_loggamma(c, nc, es, S, dl_d):
    t = es.enter_context(SB(nc, "ret_lg", [128, 4, 8], F32)); tB = Buf("ret_lg")
    S.op("sp", lambda q: q.dma_start(out=t[:, 0, :], in_=dl_d), writes=[tB], dma=tB)
    S.op("act", lambda a: a.activation(out=t[:, 1, :], in_=t[:, 0, :], func=AF.Exp, scale=-1.0), reads=[tB], writes=[tB])
    S.op("act", lambda a: a.activation(out=t[:, 2, :], in_=t[:, 1, :], func=AF.Ln, bias=1.0), reads=[tB], writes=[tB])
    S.op("dve", lambda v: v.tensor_scalar(out=t[:, 3, :], in0=t[:, 2, :], scalar1=-1.0, scalar2=None, op0=ALU.mult),
         reads=[tB], writes=[tB])
    return t[:, 3, :], tB


def ret_sfin(c, nc, es, S, kR, vR, dl_d, esf_d, sfin_out):
    lg, lgB = ret_loggamma(c, nc, es, S, dl_d)
    E = es.enter_context(SB(nc, "sf_E", [128, 16, 2], F32)); EB = Buf("sf_E")
    S.op("sp", lambda q: q.dma_start(out=E[:], in_=esf_d), writes=[EB], dma=EB)
    dec = es.enter_context(SB(nc, "sf_dec", [128, 16, 2, 4], F32)); decB = Buf("sf_dec")
    S.op("dve", lambda v: v.tensor_tensor(out=dec[:], in0=E[:].unsqueeze(3).to_broadcast([128, 16, 2, 4]),
                                          in1=lg.rearrange("p (d h) -> p d h", d=2).unsqueeze(1).to_broadcast([128, 16, 2, 4]),
                                          op=ALU.mult), reads=[EB, lgB], writes=[decB])
    S.op("act", lambda a: a.activation(out=dec[:], in_=dec[:], func=AF.Exp), reads=[decB], writes=[decB])
    kt = es.enter_context(SB(nc, "sf_k", [128, 16, 256], BF16)); ktB = Buf("sf_k")
    vt = es.enter_context(SB(nc, "sf_v", [128, 16, 512], BF16)); vtB = Buf("sf_v")
    S.op("sp", lambda q: q.dma_start(out=kt[:], in_=kR.rearrange("(t p) n -> p t n", p=128)), writes=[ktB], dma=ktB)
    S.op("sp", lambda q: q.dma_start(out=vt[:], in_=vR.rearrange("(t p) n -> p t n", p=128)), writes=[vtB], dma=vtB)
    kd = es.enter_context(SB(nc, "sf_kd", [128, 16, 2, 256], BF16)); kdB = Buf("sf_kd")
    for d in range(2):
        S.op("dve", lambda v, d=d: v.tensor_tensor(
            out=kd[:, :, d, :].rearrange("p t (h e) -> p t h e", e=64), in0=kt[:].rearrange("p t (h e) -> p t h e", e=64),
            in1=dec[:, :, d, :].unsqueeze(3).to_broadcast([128, 16, 4, 64]), op=ALU.mult), reads=[ktB, decB], writes=[kdB])
    so = es.enter_context(SB(nc, "sf_out", [64, 8, 128], F32)); soB = Buf("sf_out")
    for d in range(2):
        for h in range(4):
            ps, psB = c.psO.next()

            def mm(pe, ps=ps, d=d, h=h):
                for t in range(16):
                    i = pe.matmul(ps[0:64, 0:128], lhsT=kd[:, t, d, h * 64:(h + 1) * 64], rhs=vt[:, t, h * 128:(h + 1) * 128],
                                  start=(t == 0), stop=(t == 15))
                return i
            S.op("pe", mm, reads=[kdB, vtB], writes=[psB])
            S.op("act", lambda a, ps=ps, d=d, h=h: a.activation(out=so[:, d * 4 + h, :], in_=ps[0:64, 0:128], func=AF.Copy),
                 reads=[psB], writes=[soB])
    S.op("sp", lambda q: q.dma_start(out=sfin_out.rearrange("g k e -> k g e"), in_=so[:]), reads=[soB], dma=soB)
    return soB


def build_ret(c, A, nc, es, S):
    qkT = dram_in(nc, "qkTR", [512, 2048], BF16)
    kR = dram_in(nc, "kR", [2048, 256], BF16)
    vR = dram_in(nc, "vR", [2048, 512], BF16)
    sg = dram_in(nc, "sgate", [2048, 512], F32)
    sfa = dram_in(nc, "sfin_all", [8, 8, 64, 128], F32)
    dl_d = dram_in(nc, "ret_dl", [128, 8], F32)
    ncoef_d = dram_in(nc, "ret_ncoef", [128, 8, 2], F32)
    ekd_d = dram_in(nc, "ret_ekd", [128, 2], F32)
    eqd_d = dram_in(nc, "ret_eqd", [64, 2, 128], F32)
    emask_d = dram_in(nc, "ret_emask", [128, 2, 128], F32)
    yR = dram_out(nc, "yR", [2048, 512], F32)
    lg, lgB = ret_loggamma(c, nc, es, S, dl_d)
    lg3 = lg.rearrange("p (d h) -> p d h", d=2)

    def load(name, shape, src, dt=F32):
        t = es.enter_context(SB(nc, name, shape, dt)); b = Buf(name)
        S.op("sp", lambda q: q.dma_start(out=t[:], in_=src), writes=[b], dma=b)
        return t, b
    ncoef, ncoefB = load("r_ncoef", [128, 8, 2], ncoef_d)
    ekd, ekdB = load("r_ekd", [128, 2], ekd_d)
    eqd, eqdB = load("r_eqd", [64, 2, 128], eqd_d)
    emask, emaskB = load("r_emask", [128, 2, 128], emask_d)
    coef = es.enter_context(SB(nc, "r_coef", [128, 8, 2, 4], F32)); coefB = Buf("r_coef")
    S.op("dve", lambda v: v.tensor_tensor(out=coef[:], in0=ncoef[:].unsqueeze(3).to_broadcast([128, 8, 2, 4]),
                                          in1=lg3.unsqueeze(1).to_broadcast([128, 8, 2, 4]), op=ALU.mult),
         reads=[ncoefB, lgB], writes=[coefB])
    S.op("act", lambda a: a.activation(out=coef[:], in_=coef[:], func=AF.Exp), reads=[coefB], writes=[coefB])
    kdt = es.enter_context(SB(nc, "r_kdt", [128, 2, 4], F32)); kdtB = Buf("r_kdt")
    S.op("dve", lambda v: v.tensor_tensor(out=kdt[:], in0=ekd[:].unsqueeze(2).to_broadcast([128, 2, 4]), in1=lg3, op=ALU.mult),
         reads=[ekdB, lgB], writes=[kdtB])
    S.op("act", lambda a: a.activation(out=kdt[:], in_=kdt[:], func=AF.Exp), reads=[kdtB], writes=[kdtB])
    qdt = es.enter_context(SB(nc, "r_qdt", [64, 2, 4, 128], F32)); qdtB = Buf("r_qdt")
    S.op("dve", lambda v: v.tensor_tensor(out=qdt[:], in0=eqd[:].unsqueeze(2).to_broadcast([64, 2, 4, 128]),
                                          in1=lg3[0:64].unsqueeze(3).to_broadcast([64, 2, 4, 128]), op=ALU.mult),
         reads=[eqdB, lgB], writes=[qdtB])
    S.op("act", lambda a: a.activation(out=qdt[:], in_=qdt[:], func=AF.Exp), reads=[qdtB], writes=[qdtB])
    dm = es.enter_context(SB(nc, "r_dm", [128, 2, 4, 128], F32)); dmB = Buf("r_dm")
    S.op("dve", lambda v: v.tensor_tensor(out=dm[:], in0=emask[:].unsqueeze(2).to_broadcast([128, 2, 4, 128]),
                                          in1=lg3.unsqueeze(3).to_broadcast([128, 2, 4, 128]), op=ALU.mult),
         reads=[emaskB, lgB], writes=[dmB])
    S.op("act", lambda a: a.activation(out=dm[:], in_=dm[:], func=AF.Exp), reads=[dmB], writes=[dmB])
    dcomb = es.enter_context(SB(nc, "r_dcomb", [128, 4, 128], F32)); dcB = Buf("r_dcomb")
    S.op("dve", lambda v: v.tensor_tensor(out=dcomb[:], in0=dm[:, 0], in1=dm[:, 1], op=ALU.add), reads=[dmB], writes=[dcB])
    c128 = es.enter_context(SB(nc, "r_c128", [128, 2, 4], F32)); c128B = Buf("r_c128")
    S.op("act", lambda a: a.activation(out=c128[:], in_=lg3, func=AF.Exp, scale=128.0), reads=[lgB], writes=[c128B])
    sall = es.enter_context(SB(nc, "r_sall", [64, 8, 8, 128], F32)); sallB = Buf("r_sall")
    for cc in range(8):
        S.op("sp", lambda q, cc=cc: q.dma_start(out=sall[:, cc], in_=sfa[cc].rearrange("g k e -> k g e")), writes=[sallB], dma=sallB)
    S.op("dve", lambda v: v.tensor_tensor(out=sall[:], in0=sall[:],
                                          in1=coef[0:64].rearrange("p c d h -> p c (d h)").unsqueeze(3).to_broadcast([64, 8, 8, 128]),
                                          op=ALU.mult), reads=[coefB], writes=[sallB])
    sin_ = es.enter_context(SB(nc, "r_sin", [64, 8, 128], F32)); sinB = Buf("r_sin")
    S.op("dve", lambda v: v.reduce_sum(out=sin_[:], in_=sall[:].rearrange("p c g e -> p g e c"), axis=AX.X),
         reads=[sallB], writes=[sinB])
    QT, QTB = load("r_QT", [64, 4, 2048], qkT[0:256, :].rearrange("(h k) n -> k h n", k=64), BF16)
    KT, KTB = load("r_KT", [64, 4, 2048], qkT[256:512, :].rearrange("(h k) n -> k h n", k=64), BF16)
    kt, ktB = load("r_k", [128, 16, 256], kR.rearrange("(t p) n -> p t n", p=128), BF16)
    vt, vtB = load("r_v", [128, 16, 512], vR.rearrange("(t p) n -> p t n", p=128), BF16)
    qd = es.enter_context(SB(nc, "r_qd", [64, 2, 4, 2048], BF16)); qdB = Buf("r_qd")
    kd = es.enter_context(SB(nc, "r_kd", [128, 16, 2, 256], BF16)); kdB = Buf("r_kd")
    for d in range(2):
        for h in range(4):
            S.op("dve", lambda v, d=d, h=h: v.tensor_tensor(
                out=qd[:, d, h, :].rearrange("p (t j) -> p t j", j=128), in0=QT[:, h, :].rearrange("p (t j) -> p t j", j=128),
                in1=qdt[:, d, h, :].unsqueeze(1).to_broadcast([64, 16, 128]), op=ALU.mult), reads=[QTB, qdtB], writes=[qdB])
            S.op("pool", lambda g, d=d, h=h: g.tensor_scalar(
                out=kd[:, :, d, h * 64:(h + 1) * 64], in0=kt[:, :, h * 64:(h + 1) * 64], scalar1=kdt[:, d, h:h + 1],
                scalar2=None, op0=ALU.mult), reads=[ktB, kdtB], writes=[kdB])
    st32 = es.enter_context(SB(nc, "r_st32", [64, 8, 128], F32)); st32B = [Buf("r_st32_%d" % i) for i in range(8)]
    stb = es.enter_context(SB(nc, "r_stb", [64, 8, 16, 128], BF16)); stbB = [Buf("r_stb_%d" % i) for i in range(8)]
    for g in range(8):
        S.op("act", lambda a, g=g: a.activation(out=st32[:, g, :], in_=sin_[:, g, :], func=AF.Copy), reads=[sinB], writes=[st32B[g]])
    for step in range(16):
        for d in range(2):
            k = step if d == 0 else 15 - step
            for h in range(4):
                g = d * 4 + h
                S.op("act", lambda a, g=g, k=k: a.activation(out=stb[:, g, k, :], in_=st32[:, g, :], func=AF.Copy),
                     reads=[st32B[g]], writes=[stbB[g]])
                if step == 15:
                    continue
                ps, psB = A.psV.next()
                S.op("pe", lambda pe, ps=ps, d=d, h=h, k=k: pe.matmul(
                    ps[0:64, :], lhsT=kd[:, k, d, h * 64:(h + 1) * 64], rhs=vt[:, k, h * 128:(h + 1) * 128], start=True, stop=True),
                    reads=[kdB, vtB], writes=[psB])
                S.op("dve", lambda v, ps=ps, g=g, d=d, h=h: v.scalar_tensor_tensor(
                    out=st32[:, g, :], in0=st32[:, g, :], scalar=c128[0:64, d, h:h + 1], in1=ps[0:64, :],
                    op0=ALU.mult, op1=ALU.add), reads=[psB, c128B], writes=[st32B[g]])
    ytile = Pool(nc, es, "r_y", [128, 4, 128], F32, 2)
    sgt = Pool(nc, es, "r_sg", [128, 512], F32, 2)
    sq = Pool(nc, es, "r_sq", [128, 4, 128], F32, 2)
    pbt = Pool(nc, es, "r_pb", [128, 4, 128], BF16, 2)
    fin = []
    for k in range(16):
        ps, psB = A.psS.next()

        def mm(pe, ps=ps, k=k):
            for h in range(4):
                i = pe.matmul(ps[:, h, :], lhsT=KT[:, h, k * 128:(k + 1) * 128], rhs=QT[:, h, k * 128:(k + 1) * 128],
                              start=True, stop=True)
            return i
        S.op("pe", mm, reads=[KTB, QTB], writes=[psB])
        pb, pbB = pbt.next()
        S.op("dve", lambda v, ps=ps, pb=pb: v.tensor_tensor(out=pb[:], in0=ps[:], in1=dcomb[:], op=ALU.mult),
             reads=[psB, dcB], writes=[pbB])
        py, pyB = A.psS.next()

        def mo(pe, py=py, pb=pb, k=k):
            for h in range(4):
                pe.matmul(py[:, h, :], lhsT=pb[:, h, :], rhs=vt[:, k, h * 128:(h + 1) * 128], start=True, stop=False)
                pe.matmul(py[:, h, :], lhsT=qd[:, 0, h, k * 128:(k + 1) * 128], rhs=stb[:, h, k, :], start=False, stop=False)
                i = pe.matmul(py[:, h, :], lhsT=qd[:, 1, h, k * 128:(k + 1) * 128], rhs=stb[:, 4 + h, k, :], start=False, stop=True)
            return i
        S.op("pe", mo, reads=[pbB, vtB, qdB] + stbB, writes=[pyB])
        st_, sB = c.stat.next()
        y, yB = ytile.next()
        sgg, sgB = sgt.next()
        S.op("sp", lambda q, sgg=sgg, k=k: q.dma_start(out=sgg[:], in_=sg[k * 128:(k + 1) * 128, :]), writes=[sgB], dma=sgB)
        S.op("dve", lambda v, py=py, st_=st_: v.reduce_sum(out=st_[:, 0:4], in_=py[:], axis=AX.X), reads=[pyB], writes=[sB])
        S.op("dve", lambda v, st_=st_: v.tensor_scalar(out=st_[:, 4:8], in0=st_[:, 0:4], scalar1=1.0 / 128, scalar2=None, op0=ALU.mult),
             reads=[sB], writes=[sB])
        S.op("dve", lambda v, py=py, y=y, st_=st_: v.tensor_tensor(out=y[:], in0=py[:], in1=st_[:, 4:8].unsqueeze(2).to_broadcast([128, 4, 128]),
                                                                op=ALU.subtract), reads=[pyB, sB], writes=[yB])
        s2, s2B = sq.next()
        S.op("act", lambda a, s2=s2, y=y: a.activation(out=s2[:], in_=y[:], func=AF.Square), reads=[yB], writes=[s2B])
        S.op("dve", lambda v, s2=s2, st_=st_: v.reduce_sum(out=st_[:, 8:12], in_=s2[:], axis=AX.X), reads=[s2B], writes=[sB])
        S.op("dve", lambda v, st_=st_: v.tensor_scalar(out=st_[:, 12:16], in0=st_[:, 8:12], scalar1=1.0 / 128, scalar2=EPS,
                                                    op0=ALU.mult, op1=ALU.add), reads=[sB], writes=[sB])
        S.op("act", lambda a, st_=st_: a.activation(out=st_[:, 8:12], in_=st_[:, 12:16], func=AF.Sqrt), reads=[sB], writes=[sB])
        S.op("dve", lambda v, st_=st_: v.reciprocal(out=st_[:, 12:16], in_=st_[:, 8:12]), reads=[sB], writes=[sB])
        S.op("dve", lambda v, y=y, st_=st_: v.tensor_tensor(out=y[:], in0=y[:], in1=st_[:, 12:16].unsqueeze(2).to_broadcast([128, 4, 128]),
                                                         op=ALU.mult), reads=[sB], writes=[yB])
        S.op("pool", lambda g_, y=y, sgg=sgg: g_.tensor_tensor(out=y[:], in0=y[:], in1=sgg[:].rearrange("p (h e) -> p h e", e=128),
                                                             op=ALU.mult), reads=[sgB], writes=[yB])
        S.op("sp", lambda q, y=y, k=k: q.dma_start(out=yR[k * 128:(k + 1) * 128, :].rearrange("p (h e) -> p h e", e=128), in_=y[:]),
             reads=[yB], dma=yB)
    return ytile.b


def body_ret(nc, S):
    with ExitStack() as es:
        c = setup_lb(nc, es, S)
        c.stat = Pool(nc, es, "stat", [128, 16], F32, 6)
        A = make_attn_res(c, nc, es, S, sbuf=False)
        fin = build_ret(c, A, nc, es, S)
        S.finish(fin, "sp")
        S.barrier()


def build_LB_ret():
    nc = bass.Bass("TRN2", target_bir_lowering=False)
    with ExitStack() as es0:
        S = Sched(nc, es0)
        body_ret(nc, S)
        print("LBret sems", S.nsem, "waits", S.nwait, "cnt", S.cnt)
    return nc


def build_sfin_only():
    nc = bass.Bass("TRN2", target_bir_lowering=False)
    kR = dram_in(nc, "kR", [2048, 256], BF16); vR = dram_in(nc, "vR", [2048, 512], BF16)
    dl = dram_in(nc, "ret_dl", [128, 8], F32); esf = dram_in(nc, "ret_esf", [128, 16, 2], F32)
    sf = dram_out(nc, "sfin", [8, 64, 128], F32)
    with ExitStack() as es:
        S = Sched(nc, es)
        c = setup_lb(nc, es, S)
        c.psO = Pool(nc, es, "psO", [128, 512], F32, 2, psum=True)
        b = ret_sfin(c, nc, es, S, kR, vR, dl, esf, sf)
        S.finish([b], "sp")
    return nc


TWO_PI = 2.0 * math.pi
NE = 136


def build_s5(c, nc, es, S):
    uT_d = dram_in(nc, "s5_uT", [64, 16384], BF16)
    are_d = dram_in(nc, "s5_are", [128, 8]); aim_d = dram_in(nc, "s5_aim", [128, 8]); lst_d = dram_in(nc, "s5_lst", [128, 8])
    p1_d = dram_in(nc, "s5_p1", [128, 8, 16]); p2_d = dram_in(nc, "s5_p2", [128, 8, 16])
    cx_d = dram_in(nc, "s5_cx", [128, 4, 16]); cy_d = dram_in(nc, "s5_cy", [128, 4, 16])
    dsk_d = dram_in(nc, "s5_dsk", [128, 4])
    sgn_d = dram_in(nc, "s5_sgn", [128, 1])
    expo_d = dram_in(nc, "s5_expo", [128, 2, NE])
    idf_d = dram_in(nc, "identf", [128, 128]); jsw_d = dram_in(nc, "s5_jsw", [128, 128])
    msk_d = dram_in(nc, "s5_msk", [128, 2, 128])
    sel_d = dram_in(nc, "s5_sel", [64, 4, 8, 128], BF16)
    y8_d = dram_out(nc, "s5_y8", [4, 128, 2048], F32)

    def load(name, shape, src, dt=F32, eng="sp"):
        t = es.enter_context(SB(nc, name, shape, dt)); b = Buf(name)
        S.op(eng, lambda q: q.dma_start(out=t[:], in_=src), writes=[b], dma=b)
        return t, b

    def alloc(name, shape, dt=F32):
        return es.enter_context(SB(nc, name, shape, dt)), Buf(name)
    uT, uTB = load("s5uT", [64, 16384], uT_d, BF16)
    are, areB = load("s5are", [128, 8], are_d); aim, aimB = load("s5aim", [128, 8], aim_d); lst, lstB = load("s5lst", [128, 8], lst_d)
    p1, p1B = load("s5p1", [128, 8, 16], p1_d); p2, p2B = load("s5p2", [128, 8, 16], p2_d)
    cx, cxB = load("s5cx", [128, 4, 16], cx_d); cy, cyB = load("s5cy", [128, 4, 16], cy_d)
    dsk, dskB = load("s5dsk", [128, 4], dsk_d); sgn, sgnB = load("s5sgn", [128, 1], sgn_d)
    expo, expoB = load("s5expo", [128, 2, NE], expo_d)
    idf, idfB = load("s5idf", [128, 128], idf_d); jsw, jswB = load("s5jsw", [128, 128], jsw_d)
    msk, mskB = load("s5msk", [128, 2, 128], msk_d)
    sel, selB = load("s5sel", [64, 4, 8, 128], sel_d, BF16)

    V = lambda fn, r, w: S.op("dve", fn, reads=r, writes=w)
    Pq = lambda fn, r, w: S.op("pool", fn, reads=r, writes=w)
    ACT = lambda fn, r, w: S.op("act", fn, reads=r, writes=w)

    sc, scB = alloc("s5sc", [128, 12, 8])
    DT, RHO, TH, NR, NI, L2, CR, CI, T0, T1, NSG, T2 = range(12)
    ACT(lambda a: a.activation(out=sc[:, DT], in_=lst[:], func=AF.Exp), [lstB], [scB])
    V(lambda v: v.tensor_tensor(out=sc[:, RHO], in0=are[:], in1=sc[:, DT], op=ALU.mult), [areB, scB], [scB])
    V(lambda v: v.tensor_tensor(out=sc[:, TH], in0=aim[:], in1=sc[:, DT], op=ALU.mult), [aimB, scB], [scB])
    V(lambda v: v.tensor_scalar(out=sc[:, NSG, 0:1], in0=sgn[:], scalar1=-1.0, scalar2=None, op0=ALU.mult), [sgnB], [scB])
    nsg = sc[:, NSG, 0:1]
    pw, pwB = alloc("s5pw", [128, 2, 2, 8, NE])
    ph, phB = alloc("s5ph", [128, 8, NE])
    mg, mgB = alloc("s5mg", [128, 8, NE])
    phi, phiB = alloc("s5phi", [128, 8, NE], mybir.dt.int32)
    phf, phfB = alloc("s5phf", [128, 8, NE])
    for o in range(2):
        ex = expo[:, o, :].unsqueeze(1).to_broadcast([128, 8, NE])
        V(lambda v, ex=ex: v.tensor_tensor(out=mg[:], in0=ex, in1=sc[:, RHO].unsqueeze(2).to_broadcast([128, 8, NE]), op=ALU.mult),
          [expoB, scB], [mgB])
        ACT(lambda a: a.activation(out=mg[:], in_=mg[:], func=AF.Exp), [mgB], [mgB])
        for ri, shift in ((1, 0.0), (0, 0.25)):
            V(lambda v, ex=ex: v.tensor_tensor(out=ph[:], in0=ex, in1=sc[:, TH].unsqueeze(2).to_broadcast([128, 8, NE]), op=ALU.mult),
              [expoB, scB], [phB])
            V(lambda v, shift=shift: v.tensor_scalar(out=ph[:], in0=ph[:], scalar1=1.0 / TWO_PI, scalar2=shift, op0=ALU.mult,
                                                     op1=ALU.add), [phB], [phB])
            V(lambda v: v.tensor_copy(out=phi[:], in_=ph[:]), [phB], [phiB])
            V(lambda v: v.tensor_copy(out=phf[:], in_=phi[:]), [phiB], [phfB])
            V(lambda v: v.tensor_tensor(out=ph[:], in0=ph[:], in1=phf[:], op=ALU.subtract), [phfB], [phB])
            ACT(lambda a: a.activation(out=ph[:], in_=ph[:], func=AF.Sin, scale=6.283185), [phB], [phB])
            V(lambda v, o=o, ri=ri: v.tensor_tensor(out=pw[:, o, ri], in0=ph[:], in1=mg[:], op=ALU.mult), [phB, mgB], [pwB])
    i1 = 1 + 7
    V(lambda v: v.tensor_scalar(out=sc[:, NR], in0=pw[:, 0, 0, :, i1], scalar1=-1.0, scalar2=None, op0=ALU.add), [pwB], [scB])
    V(lambda v: v.tensor_copy(out=sc[:, NI], in_=pw[:, 0, 1, :, i1]), [pwB], [scB])
    V(lambda v: v.tensor_tensor(out=sc[:, T0], in0=are[:], in1=are[:], op=ALU.mult), [areB], [scB])
    V(lambda v: v.tensor_tensor(out=sc[:, T1], in0=aim[:], in1=aim[:], op=ALU.mult), [aimB], [scB])
    V(lambda v: v.tensor_tensor(out=sc[:, L2], in0=sc[:, T0], in1=sc[:, T1], op=ALU.add), [scB], [scB])
    V(lambda v: v.reciprocal(out=sc[:, L2], in_=sc[:, L2]), [scB], [scB])
    V(lambda v: v.tensor_tensor(out=sc[:, T0], in0=sc[:, NR], in1=are[:], op=ALU.mult), [scB, areB], [scB])
    V(lambda v: v.tensor_tensor(out=sc[:, T1], in0=sc[:, NI], in1=aim[:], op=ALU.mult), [scB, aimB], [scB])
    V(lambda v: v.tensor_tensor(out=sc[:, T2], in0=sc[:, T0], in1=sc[:, T1], op=ALU.add), [scB], [scB])
    V(lambda v: v.tensor_tensor(out=sc[:, CR], in0=sc[:, T2], in1=sc[:, L2], op=ALU.mult), [scB], [scB])
    V(lambda v: v.tensor_tensor(out=sc[:, T0], in0=sc[:, NI], in1=are[:], op=ALU.mult), [scB, areB], [scB])
    V(lambda v: v.tensor_tensor(out=sc[:, T1], in0=sc[:, NR], in1=aim[:], op=ALU.mult), [scB, aimB], [scB])
    V(lambda v: v.tensor_tensor(out=sc[:, T2], in0=sc[:, T0], in1=sc[:, T1], op=ALU.subtract), [scB], [scB])
    V(lambda v: v.tensor_tensor(out=sc[:, CI], in0=sc[:, T2], in1=sc[:, L2], op=ALU.mult), [scB], [scB])
    bxy, bxyB = alloc("s5bxy", [128, 2, 8, 16])
    cxy, cxyB = alloc("s5cxy", [128, 2, 4, 16])
    tb, tbB = alloc("s5tb", [128, 3, 8, 16])
    V(lambda v: v.tensor_scalar(out=tb[:, 2], in0=p2[:], scalar1=sgn[:, 0:1], scalar2=None, op0=ALU.mult), [p2B, sgnB], [tbB])
    crb = sc[:, CR].unsqueeze(2).to_broadcast([128, 8, 16]); cib = sc[:, CI].unsqueeze(2).to_broadcast([128, 8, 16])
    V(lambda v: v.tensor_tensor(out=tb[:, 0], in0=p1[:], in1=crb, op=ALU.mult), [p1B, scB], [tbB])
    V(lambda v: v.tensor_tensor(out=tb[:, 1], in0=tb[:, 2], in1=cib, op=ALU.mult), [tbB, scB], [tbB])
    V(lambda v: v.tensor_tensor(out=bxy[:, 0], in0=tb[:, 0], in1=tb[:, 1], op=ALU.add), [tbB], [bxyB])
    V(lambda v: v.tensor_tensor(out=tb[:, 0], in0=tb[:, 2], in1=crb, op=ALU.mult), [tbB, scB], [tbB])
    V(lambda v: v.tensor_tensor(out=tb[:, 1], in0=p1[:], in1=cib, op=ALU.mult), [p1B, scB], [tbB])
    V(lambda v: v.tensor_tensor(out=bxy[:, 1], in0=tb[:, 0], in1=tb[:, 1], op=ALU.subtract), [tbB], [bxyB])
    V(lambda v: v.tensor_scalar(out=cxy[:, 0], in0=cx[:], scalar1=nsg, scalar2=None, op0=ALU.mult), [cxB, scB], [cxyB])
    V(lambda v: v.tensor_scalar(out=cxy[:, 1], in0=cy[:], scalar1=-1.0, scalar2=None, op0=ALU.mult), [cyB], [cxyB])
    cab, cabB = alloc("s5cab", [128, 2, 7, 8])
    V(lambda v: v.tensor_copy(out=cab[:, 0, 0], in_=pw[:, 0, 0, :, 128 + 7]), [pwB], [cabB])
    V(lambda v: v.tensor_scalar(out=cab[:, 1, 0], in0=pw[:, 0, 1, :, 128 + 7], scalar1=nsg, scalar2=None, op0=ALU.mult), [pwB, scB], [cabB])
    for i in range(1, 7):
        V(lambda v, i=i: v.tensor_tensor(out=sc[:, T0], in0=cab[:, 0, i - 1], in1=cab[:, 0, i - 1], op=ALU.mult), [cabB], [scB])
        V(lambda v, i=i: v.tensor_tensor(out=sc[:, T1], in0=cab[:, 1, i - 1], in1=cab[:, 1, i - 1], op=ALU.mult), [cabB], [scB])
        V(lambda v, i=i: v.tensor_tensor(out=cab[:, 0, i], in0=sc[:, T0], in1=sc[:, T1], op=ALU.subtract), [scB], [cabB])
        V(lambda v, i=i: v.tensor_tensor(out=sc[:, T2], in0=cab[:, 0, i - 1], in1=cab[:, 1, i - 1], op=ALU.mult), [cabB], [scB])
        V(lambda v, i=i: v.tensor_scalar(out=cab[:, 1, i], in0=sc[:, T2], scalar1=2.0, scalar2=None, op0=ALU.mult), [scB], [cabB])

    tabR = Pool(nc, es, "s5tabR", [128, NE, 16], F32, 2)
    tabL = Pool(nc, es, "s5tabL", [128, NE, 16], F32, 2)
    tmpT = Pool(nc, es, "s5tmpT", [128, NE, 16], F32, 2)
    u8p = Pool(nc, es, "s5u8", [128, 16 + 4096 + 16], BF16, 2)
    lagp = Pool(nc, es, "s5lag", [128, 31, 128], BF16, 2)
    bmp = Pool(nc, es, "s5bm", [128, 2, 16, 128], BF16, 2)
    y8p = Pool(nc, es, "s5y8", [128, 2048], F32, 2)
    zp = Pool(nc, es, "s5z", [128, 128], F32, 6)
    xp = Pool(nc, es, "s5x", [128, 2, 128], F32, 2)
    mtp = Pool(nc, es, "s5mt", [128, 128], F32, 3)
    f0p = Pool(nc, es, "s5f0", [128, 2, 128], F32, 2)
    psB_ = Pool(nc, es, "s5psB", [128, 512], F32, 3, psum=True)
    psS_ = Pool(nc, es, "s5psS", [128, 128], F32, 4, psum=True)
    fin = []
    for gl in range(4):
        gdf, gdb = gl * 2, gl * 2 + 1
        def gen(pool, o, gd, xy, which):
            t, tB_ = pool.next()
            t1, t1B = tmpT.next()
            X = xy[:, 0, which].unsqueeze(1).to_broadcast([128, NE, 16]); Y = xy[:, 1, which].unsqueeze(1).to_broadcast([128, NE, 16])
            xyB = bxyB if xy is bxy else cxyB
            V(lambda v: v.tensor_tensor(out=t[:], in0=pw[:, o, 0, gd].unsqueeze(2).to_broadcast([128, NE, 16]), in1=X, op=ALU.mult),
              [pwB, xyB], [tB_])
            Pq(lambda g: g.tensor_tensor(out=t1[:], in0=pw[:, o, 1, gd].unsqueeze(2).to_broadcast([128, NE, 16]), in1=Y, op=ALU.mult),
               [pwB, xyB], [t1B])
            V(lambda v: v.tensor_tensor(out=t[:], in0=t[:], in1=t1[:], op=ALU.add), [t1B], [tB_])
            return t, tB_
        RfA, RfAB = gen(tabR, 0, gdf, cxy, gl)
        RbD, RbDB = gen(tabR, 1, gdb, cxy, gl)
        LfD, LfDB = gen(tabL, 1, gdf, bxy, gdf)
        LbA, LbAB = gen(tabL, 0, gdb, bxy, gdb)

        def blk(t, j0):
            return t[:, j0:j0 + 8, :].rearrange("p x c -> p (x c)")
        Lf = blk(LfD, 128)
        Lb = blk(LbA, 7)
        u8, u8B = u8p.next()
        V(lambda v: v.memset(u8[:], 0.0), [], [u8B])
        for ct in range(4):
            ps, psB = psB_.next()

            def mm(pe, ps=ps, ct=ct):
                for s in range(8):
                    b0 = ct * 4096 + s
                    i = pe.matmul(ps[:], lhsT=sel[:, gl, s, :], rhs=uT[:, b0:b0 + 8 * 511 + 1:8], start=(s == 0), stop=(s == 7))
                return i
            S.op("pe", mm, reads=[selB, uTB], writes=[psB])
            ACT(lambda a, ps=ps, ct=ct: a.activation(
                out=u8[:, 16 + ct * 1024:16 + (ct + 1) * 1024].rearrange("p (k m) -> p k m", m=32)[:, :, 0:16],
                in_=ps[:].rearrange("p (k m) -> p k m", m=16), func=AF.Copy), [psB], [u8B])
        lag, lagB = lagp.next()
        f0, f0B = f0p.next()
        for dl in range(16):
            ps, psB = psS_.next()
            S.op("pe", lambda pe, ps=ps, dl=dl: pe.matmul(ps[:], lhsT=Lf, rhs=blk(RfA, 8 * dl + 7), start=True, stop=True),
                 reads=[LfDB, RfAB], writes=[psB])
            if dl == 0:
                V(lambda v, ps=ps: v.tensor_tensor(out=f0[:, 0], in0=ps[:], in1=msk[:, 0], op=ALU.mult), [psB, mskB], [f0B])
            else:
                ACT(lambda a, ps=ps, dl=dl: a.activation(out=lag[:, dl - 1, :], in_=ps[:], func=AF.Copy), [psB], [lagB])
            ps, psB = psS_.next()
            S.op("pe", lambda pe, ps=ps, dl=dl: pe.matmul(ps[:], lhsT=Lb, rhs=blk(RbD, 128 - 8 * dl), start=True, stop=True),
                 reads=[LbAB, RbDB], writes=[psB])
            if dl == 0:
                V(lambda v, ps=ps: v.tensor_tensor(out=f0[:, 1], in0=ps[:], in1=msk[:, 1], op=ALU.mult), [psB, mskB], [f0B])
                V(lambda v: v.tensor_tensor(out=f0[:, 0], in0=f0[:, 0], in1=f0[:, 1], op=ALU.add), [], [f0B])
                V(lambda v: v.scalar_tensor_tensor(out=lag[:, 30, :], in0=idf[:], scalar=dsk[:, gl:gl + 1], in1=f0[:, 0],
                                                   op0=ALU.mult, op1=ALU.add), [idfB, dskB, f0B], [lagB])
            else:
                ACT(lambda a, ps=ps, dl=dl: a.activation(out=lag[:, 14 + dl, :], in_=ps[:], func=AF.Copy), [psB], [lagB])
        bm, bmB = bmp.next()
        for m in range(16):
            for d_, (tbl, tblB, j0) in enumerate(((LfD, LfDB, 1 + 8 * m), (LbA, LbAB, 7 + 8 * m))):
                ps, psB = psS_.next()
                S.op("pe", lambda pe, ps=ps, tbl=tbl, j0=j0: pe.matmul(ps[:], lhsT=blk(tbl, j0), rhs=idf[:], start=True, stop=True),
                     reads=[tblB, idfB], writes=[psB])
                ACT(lambda a, ps=ps, d_=d_, m=m: a.activation(out=bm[:, d_, m, :], in_=ps[:], func=AF.Copy), [psB], [bmB])
        y8, y8B = y8p.next()
        for xt in range(8):
            ps, psB = psB_.next()
            x0 = 16 + xt * 512

            def mm(pe, ps=ps, x0=x0):
                pe.matmul(ps[:], lhsT=lag[:, 30, :], rhs=u8[:, x0:x0 + 512], start=True, stop=False)
                for dl in range(1, 16):
                    pe.matmul(ps[:], lhsT=lag[:, dl - 1, :], rhs=u8[:, x0 - dl:x0 - dl + 512], start=False, stop=False)
                for dl in range(1, 16):
                    i = pe.matmul(ps[:], lhsT=lag[:, 14 + dl, :], rhs=u8[:, x0 + dl:x0 + dl + 512], start=False, stop=(dl == 15))
                return i
            S.op("pe", mm, reads=[lagB, u8B], writes=[psB])
            ACT(lambda a, ps=ps, xt=xt: a.activation(
                out=y8[:, xt * 256:(xt + 1) * 256].rearrange("p (k m) -> p k m", m=16),
                in_=ps[:].rearrange("p (k m) -> p k m", m=32)[:, :, 0:16], func=AF.Copy), [psB], [y8B])
        xs, xsB = xp.next()
        for d_ in range(2):
            gd = gl * 2 + d_
            ps, psB = psS_.next()

            def mm(pe, ps=ps, d_=d_):
                for m in range(16):
                    i = pe.matmul(ps[:], lhsT=bm[:, d_, m, :], rhs=u8[:, 16 + m:16 + m + 32 * 127 + 1:32], start=(m == 0), stop=(m == 15))
                return i
            S.op("pe", mm, reads=[bmB, u8B], writes=[psB])
            z, zB = zp.next()
            ACT(lambda a, ps=ps, z=z: a.activation(out=z[:], in_=ps[:], func=AF.Copy), [psB], [zB])
            for i in range(7):
                sh = 1 << i
                mt, mtB = mtp.next()
                V(lambda v, mt=mt, i=i, gd=gd: v.tensor_scalar(out=mt[:], in0=idf[:], scalar1=cab[:, 0, i, gd:gd + 1], scalar2=None,
                                                              op0=ALU.mult), [idfB, cabB], [mtB])
                V(lambda v, mt=mt, i=i, gd=gd: v.scalar_tensor_tensor(out=mt[:], in0=jsw[:], scalar=cab[:, 1, i, gd:gd + 1], in1=mt[:],
                                                                     op0=ALU.mult, op1=ALU.add), [jswB, cabB], [mtB])
                ps2, ps2B = psS_.next()
                zn, znB = zp.next()
                if d_ == 0:
                    S.op("pe", lambda pe, ps2=ps2, mt=mt, z=z, sh=sh: pe.matmul(ps2[:, sh:128], lhsT=mt[:], rhs=z[:, 0:128 - sh],
                                                                             start=True, stop=True), reads=[mtB, zB], writes=[ps2B])
                    V(lambda v, zn=zn, z=z, ps2=ps2, sh=sh: v.tensor_tensor(out=zn[:, sh:128], in0=z[:, sh:128], in1=ps2[:, sh:128],
                                                                         op=ALU.add), [zB, ps2B], [znB])
                    Pq(lambda g, zn=zn, z=z, sh=sh: g.tensor_copy(out=zn[:, 0:sh], in_=z[:, 0:sh]), [zB], [znB])
                else:
                    S.op("pe", lambda pe, ps2=ps2, mt=mt, z=z, sh=sh: pe.matmul(ps2[:, 0:128 - sh], lhsT=mt[:], rhs=z[:, sh:128],
                                                                             start=True, stop=True), reads=[mtB, zB], writes=[ps2B])
                    V(lambda v, zn=zn, z=z, ps2=ps2, sh=sh: v.tensor_tensor(out=zn[:, 0:128 - sh], in0=z[:, 0:128 - sh],
                                                                         in1=ps2[:, 0:128 - sh], op=ALU.add), [zB, ps2B], [znB])
                    Pq(lambda g, zn=zn, z=z, sh=sh: g.tensor_copy(out=zn[:, 128 - sh:128], in_=z[:, 128 - sh:128]), [zB], [znB])
                z, zB = zn, znB
            if d_ == 0:
                V(lambda v, z=z: v.tensor_copy(out=xs[:, 0, 1:128], in_=z[:, 0:127]), [zB], [xsB])
                V(lambda v: v.memset(xs[:, 0, 0:1], 0.0), [], [xsB])
            else:
                V(lambda v, z=z: v.tensor_copy(out=xs[:, 1, 0:127], in_=z[:, 1:128]), [zB], [xsB])
                V(lambda v: v.memset(xs[:, 1, 127:128], 0.0), [], [xsB])
        for m in range(16):
            ps, psB = psS_.next()

            def mm(pe, ps=ps, m=m):
                pe.matmul(ps[:], lhsT=blk(RfA, 8 * m + 1 + 7), rhs=xs[:, 0, :], start=True, stop=False)
                return pe.matmul(ps[:], lhsT=blk(RbD, 8 * m), rhs=xs[:, 1, :], start=False, stop=True)
            S.op("pe", mm, reads=[RfAB, RbDB, xsB], writes=[psB])
            V(lambda v, ps=ps, m=m: v.tensor_tensor(out=y8[:, m:m + 16 * 127 + 1:16], in0=y8[:, m:m + 16 * 127 + 1:16], in1=ps[:],
                                                  op=ALU.add), [psB], [y8B])
        S.op("sp", lambda q, y8=y8, gl=gl: q.dma_start(out=y8_d[gl], in_=y8[:]), reads=[y8B], dma=y8B)
        fin.append(y8B)
    return fin


def body_s5(nc, S):
    with ExitStack() as es:
        c = setup_lb(nc, es, S)
        fin = build_s5(c, nc, es, S)
        S.finish(fin, "sp")
        S.barrier()


def build_LB_s5():
    nc = bass.Bass("TRN2", target_bir_lowering=False)
    with ExitStack() as es0:
        S = Sched(nc, es0)
        body_s5(nc, S)
        print("LBs5 sems", S.nsem, "waits", S.nwait, "cnt", S.cnt)
    return nc


def body_LC(nc, S):
    x1 = dram_in(nc, "x", [NTOK, D])
    yA = dram_in(nc, "yA", [NTOK, 512]); yB = dram_in(nc, "yB", [NTOK, 512]); yR = dram_in(nc, "yR", [NTOK, 512])
    yS = dram_in(nc, "yS", [NTOK, 512])
    wglu_d = dram_in(nc, "wglu", [512, 512]); bglu_d = dram_in(nc, "bglu", [128, 512])
    og_d = dram_in(nc, "ogain", [128, 2048])
    wout = dram_in(nc, "wout", [D, D])
    g0 = dram_in(nc, "g0", [128, 16])
    wg = dram_in(nc, "wg", [D, DFF]); wu = dram_in(nc, "wu", [D, DFF]); wd = dram_in(nc, "wd", [DFF, D])
    xo = dram_out(nc, "x1", [NTOK, D])
    with ExitStack() as es:
        c = setup_common(nc, es, S)
        W = make_wpools(nc, es)
        gcol = es.enter_context(SB(nc, "gcol", [128, 16], F32)); gB = Buf("gcol")
        S.op("sp", lambda q: q.dma_start(out=gcol[:], in_=g0), writes=[gB], dma=gB)
        og = es.enter_context(SB(nc, "og", [128, 2048], F32)); ogB = Buf("og")
        S.op("sp", lambda q: q.dma_start(out=og[:], in_=og_d), writes=[ogB], dma=ogB)
        bglu = es.enter_context(SB(nc, "bglu_sb", [128, 512], F32)); bgB = Buf("bglu")
        S.op("sp", lambda q: q.dma_start(out=bglu[:], in_=bglu_d), writes=[bgB], dma=bgB)
        wglu = es.enter_context(SB(nc, "wglu_sb", [128, 4, 512], BF16)); wgluB = Buf("wglu")
        S.op("pool", lambda q: q.dma_start(out=wglu[:], in_=wglu_d.rearrange("(kc p) n -> p kc n", p=128)), writes=[wgluB], dma=wgluB)
        hT = es.enter_context(SB(nc, "hT", [128, 16, 512], BF16)); hTB = Buf("hT")
        xt = []; xtB = []
        for tt in range(4):
            xt.append(es.enter_context(SB(nc, "xt%d" % tt, [128, 2048], F32))); xtB.append(Buf("xt%d" % tt))
        y4p = Pool(nc, es, "y4", [128, 4, 512], F32, 1)
        t5 = Pool(nc, es, "lct", [128, 512], F32, 3)
        zbp = Pool(nc, es, "lczb", [128, 512], BF16, 2)
        zTp = Pool(nc, es, "lczT", [128, 4, 128], BF16, 2)
        for st in range(NTOK // 512):
            for tt in range(4):
                r0 = st * 512 + tt * 128
                S.op("sp", lambda q, tt=tt, r0=r0: q.dma_start(out=xt[tt][:], in_=x1[r0:r0 + 128, :]), writes=[xtB[tt]], dma=xtB[tt])
                y4, y4B = y4p.next()
                for i, src in enumerate((yA, yB, yR, yS)):
                    S.op("sp", lambda q, y4=y4, i=i, src=src, r0=r0: q.dma_start(out=y4[:, i, :], in_=src[r0:r0 + 128, :]), writes=[y4B], dma=y4B)
                ys = y4[:, 3, :]
                a, aB = t5.next(); b, bB = t5.next()
                S.op("act", lambda A_, a=a, ys=ys: A_.activation(out=a[:], in_=ys, func=AF.Square), reads=[y4B], writes=[aB])
                S.op("dve", lambda v, a=a: v.tensor_scalar(out=a[:], in0=a[:], scalar1=0.044715, scalar2=1.0, op0=ALU.mult, op1=ALU.add),
                     reads=[], writes=[aB])
                S.op("dve", lambda v, a=a, ys=ys: v.tensor_tensor(out=a[:], in0=a[:], in1=ys, op=ALU.mult), reads=[y4B], writes=[aB])
                S.op("act", lambda A_, a=a: A_.activation(out=a[:], in_=a[:], func=AF.Sigmoid, scale=1.5957691216057308), reads=[], writes=[aB])
                S.op("dve", lambda v, a=a, ys=ys: v.tensor_tensor(out=a[:], in0=a[:], in1=ys, op=ALU.mult), reads=[y4B], writes=[aB])
                zb, zbB = zbp.next()
                S.op("act", lambda A_, a=a, zb=zb: A_.activation(out=zb[:], in_=a[:], func=AF.Copy), reads=[aB], writes=[zbB])
                pt, pB = c.psT.next()

                def tr(pe, pt=pt, zb=zb):
                    for j in range(4):
                        i = pe.transpose(out=pt[:, j, :], in_=zb[:, j * 128:(j + 1) * 128], identity=c.ident[:])
                    return i
                S.op("pe", tr, reads=[zbB, c.identB], writes=[pB])
                zT, zTB = zTp.next()
                S.op("act", lambda A_, pt=pt, zT=zT: A_.activation(out=zT[:], in_=pt[:, 0:4, :], func=AF.Copy), reads=[pB], writes=[zTB])
                po, poB = c.psO.next()

                def mmg(pe, po=po, zT=zT):
                    for kc in range(4):
                        i = pe.matmul(po[:], lhsT=zT[:, kc, :], rhs=wglu[:, kc, :], start=(kc == 0), stop=(kc == 3))
                    return i
                S.op("pe", mmg, reads=[zTB, wgluB], writes=[poB])
                S.op("dve", lambda v, b=b, po=po: v.tensor_tensor(out=b[:], in0=po[:], in1=bglu[:], op=ALU.add), reads=[poB, bgB], writes=[bB])
                S.op("act", lambda A_, b=b: A_.activation(out=b[:], in_=b[:], func=AF.Sigmoid), reads=[], writes=[bB])
                S.op("dve", lambda v, a=a, b=b, y4=y4: v.tensor_tensor(out=y4[:, 3, :], in0=a[:], in1=b[:], op=ALU.mult), reads=[aB, bB], writes=[y4B])
                sq, sqB = t5.next()
                st_, sB = c.stat.next()
                for i in range(4):
                    S.op("act", lambda A_, sq=sq, y4=y4, i=i, st_=st_: A_.activation(out=sq[:], in_=y4[:, i, :], func=AF.Square,
                                                                                     accum_out=st_[:, i:i + 1]), reads=[y4B], writes=[sqB, sB])
                S.op("dve", lambda v, st_=st_: v.tensor_scalar(out=st_[:, 4:8], in0=st_[:, 0:4], scalar1=1.0 / 512, scalar2=EPS,
                                                            op0=ALU.mult, op1=ALU.add), reads=[], writes=[sB])
                S.op("act", lambda A_, st_=st_: A_.activation(out=st_[:, 8:12], in_=st_[:, 4:8], func=AF.Sqrt), reads=[], writes=[sB])
                S.op("dve", lambda v, st_=st_: v.reciprocal(out=st_[:, 12:16], in_=st_[:, 8:12]), reads=[], writes=[sB])
                S.op("dve", lambda v, y4=y4, st_=st_: v.tensor_tensor(out=y4[:], in0=y4[:], in1=st_[:, 12:16].unsqueeze(2).to_broadcast([128, 4, 512]),
                                                                   op=ALU.mult), reads=[sB], writes=[y4B])
                xn, xnB = c.xn.next()
                S.op("pool", lambda g_, y4=y4, xn=xn: g_.tensor_tensor(out=xn[:], in0=y4[:].rearrange("p g e -> p (g e)"), in1=og[:], op=ALU.mult),
                     reads=[y4B, ogB], writes=[xnB])
                for half in range(2):
                    pt, pB = c.psT.next()

                    def tr2(pe, pt=pt, half=half, xn=xn):
                        for j in range(8):
                            kc = half * 8 + j
                            i = pe.transpose(out=pt[:, j, :], in_=xn[:, kc * 128:(kc + 1) * 128], identity=c.ident[:])
                        return i
                    S.op("pe", tr2, reads=[xnB, c.identB], writes=[pB])
                    S.op("act", lambda A_, pt=pt, half=half, tt=tt: A_.activation(
                        out=hT[:, half * 8:(half + 1) * 8, tt * 128:(tt + 1) * 128], in_=pt[:], func=AF.Copy), reads=[pB], writes=[hTB])
            for dg in range(4):
                ws, wB = W.wgu.next()
                S.op("pool", lambda q, ws=ws, dg=dg: q.dma_start(
                    out=ws[:], in_=wout[:, dg * 512:(dg + 1) * 512].rearrange("(kc p) n -> p kc n", p=128)), writes=[wB], dma=wB)
                for tt in range(4):
                    po, poB = c.psO.next()

                    def mm(pe, po=po, ws=ws, tt=tt):
                        for kc in range(16):
                            i = pe.matmul(po[:], lhsT=hT[:, kc, tt * 128:(tt + 1) * 128], rhs=ws[:, kc, :], start=(kc == 0), stop=(kc == 15))
                        return i
                    S.op("pe", mm, reads=[wB, hTB], writes=[poB])
                    S.op("dve", lambda v, po=po, tt=tt, dg=dg: v.tensor_tensor(
                        out=xt[tt][:, dg * 512:(dg + 1) * 512], in0=xt[tt][:, dg * 512:(dg + 1) * 512], in1=po[:], op=ALU.add),
                        reads=[poB], writes=[xtB[tt]])
            for tt in range(4):
                rms_to_hT(c, xt[tt][:], xtB[tt], gcol[:], gB, hT, hTB, tt * 128)
            ffn_supertile(c, xt, xtB, hT, hTB, wg, wu, wd, W)
            for tt in range(4):
                r0 = st * 512 + tt * 128
                S.op("sp", lambda q, tt=tt, r0=r0: q.dma_start(out=xo[r0:r0 + 128, :], in_=xt[tt][:]), reads=[xtB[tt]], dma=xtB[tt])
        S.finish(list(xtB), "sp")
        S.barrier()


def build_LC():
    nc = bass.Bass("TRN2", target_bir_lowering=False)
    with ExitStack() as es0:
        S = Sched(nc, es0)
        body_LC(nc, S)
        print("LC sems", S.nsem, "waits", S.nwait, "cnt", S.cnt)
    return nc


BF = ml_dtypes.bfloat16

def gcol(g):
    return np.ascontiguousarray(g.reshape(16, 128).T).astype(np.float32)

def rot_table(core):
    pos = (np.arange(2048) + core * 2048).astype(np.float32)
    half = 32
    inv = (np.float32(10000.0) ** (-(np.arange(half, dtype=np.float32) / np.float32(half)))).astype(np.float32)
    ang = (pos[:, None] * inv[None, :]).astype(np.float32).astype(np.float64)
    cs = np.stack([np.cos(ang), np.sin(ang)], axis=1)
    t = np.repeat(cs[:, :, None, :], 8, axis=2)
    t[:, :, 4:8, :] *= 0.125
    return t.astype(np.float32)

def la_inputs(inp, l, core, which=0, inproj=True):
    x = inp["x"][0, core * 2048:(core + 1) * 2048]
    d = {"x": x, "g0": gcol(inp["norm_gain"][l, 0 if which == 0 else 2]),
         "wg": inp["ffn_w_gate"][l, which], "wu": inp["ffn_w_up"][l, which], "wd": inp["ffn_w_down"][l, which],
         "ident": np.eye(128, dtype=BF)}
    if inproj:
        d["g1"] = gcol(inp["norm_gain"][l, 1])
        d["win"] = inp["w_in"][l]
        d["gqk"] = np.ascontiguousarray(np.broadcast_to(inp["qk_gain"][l].reshape(1, 4, 64), (128, 4, 64))).astype(np.float32)
        d["rot"] = rot_table(core)
    return d

NEG = -30000.0

def t5_bucket_np(rel):
    n = np.abs(rel); nf = np.maximum(n, 1).astype(np.float32)
    large = 8 + (np.log(nf / np.float32(8)) / np.float32(math.log(256)) * np.float32(8)).astype(np.int32)
    large = np.minimum(large, 15)
    return np.where(rel > 0, 16, 0) + np.where(n < 8, n, large)

def biasB_tables(t5):
    out = np.empty((3, 8, 128, 2, 128), np.float32)
    kk = np.arange(128)[:, None, None]; m = np.arange(2)[None, :, None]; i = np.arange(128)[None, None, :]
    rel = (128 * m + kk - 64) - i
    ok = np.abs(rel) <= 64
    for bi, d in enumerate((1, 4, 16)):
        b = t5_bucket_np(rel * d)
        tab = t5[b]
        out[bi] = np.where(ok[None], np.moveaxis(tab, -1, 0), NEG)
    return out

def biasA_tables(rpb, core):
    out = np.empty((8, 5, 128, 8, 128), np.float32)
    p = np.arange(128); kkr = p // 64; kc = p % 64
    q = np.arange(128); qr = q // 64; qc = q % 64
    cs = np.clip(qc - 8, 0, 48)
    colok = (kc[:, None] >= cs[None, :]) & (kc[:, None] < cs[None, :] + 16)
    dc = np.clip(kc[:, None] - qc[None, :], -15, 15) + 15
    for ti, gb in enumerate((8, 16 * core, 16 * core + 1, 16 * core + 14, 16 * core + 15)):
        r = 2 * gb + qr
        R0 = np.clip(r - 4, 0, 248)
        for j in range(8):
            kr = 2 * gb - 7 + 2 * j + kkr
            rowok = (kr[:, None] >= R0[None, :]) & (kr[:, None] < R0[None, :] + 8) & (kr[:, None] >= 0) & (kr[:, None] <= 255)
            dr = np.clip(kr[:, None] - r[None, :] + 7, 0, 14)
            ok = rowok & colok
            out[:, ti, :, j, :] = np.where(ok[None], rpb[:, dr, dc], NEG)
    return out

def halo(arr_full, lo, hi, axis):
    N = arr_full.shape[axis]
    shp = list(arr_full.shape); shp[axis] = hi - lo
    out = np.zeros(shp, arr_full.dtype)
    a = max(lo, 0); b = min(hi, N)
    src = [slice(None)] * arr_full.ndim; dst = [slice(None)] * arr_full.ndim
    src[axis] = slice(a, b); dst[axis] = slice(a - lo, b - lo)
    out[tuple(dst)] = arr_full[tuple(src)]
    return out

def attn_inputs(G, inp, l, core):
    t0 = core * 2048
    d = {"identf": np.eye(128, dtype=np.float32)}
    d["qTA"] = np.ascontiguousarray(G["qTA"][:, t0:t0 + 2048])
    lo = t0 - 7 * 64; hi = lo + 3072
    d["kTAh"] = halo(G["kTA"], lo, hi, 1); d["vAh"] = halo(G["vA"], lo, hi, 0)
    ones = np.ones((16384, 512), BF)
    d["valA"] = halo(ones, lo, hi, 0)
    d["biasA"] = biasA_tables(inp["na_rpb"][l], core)
    d["qTB"] = np.ascontiguousarray(G["qTB"][:, t0:t0 + 2048])
    lo = t0 - 1024; hi = lo + 4096
    d["kTBh"] = halo(G["kTB"], lo, hi, 1); d["vBh"] = halo(G["vB"], lo, hi + 16, 0)
    d["valB"] = halo(ones, lo, hi + 16, 0)
    d["biasB"] = biasB_tables(inp["t5_bias"])
    return d

BIGE = 1.0e7

def ret_consts(inp, l, core):
    d = {}
    d["ret_dl"] = np.ascontiguousarray(np.broadcast_to(inp["ret_decay_logit"][l].reshape(1, 8), (128, 8))).astype(np.float32)
    p = np.arange(128)[:, None]; t = np.arange(16)[None, :]
    tl = t * 128 + p
    d["ret_esf"] = np.stack([2047 - tl, tl], axis=2).astype(np.float32)
    nc_ = np.full((128, 8, 2), BIGE, np.float32)
    for c2 in range(8):
        if c2 < core: nc_[:, c2, 0] = 2048.0 * (core - 1 - c2)
        if c2 > core: nc_[:, c2, 1] = 2048.0 * (c2 - core - 1)
    d["ret_ncoef"] = nc_
    j = np.arange(128)
    d["ret_ekd"] = np.stack([127 - j, j], axis=1).astype(np.float32)
    d["ret_eqd"] = np.ascontiguousarray(np.broadcast_to(np.stack([j + 1, 128 - j], axis=0)[None], (64, 2, 128))).astype(np.float32)
    s = np.arange(128)[:, None]; tt = np.arange(128)[None, :]
    ef = np.where(s <= tt, tt - s, BIGE); eb = np.where(s > tt, s - tt, BIGE)
    d["ret_emask"] = np.stack([ef, eb], axis=1).astype(np.float32)
    return d

def s5_consts():
    d = {}
    d["s5_sgn"] = np.concatenate([-np.ones(64), np.ones(64)]).astype(np.float32).reshape(128, 1)
    asc = np.arange(-7, 129).astype(np.float32); desc = asc[::-1].copy()
    d["s5_expo"] = np.ascontiguousarray(np.broadcast_to(np.stack([asc, desc], 0)[None], (128, 2, 136))).astype(np.float32)
    d["identf"] = np.eye(128, dtype=np.float32)
    j = np.zeros((128, 128), np.float32)
    for k in range(64):
        j[k, k + 64] = 1.0; j[k + 64, k] = 1.0
    d["s5_jsw"] = j
    s = (np.arange(128) // 16)[:, None]; t = (np.arange(128) // 16)[None, :]
    d["s5_msk"] = np.stack([(s <= t), (s >= t)], axis=1).astype(np.float32)
    sel = np.zeros((64, 4, 8, 128), np.float32)
    for g in range(4):
        for s_ in range(8):
            for ci in range(16):
                sel[g * 16 + ci, g, s_, s_ * 16 + ci] = 1.0
    d["s5_sel"] = sel.astype(BF)
    return d

def s5_inputs(inp, l, core, uT_full):
    d = s5_consts()
    gs = slice(4 * core, 4 * core + 4)
    d["s5_uT"] = np.ascontiguousarray(uT_full[64 * core:64 * core + 64, :])
    def dup(a):
        x = np.transpose(a, (2, 1, 0)).reshape(64, 8)
        return np.ascontiguousarray(np.concatenate([x, x], 0)).astype(np.float32)
    d["s5_are"] = dup(inp["s5_a_re"][l][:, gs]); d["s5_aim"] = dup(inp["s5_a_im"][l][:, gs])
    ls = inp["s5_log_step"][l][:, gs]
    d["s5_lst"] = np.ascontiguousarray(np.broadcast_to(np.transpose(ls, (1, 0)).reshape(1, 8), (128, 8))).astype(np.float32)
    br = np.transpose(inp["s5_b_re"][l][:, gs], (2, 1, 0, 3)).reshape(64, 8, 16)
    bi = np.transpose(inp["s5_b_im"][l][:, gs], (2, 1, 0, 3)).reshape(64, 8, 16)
    d["s5_p1"] = np.ascontiguousarray(np.concatenate([br, bi], 0)).astype(np.float32)
    d["s5_p2"] = np.ascontiguousarray(np.concatenate([bi, br], 0)).astype(np.float32)
    cr = np.transpose(inp["s5_c_re"][l][gs], (2, 0, 1))
    ci = np.transpose(inp["s5_c_im"][l][gs], (2, 0, 1))
    d["s5_cx"] = np.ascontiguousarray(np.concatenate([cr, ci], 0)).astype(np.float32)
    d["s5_cy"] = np.ascontiguousarray(np.concatenate([ci, cr], 0)).astype(np.float32)
    dsk = inp["s5_d"][l].reshape(32, 16)[gs]
    d["s5_dsk"] = np.ascontiguousarray(np.tile(np.transpose(dsk, (1, 0)), (8, 1))).astype(np.float32)
    return d


_PROGS = {}


def _prog(name, fn):
    if name not in _PROGS:
        _PROGS[name] = fn()
    return _PROGS[name]


def build_LB_all():
    nc = bass.Bass("TRN2", target_bir_lowering=False)
    with ExitStack() as es0:
        S = Sched(nc, es0)
        DRAM_PFX[0] = ""
        DRAM_OVERRIDE.clear()
        NAME_PFX[0] = "t_"
        body_attn(nc, S, True, True)
        NAME_PFX[0] = "r_"
        body_ret(nc, S)
        NAME_PFX[0] = "s_"
        body_s5(nc, S)
        NAME_PFX[0] = ""
    return nc


def build_LCLA():
    nc = bass.Bass("TRN2", target_bir_lowering=False)
    xmid = nc.dram_tensor("xmid", [NTOK, D], F32).ap()
    with ExitStack() as es0:
        S = Sched(nc, es0)
        DRAM_OVERRIDE.clear()
        DRAM_PFX[0] = "c_"; NAME_PFX[0] = "c_"
        DRAM_OVERRIDE["x1"] = xmid
        body_LC(nc, S)
        DRAM_OVERRIDE.clear()
        DRAM_PFX[0] = "a_"; NAME_PFX[0] = "a_"
        DRAM_OVERRIDE["x"] = xmid
        body_LA(nc, S, True, True)
        DRAM_OVERRIDE.clear()
        DRAM_PFX[0] = ""; NAME_PFX[0] = ""
    return nc


def _run(nc, maps):
    res = run_bass_kernel_spmd(nc, maps, core_ids=list(range(8)))
    return res.results


def _f32(a):
    return np.ascontiguousarray(np.asarray(a), dtype=np.float32)


def _la_map(inp, l, c):
    m = {"g0": gcol(inp["norm_gain"][l, 0]), "g1": gcol(inp["norm_gain"][l, 1]),
         "wg": inp["ffn_w_gate"][l, 0], "wu": inp["ffn_w_up"][l, 0], "wd": inp["ffn_w_down"][l, 0],
         "win": inp["w_in"][l], "ident": np.eye(128, dtype=BF),
         "gqk": np.ascontiguousarray(np.broadcast_to(inp["qk_gain"][l].reshape(1, 4, 64), (128, 4, 64))).astype(np.float32),
         "rot": rot_table(c)}
    rc = ret_consts(inp, l, c)
    m["ret_dl"] = rc["ret_dl"]; m["ret_esf"] = rc["ret_esf"]
    return m


def _lb(ra, inp, l, pfx):
    g = lambda c, k: np.asarray(ra[c][pfx + k])
    G = {}
    for k in ("qTA", "kTA", "qTB", "kTB", "uT"):
        G[k] = np.concatenate([g(c, k) for c in range(8)], axis=1)
    for k in ("vA", "vB"):
        G[k] = np.concatenate([g(c, k) for c in range(8)], axis=0)
    sfin_all = np.stack([g(c, "sfin") for c in range(8)], 0)
    nc = _prog("LB", build_LB_all)
    maps = []
    for c in range(8):
        m = attn_inputs(G, inp, l, c)
        rc = ret_consts(inp, l, c)
        m.update({"qkTR": g(c, "qkTR"), "kR": g(c, "kR"), "vR": g(c, "vR"), "sgate": g(c, "sgate"), "sfin_all": sfin_all})
        for k in ("ret_dl", "ret_ncoef", "ret_ekd", "ret_eqd", "ret_emask"):
            m[k] = rc[k]
        m.update(s5_inputs(inp, l, c, G["uT"]))
        maps.append(m)
    rb = _run(nc, maps)
    y8 = np.stack([np.asarray(rb[c]["s5_y8"]) for c in range(8)], 0)
    yS = y8.reshape(8, 4, 8, 16, 2048).transpose(4, 2, 0, 1, 3).reshape(16384, 512)
    out = []
    for c in range(8):
        sl = slice(c * 2048, (c + 1) * 2048)
        out.append({"x": g(c, "x1"), "yA": np.asarray(rb[c]["yA"]), "yB": np.asarray(rb[c]["yB"]),
                    "yR": np.asarray(rb[c]["yR"]), "yS": np.ascontiguousarray(yS[sl])})
    return out


def _lc_map(inp, l):
    return {"wglu": inp["s5_w_glu"][l],
            "bglu": np.ascontiguousarray(np.broadcast_to(inp["s5_b_glu"][l][None], (128, 512))).astype(np.float32),
            "ogain": np.ascontiguousarray(np.broadcast_to(inp["out_gain"][l][None], (128, 2048))).astype(np.float32),
            "wout": inp["w_out"][l], "g0": gcol(inp["norm_gain"][l, 2]),
            "wg": inp["ffn_w_gate"][l, 1], "wu": inp["ffn_w_up"][l, 1], "wd": inp["ffn_w_down"][l, 1],
            "ident": np.eye(128, dtype=BF)}


def kernel(**inputs):
    inp = {k: np.asarray(v) for k, v in inputs.items()}
    x = _f32(inp["x"])[0]
    nc = _prog("LA", lambda: build_LA(True, True))
    maps = []
    for c in range(8):
        m = _la_map(inp, 0, c)
        m["x"] = np.ascontiguousarray(x[c * 2048:(c + 1) * 2048])
        maps.append(m)
    ra = _run(nc, maps)
    pfx = ""
    for l in range(4):
        mix = _lb(ra, inp, l, pfx)
        if l < 3:
            nc = _prog("LCLA", build_LCLA)
            maps = []
            for c in range(8):
                m = {"c_" + k: v for k, v in mix[c].items()}
                m.update({"c_" + k: v for k, v in _lc_map(inp, l).items()})
                m.update({"a_" + k: v for k, v in _la_map(inp, l + 1, c).items()})
                maps.append(m)
            ra = _run(nc, maps)
            pfx = "a_"
        else:
            nc = _prog("LC", build_LC)
            maps = []
            for c in range(8):
                m = dict(mix[c]); m.update(_lc_map(inp, l))
                maps.append(m)
            rc_ = _run(nc, maps)
    return np.concatenate([np.asarray(rc_[c]["x1"]) for c in range(8)], axis=0)[None].astype(np.float32)
```
